# Optimizing a Trainium2 kernel written in Bass

```python
import math
import jax
import jax.numpy as jnp
from jax import lax
import numpy as np

D_MODEL = 1024
BATCH = 4
SEQ = 4096
DEPTH = 2
DEC_BATCH = 32
DEC_SEQ = 8
PAST_LEN = 8192
PAGE_SIZE = 128

N_A_LAYERS = DEPTH // 2
N_B_LAYERS = DEPTH - N_A_LAYERS

GDN_DK = 128
GDN_DV = 128
GDN_QK_HEADS = D_MODEL // GDN_DK
GDN_V_HEADS = 2 * GDN_QK_HEADS
GDN_QK_WIDTH = GDN_QK_HEADS * GDN_DK
GDN_V_WIDTH = GDN_V_HEADS * GDN_DV
GDN_CONV_DIM = 2 * GDN_QK_WIDTH + GDN_V_WIDTH
GDN_IN_WIDTH = GDN_CONV_DIM + GDN_V_WIDTH + 2 * GDN_V_HEADS
CONV_WIDTH = 4
GDN_CHUNK = 64

DIFF_DH = 64
DIFF_HEADS = D_MODEL // (2 * DIFF_DH)
DIFF_VH = 2 * DIFF_DH
DIFF_WIDTH = DIFF_HEADS * DIFF_VH
ROPE_DIMS = DIFF_DH // 4
ROPE_THETA = 500000.0
Q_BLOCK = 128

NORM_EPS = 1e-6
F32 = jnp.float32

kernel_name = 'yoco_gated_delta_diff_attn_step'


def rms_norm(x, gain):
    xf = x.astype(F32)
    y = xf * lax.rsqrt(jnp.mean(xf * xf, axis=-1, keepdims=True) + NORM_EPS)
    return (y * gain.astype(F32)).astype(x.dtype)


def l2_normalize(x):
    return x * lax.rsqrt(jnp.sum(x * x, axis=-1, keepdims=True) + NORM_EPS)


def ada_split(c, w, b, n):
    return jnp.split(c @ w + b, n, axis=-1)


def modulate(x, gain, shift, scale):
    return rms_norm(x, gain) * (1 + scale[:, None, :]) + shift[:, None, :]


def rotary_tables(pos):
    inv_freq = ROPE_THETA ** (-jnp.arange(0, ROPE_DIMS, 2, dtype=F32) / ROPE_DIMS)
    ang = pos.astype(F32)[:, None] * inv_freq[None, :]
    return jnp.cos(ang), jnp.sin(ang)


def partial_rotary(x, cos, sin):
    half = ROPE_DIMS // 2
    c = cos[None, :, None, None, :]
    s = sin[None, :, None, None, :]
    x1 = x[..., :half]
    x2 = x[..., half:ROPE_DIMS]
    return jnp.concatenate([x1 * c - x2 * s, x2 * c + x1 * s, x[..., ROPE_DIMS:]], axis=-1)


def causal_conv(buf, u, w):
    L = u.shape[1]
    xc = jnp.concatenate([buf.astype(u.dtype), u], axis=1)
    y = xc[:, 0:L] * w[0]
    for j in range(1, CONV_WIDTH):
        y = y + xc[:, j:j + L] * w[j]
    return y, xc[:, L:]


def _to_chunks(t, n, c):
    b = t.shape[0]
    pad = n * c - t.shape[1]
    t = jnp.pad(t, [(0, 0), (0, pad)] + [(0, 0)] * (t.ndim - 2))
    t = t.reshape((b, n, c) + t.shape[2:])
    return jnp.transpose(t, (1, 0, 3, 2) + tuple(range(4, t.ndim)))


def gated_delta_rule(q, k, v, g, beta, s0):
    b, L, h, _ = q.shape
    c = min(GDN_CHUNK, L)
    n = -(-L // c)
    qc = _to_chunks(q, n, c)
    kc = _to_chunks(k, n, c)
    vc = _to_chunks(v, n, c)
    gc = jnp.cumsum(_to_chunks(g, n, c), axis=-1)
    bc = _to_chunks(beta, n, c)
    lower = jnp.tril(jnp.ones((c, c), dtype=bool))
    decay = jnp.exp(jnp.where(lower, gc[..., :, None] - gc[..., None, :], -jnp.inf))
    kb = kc * bc[..., None]
    a = jnp.tril(jnp.einsum('nbhik,nbhjk->nbhij', kb, kc) * decay, -1)
    eye = jnp.eye(c, dtype=F32)
    t_inv = lax.linalg.triangular_solve(eye + a, jnp.broadcast_to(eye, a.shape), left_side=True, lower=True)
    u = t_inv @ (vc * bc[..., None])
    w = t_inv @ (kb * jnp.exp(gc)[..., None])
    qk = jnp.einsum('nbhik,nbhjk->nbhij', qc, kc) * decay

    def step(s, inp):
        qi, ki, ui, wi, gi, qki = inp
        v_new = ui - jnp.einsum('bhck,bhkv->bhcv', wi, s)
        o = (jnp.einsum('bhck,bhkv->bhcv', qi * jnp.exp(gi)[..., None], s)
             + jnp.einsum('bhij,bhjv->bhiv', qki, v_new))
        g_last = gi[..., -1:]
        s = (s * jnp.exp(g_last)[..., None]
             + jnp.einsum('bhck,bhcv->bhkv', ki * jnp.exp(g_last - gi)[..., None], v_new))
        return s, o

    s, o = lax.scan(step, s0, (qc, kc, u, w, gc, qk))
    o = jnp.transpose(o, (1, 0, 3, 2, 4)).reshape(b, n * c, h, -1)[:, :L]
    return o, s


def gdn_mixer(h, conv_buf, s0, w_in, conv_w, a_log, dt_bias, out_gain, w_out):
    b, L, _ = h.shape
    proj = h @ w_in
    qkv, z, beta_in, a_in = jnp.split(
        proj, [GDN_CONV_DIM, GDN_CONV_DIM + GDN_V_WIDTH, GDN_CONV_DIM + GDN_V_WIDTH + GDN_V_HEADS], axis=-1)
    qkv, new_buf = causal_conv(conv_buf, qkv, conv_w)
    qkv = jax.nn.silu(qkv.astype(F32))
    q, k, v = jnp.split(qkv, [GDN_QK_WIDTH, 2 * GDN_QK_WIDTH], axis=-1)
    rep = GDN_V_HEADS // GDN_QK_HEADS
    q = jnp.repeat(l2_normalize(q.reshape(b, L, GDN_QK_HEADS, GDN_DK)) * GDN_DK ** -0.5, rep, axis=2)
    k = jnp.repeat(l2_normalize(k.reshape(b, L, GDN_QK_HEADS, GDN_DK)), rep, axis=2)
    v = v.reshape(b, L, GDN_V_HEADS, GDN_DV)
    beta = jax.nn.sigmoid(beta_in.astype(F32))
    g = -jnp.exp(a_log.astype(F32)) * jax.nn.softplus(a_in.astype(F32) + dt_bias.astype(F32))
    o, s = gated_delta_rule(q, k, v, g, beta, s0.astype(F32))
    o = rms_norm(o, out_gain) * jax.nn.silu(z.astype(F32)).reshape(b, L, GDN_V_HEADS, GDN_DV)
    y = o.reshape(b, L, GDN_V_WIDTH).astype(h.dtype) @ w_out
    return y, new_buf, s.astype(s0.dtype)


def shared_kv(x, c, pos, ada_w_kv, ada_b_kv, norm_kv, w_kv, k_gain):
    b, L, _ = x.shape
    shift, scale = ada_split(c, ada_w_kv, ada_b_kv, 2)
    h = modulate(x, norm_kv, shift, scale)
    k, v = jnp.split(h @ w_kv, [DIFF_WIDTH], axis=-1)
    cos, sin = rotary_tables(pos)
    k = partial_rotary(rms_norm(k.reshape(b, L, DIFF_HEADS, 2, DIFF_DH).astype(F32), k_gain), cos, sin)
    k = k.reshape(b, L, DIFF_HEADS, 2 * DIFF_DH).astype(x.dtype)
    v = v.reshape(b, L, DIFF_HEADS, DIFF_VH)
    return k, v


def diff_block(q, k, v, q_pos, k_pos, lam):
    s = jnp.einsum('bqhmd,bkhmd->bhmqk', q, k) * DIFF_DH ** -0.5
    s = jnp.where((k_pos[None, :] <= q_pos[:, None])[None, None, None], s, -jnp.inf)
    p = jax.nn.softmax(s, axis=-1)
    p = p[:, :, 0] - lam * p[:, :, 1]
    return jnp.einsum('bhqk,bkhe->bqhe', p, v)


def diff_mixer(h, k_all, v_all, q_pos, k_pos, layer_idx, w_in, q_gain, lam_p, sub_gain, w_out):
    b, L, _ = h.shape
    q, z = jnp.split(h @ w_in, [DIFF_WIDTH], axis=-1)
    cos, sin = rotary_tables(q_pos)
    q = partial_rotary(rms_norm(q.reshape(b, L, DIFF_HEADS, 2, DIFF_DH).astype(F32), q_gain), cos, sin)
    lam_init = 0.8 - 0.6 * math.exp(-0.3 * layer_idx)
    lp = lam_p.astype(F32)
    lam = jnp.exp(jnp.sum(lp[0] * lp[1])) - jnp.exp(jnp.sum(lp[2] * lp[3])) + lam_init
    kf = k_all.reshape(b, k_all.shape[1], DIFF_HEADS, 2, DIFF_DH).astype(F32)
    vf = v_all.astype(F32)
    qb = Q_BLOCK if L % Q_BLOCK == 0 else L
    nb = L // qb
    qs = jnp.swapaxes(q.reshape(b, nb, qb, DIFF_HEADS, 2, DIFF_DH), 0, 1)
    ps = q_pos.reshape(nb, qb)
    o = lax.map(lambda t: diff_block(t[0], kf, vf, t[1], k_pos, lam), (qs, ps))
    o = jnp.swapaxes(o, 0, 1).reshape(b, L, DIFF_HEADS, DIFF_VH)
    o = rms_norm(o, sub_gain) * (1.0 - lam_init)
    o = o * jax.nn.silu(z.astype(F32)).reshape(b, L, DIFF_HEADS, DIFF_VH)
    return o.reshape(b, L, DIFF_WIDTH).astype(h.dtype) @ w_out


def run_group(x, c, past_len, conv_bufs, gdn_states, k_past, v_past, p):
    L = x.shape[1]
    pos = past_len + jnp.arange(L, dtype=jnp.int32)
    k_pos = jnp.arange(past_len + L, dtype=jnp.int32)
    new_bufs, new_states = [], []
    k_new = v_new = k_all = v_all = None
    for l in range(DEPTH):
        if l < N_A_LAYERS:
            shift, scale, gate = ada_split(c, p['ada_w_a'][l], p['ada_b_a'][l], 3)
            h = modulate(x, p['norm_a'][l], shift, scale)
            y, buf, s = gdn_mixer(h, conv_bufs[l], gdn_states[l], p['w_in_a'][l], p['conv_w_a'][l],
                                  p['a_log'][l], p['dt_bias'][l], p['gdn_out_gain'][l], p['w_out_a'][l])
            x = x + gate[:, None, :] * y
            new_bufs.append(buf)
            new_states.append(s)
            if l == N_A_LAYERS - 1:
                k_new, v_new = shared_kv(x, c, pos, p['ada_w_kv'], p['ada_b_kv'], p['norm_kv'],
                                         p['w_kv'], p['k_gain'])
                k_all = jnp.concatenate([k_past.astype(k_new.dtype), k_new], axis=1)
                v_all = jnp.concatenate([v_past.astype(v_new.dtype), v_new], axis=1)
        else:
            j = l - N_A_LAYERS
            shift, scale, gate = ada_split(c, p['ada_w_b'][j], p['ada_b_b'][j], 3)
            h = modulate(x, p['norm_b'][j], shift, scale)
            y = diff_mixer(h, k_all, v_all, pos, k_pos, l, p['w_in_b'][j], p['q_gain'][j],
                           p['lam_params'][j], p['subln_gain'][j], p['w_out_b'][j])
            x = x + gate[:, None, :] * y
    return x, jnp.stack(new_bufs), jnp.stack(new_states), k_new, v_new


def setup_inputs(seed: int = 0) -> dict:
    key = jax.random.key(seed)
    ks = iter(jax.random.split(key, 48))

    def nrm(shape, scale):
        return jax.random.normal(next(ks), shape, F32) * scale

    def gain(shape):
        return 1.0 + nrm(shape, 0.01)

    n_pages = PAST_LEN // PAGE_SIZE
    n_phys = (DEC_BATCH * n_pages * 5) // 4
    D = D_MODEL
    ada_s = 0.5 * D ** -0.5
    x_prompt = nrm((BATCH, SEQ, D), 1.0)
    x_sample = nrm((DEC_BATCH, DEC_SEQ, D), 1.0)
    state_gdn = nrm((N_A_LAYERS, DEC_BATCH, GDN_V_HEADS, GDN_DK, GDN_DV), GDN_DK ** -0.5)
    state_conv = nrm((N_A_LAYERS, DEC_BATCH, CONV_WIDTH - 1, GDN_CONV_DIM), 1.0)
    cache_k = nrm((n_phys, PAGE_SIZE, DIFF_HEADS, 2 * DIFF_DH), 1.0)
    cache_v = nrm((n_phys, PAGE_SIZE, DIFF_HEADS, DIFF_VH), 1.0)
    page_table = jax.random.permutation(next(ks), n_phys)[:DEC_BATCH * n_pages].reshape(
        DEC_BATCH, n_pages).astype(jnp.int32)
    c_prompt = nrm((BATCH, D), 1.0)
    c_sample = nrm((DEC_BATCH, D), 1.0)
    ada_w_a = nrm((N_A_LAYERS, D, 3 * D), ada_s)
    ada_b_a = nrm((N_A_LAYERS, 3 * D), 0.01)
    norm_a = gain((N_A_LAYERS, D))
    w_in_a = nrm((N_A_LAYERS, D, GDN_IN_WIDTH), D ** -0.5)
    conv_w_a = nrm((N_A_LAYERS, CONV_WIDTH, GDN_CONV_DIM), CONV_WIDTH ** -0.5)
    a_log = jnp.log(jax.random.uniform(next(ks), (N_A_LAYERS, GDN_V_HEADS), F32, 1.0, 16.0))
    dt = jnp.exp(jax.random.uniform(next(ks), (N_A_LAYERS, GDN_V_HEADS), F32,
                                    math.log(1e-3), math.log(1e-1)))
    dt_bias = dt + jnp.log(-jnp.expm1(-dt))
    gdn_out_gain = gain((N_A_LAYERS, GDN_DV))
    w_out_a = nrm((N_A_LAYERS, GDN_V_WIDTH, D), GDN_V_WIDTH ** -0.5)
    ada_w_kv = nrm((D, 2 * D), ada_s)
    ada_b_kv = nrm((2 * D,), 0.01)
    norm_kv = gain((D,))
    w_kv = nrm((D, 2 * DIFF_WIDTH), D ** -0.5)
    k_gain = gain((DIFF_DH,))
    ada_w_b = nrm((N_B_LAYERS, D, 3 * D), ada_s)
    ada_b_b = nrm((N_B_LAYERS, 3 * D), 0.01)
    norm_b = gain((N_B_LAYERS, D))
    w_in_b = nrm((N_B_LAYERS, D, 2 * DIFF_WIDTH), D ** -0.5)
    q_gain = gain((N_B_LAYERS, DIFF_DH))
    lam_params = nrm((N_B_LAYERS, 4, DIFF_DH), 0.1)
    subln_gain = gain((N_B_LAYERS, DIFF_VH))
    w_out_b = nrm((N_B_LAYERS, DIFF_WIDTH, D), DIFF_WIDTH ** -0.5)
    return {'x_prompt': x_prompt, 'x_sample': x_sample, 'state_gdn': state_gdn, 'state_conv': state_conv,
            'cache_k': cache_k, 'cache_v': cache_v, 'page_table': page_table,
            'c_prompt': c_prompt, 'c_sample': c_sample,
            'ada_w_a': ada_w_a, 'ada_b_a': ada_b_a, 'norm_a': norm_a, 'w_in_a': w_in_a, 'conv_w_a': conv_w_a,
            'a_log': a_log, 'dt_bias': dt_bias, 'gdn_out_gain': gdn_out_gain, 'w_out_a': w_out_a,
            'ada_w_kv': ada_w_kv, 'ada_b_kv': ada_b_kv, 'norm_kv': norm_kv, 'w_kv': w_kv, 'k_gain': k_gain,
            'ada_w_b': ada_w_b, 'ada_b_b': ada_b_b, 'norm_b': norm_b, 'w_in_b': w_in_b, 'q_gain': q_gain,
            'lam_params': lam_params, 'subln_gain': subln_gain, 'w_out_b': w_out_b}


def reference(x_prompt, x_sample, state_gdn, state_conv, cache_k, cache_v, page_table, c_prompt, c_sample,
              ada_w_a, ada_b_a, norm_a, w_in_a, conv_w_a, a_log, dt_bias, gdn_out_gain, w_out_a,
              ada_w_kv, ada_b_kv, norm_kv, w_kv, k_gain,
              ada_w_b, ada_b_b, norm_b, w_in_b, q_gain, lam_params, subln_gain, w_out_b):
    p = dict(ada_w_a=ada_w_a, ada_b_a=ada_b_a, norm_a=norm_a, w_in_a=w_in_a, conv_w_a=conv_w_a,
             a_log=a_log, dt_bias=dt_bias, gdn_out_gain=gdn_out_gain, w_out_a=w_out_a,
             ada_w_kv=ada_w_kv, ada_b_kv=ada_b_kv, norm_kv=norm_kv, w_kv=w_kv, k_gain=k_gain,
             ada_w_b=ada_w_b, ada_b_b=ada_b_b, norm_b=norm_b, w_in_b=w_in_b, q_gain=q_gain,
             lam_params=lam_params, subln_gain=subln_gain, w_out_b=w_out_b)
    b = x_prompt.shape[0]
    conv0 = jnp.zeros((N_A_LAYERS, b, CONV_WIDTH - 1, GDN_CONV_DIM), x_prompt.dtype)
    st0 = jnp.zeros((N_A_LAYERS, b, GDN_V_HEADS, GDN_DK, GDN_DV), state_gdn.dtype)
    k0 = jnp.zeros((b, 0, DIFF_HEADS, 2 * DIFF_DH), x_prompt.dtype)
    v0 = jnp.zeros((b, 0, DIFF_HEADS, DIFF_VH), x_prompt.dtype)
    y_prompt, conv_p, st_p, k_p, v_p = run_group(x_prompt, c_prompt, 0, conv0, st0, k0, v0, p)
    db = x_sample.shape[0]
    past = page_table.shape[1] * PAGE_SIZE
    k_past = cache_k[page_table].reshape(db, past, DIFF_HEADS, 2 * DIFF_DH)
    v_past = cache_v[page_table].reshape(db, past, DIFF_HEADS, DIFF_VH)
    y_sample, conv_s, st_s, k_s, v_s = run_group(x_sample, c_sample, past, state_conv, state_gdn,
                                                 k_past, v_past, p)
    return (y_prompt, y_sample, st_p, conv_p, k_p, v_p, st_s, conv_s, k_s, v_s)
```

```python
import math
from contextlib import ExitStack

import numpy as np
import concourse.bass as bass
import concourse.mybir as mybir
from concourse.bass_utils import run_bass_kernel_spmd

F32 = mybir.dt.float32
BF16 = mybir.dt.bfloat16
I32 = mybir.dt.int32
AF = mybir.ActivationFunctionType
ALU = mybir.AluOpType
AX = mybir.AxisListType
EPOCH = 1 << 28
EPS = 1e-6
DEBUG = False
P3STEP = 9
VVAR = 3
NDSEM = 24
KVSTEP = 9
VW = 160
MAXPHASE = 3
INV_LEVELS = 7
NEG = -30000.0

D = 1024
HV = 16
HQ = 8
NQKV = 4096
NIN = 6176
DH = 64


class Buf:
    def __init__(self, name):
        self.name = name
        self.writer = None
        self.readers = []
        self.dsem = None
        self.dcount = 0


class Tile(Buf):
    def __init__(self, name, t):
        super().__init__(name)
        self.t = t

    def __getitem__(self, k):
        return self.t[k]


class MK:
    def __init__(self, nc, es):
        self.nc = nc
        self.es = es
        self.names = ['pe', 'dve', 'act', 'pool', 'sp']
        self.cnt = {k: 0 for k in self.names}
        self.sems = {k: [] for k in self.names}
        self.waited = {k: {} for k in self.names}
        self.q = {k: [] for k in self.names}
        self.n = 0
        self.out_owners = []
        self.all_owners = []
        self.groups = []
        self.ngrp = 0
        self.same_sync = {'pe': False, 'dve': True, 'act': True, 'pool': True, 'sp': True}

    def tile(self, shape, dt, name=None, es=None):
        self.n += 1
        name = name or f"t{self.n}"
        t = (es or self.es).enter_context(self.nc.sbuf_tensor(f"{name}_{self.n}", list(shape), dt))
        return Tile(name, t)

    def psum(self, shape, dt, name=None, es=None):
        self.n += 1
        name = name or f"p{self.n}"
        t = (es or self.es).enter_context(self.nc.psum_tensor(f"{name}_{self.n}", list(shape), dt))
        return Tile(name, t)

    def dram(self, name, shape, dt, kind="Internal"):
        t = self.nc.dram_tensor(name, list(shape), dt, kind=kind)
        return Tile(name, t.ap())

    def _sem(self, e, epoch):
        while len(self.sems[e]) <= epoch:
            self.sems[e].append(self.es.enter_context(self.nc.semaphore(f"s_{e}_{len(self.sems[e])}")))
        return self.sems[e][epoch]

    def _wait_tok(self, E, tok):
        if tok[0] == 'e':
            _, e, count = tok
            if e == E and not self.same_sync[E]:
                return
            key = ('e', e)
            if self.waited[E].get(key, 0) >= count:
                return
            self.waited[E][key] = count
            ep, v = (count - 1) // EPOCH, (count - 1) % EPOCH + 1
            self.q[E].append(('w', self._sem(e, ep), v))
        else:
            g = tok[1].grp
            key = ('d', id(g))
            if self.waited[E].get(key, 0) >= g.dcount:
                return
            self.waited[E][key] = g.dcount
            self.q[E].append(('w', g.dsem, 16 * g.dcount))

    def deps(self, E, reads, writes):
        for b in reads:
            if b.writer is not None:
                self._wait_tok(E, b.writer)
        for b in writes:
            if b.writer is not None:
                self._wait_tok(E, b.writer)
            for r in b.readers:
                self._wait_tok(E, r)

    def _record(self, tok, reads, writes):
        for b in writes:
            b.writer = tok
            b.readers = []
        for b in reads:
            if b not in writes:
                b.readers.append(tok)
                if len(b.readers) > 24:
                    b.readers = b.readers[-24:] if False else b.readers

    def op(self, E, reads, writes, fn):
        self.deps(E, reads, writes)
        self.cnt[E] += 1
        c = self.cnt[E]
        self.q[E].append(('i', fn, self._sem(E, (c - 1) // EPOCH), 1))
        self._record(('e', E, c), reads, writes)

    def dma(self, Q, owner, reads, writes, fn, is_output=False):
        self.deps(Q, reads, writes)
        if getattr(owner, 'grp', None) is None:
            if len(self.groups) < NDSEM:
                g = Buf(f"grp{len(self.groups)}")
                g.dsem = self.es.enter_context(self.nc.semaphore(f"dgrp_{len(self.groups)}"))
                self.groups.append(g)
            owner.grp = self.groups[self.ngrp % NDSEM]
            self.ngrp += 1
        g = owner.grp
        g.dcount += 1
        owner.dcount += 1
        if owner not in self.all_owners:
            self.all_owners.append(owner)
        self.q[Q].append(('i', fn, g.dsem, 16))
        self._record(('d', owner), reads, writes)
        if is_output and owner not in self.out_owners:
            self.out_owners.append(owner)

    def finish(self):
        for g in self.groups:
            if g.dcount:
                self.q['sp'].append(('w', g.dsem, 16 * g.dcount))
        nc = self.nc
        with nc.Block() as block:
            regs = {'pe': block.tensor, 'dve': block.vector, 'act': block.scalar,
                    'pool': block.gpsimd, 'sp': block.sync}
            for E in self.names:
                items = self.q[E]

                def body(e, items=items):
                    for it in items:
                        if it[0] == 'w':
                            e.wait_ge(it[1], it[2])
                        else:
                            ins = it[1](e)
                            if it[2] is not None:
                                ins.then_inc(it[2], it[3])
                regs[E](body)


def host_consts(sample):
    i = np.arange(128)
    if sample:
        same = (i[:, None] // 8) == (i[None, :] // 8)
    else:
        same = np.ones((128, 128), bool)
    c = {}
    c['m1'] = ((i[:, None] <= i[None, :]) & same).astype(np.float32)
    c['m2'] = ((i[:, None] > i[None, :]) & same).astype(np.float32)
    c['nm_strict'] = np.where((i[:, None] < i[None, :]) & same, 0.0, NEG).astype(np.float32)
    c['nm_incl'] = np.where((i[:, None] <= i[None, :]) & same, 0.0, NEG).astype(np.float32)
    c['same'] = same.astype(np.float32)
    nlev = 3 if sample else 7
    ml = np.zeros((128, nlev, 2, 128), np.float32)
    for lv in range(nlev):
        b = 1 << lv
        r, cc = i[:, None], i[None, :]
        low = ((r // (2 * b)) == (cc // (2 * b))) & ((r // b) % 2 == 1) & ((cc // b) % 2 == 0)
        ml[:, lv, 1, :] = -low.astype(np.float32)
        ml[:, lv, 0, :] = -low.T.astype(np.float32)
    c['ml'] = ml
    return c


class Ctx:
    pass


def _act(mk, reads, writes, out, in_, func, **kw):
    mk.op('act', reads, writes, lambda e: e.activation(out=out, in_=in_, func=func, **kw))


def _tt(mk, E, reads, writes, out, in0, in1, op):
    mk.op(E, reads, writes, lambda e: e.tensor_tensor(out=out, in0=in0, in1=in1, op=op))


def _ts(mk, E, reads, writes, out, in0, s1, op0, s2=None, op1=None):
    if op1 is None:
        mk.op(E, reads, writes, lambda e: e.tensor_scalar(out=out, in0=in0, scalar1=s1, scalar2=None, op0=op0))
    else:
        mk.op(E, reads, writes, lambda e: e.tensor_scalar(out=out, in0=in0, scalar1=s1, scalar2=s2, op0=op0, op1=op1))


def _stt(mk, reads, writes, out, in0, scalar, in1, op0, op1, E='dve'):
    mk.op(E, reads, writes, lambda e: e.scalar_tensor_tensor(out=out, in0=in0, scalar=scalar, in1=in1, op0=op0, op1=op1))


def _mm(mk, reads, writes, out, lhsT, rhs, start=True, stop=True):
    mk.op('pe', reads, writes, lambda e: e.matmul(out, lhsT=lhsT, rhs=rhs, start=start, stop=stop))


def _tr(mk, reads, writes, out, in_, ident):
    mk.op('pe', reads, writes, lambda e: e.transpose(out=out, in_=in_, identity=ident))


def _copy(mk, E, reads, writes, out, in_):
    if E == 'act':
        mk.op('act', reads, writes, lambda e: e.activation(out=out, in_=in_, func=AF.Copy))
    else:
        mk.op(E, reads, writes, lambda e: e.tensor_copy(out=out, in_=in_))


def _load(mk, tile, out_ap, in_ap, Q='sp', reads=()):
    mk.dma(Q, tile, list(reads), [tile], lambda e: e.dma_start(out=out_ap, in_=in_ap))


def _store(mk, tile, out_ap, in_ap, Q='sp', writes=(), is_output=True):
    mk.dma(Q, tile, [tile], list(writes), lambda e: e.dma_start(out=out_ap, in_=in_ap), is_output=is_output)


def setup_consts(cx, din):
    mk = cx.mk
    cx.identb = mk.tile([128, 128], BF16, 'identb')
    cx.identf = mk.tile([128, 128], F32, 'identf')
    cx.onesf = mk.tile([128, 128], F32, 'onesf')
    cx.onesb = mk.tile([128, 128], BF16, 'onesb')
    mk.op('pool', [], [cx.onesf], lambda e: e.memset(cx.onesf[:], 1.0))
    mk.op('pool', [], [cx.onesb], lambda e: e.memset(cx.onesb[:], 1.0))
    mk.op('pool', [], [cx.identf], lambda e: e.memset(cx.identf[:], 1.0))
    mk.op('pool', [cx.identf], [cx.identf], lambda e: e.affine_select(
        out=cx.identf[:], in_=cx.identf[:], pattern=[[-1, 128]], compare_op=ALU.is_equal, fill=0.0,
        base=0, channel_multiplier=1))
    _copy(mk, 'pool', [cx.identf], [cx.identb], cx.identb[:], cx.identf[:])


def load_masks(cx, din, pre, nlev=7, es_tmp=None):
    mk = cx.mk
    m = Ctx()
    m.nlev = nlev
    m.m1 = mk.tile([128, 128], F32, pre + 'm1')
    m.m2 = mk.tile([128, 128], F32, pre + 'm2')
    m.nm_incl4 = mk.tile([128, 4, 128], BF16, pre + 'nmi4')
    m.strict01 = mk.tile([128, 128], F32, pre + 's01')
    m.ML = mk.tile([128, nlev, 2, 128], BF16, pre + 'ML')
    tmp = mk.tile([128, 2, 128], F32, pre + 'nmtmp', es=es_tmp)
    mlf = mk.tile([128, nlev, 2, 128], F32, pre + 'mlf', es=es_tmp)
    _load(mk, m.m1, m.m1[:], din[pre + 'm1'])
    _load(mk, m.m2, m.m2[:], din[pre + 'm2'])
    _load(mk, tmp, tmp[:, 0, :], din[pre + 'nm_strict'])
    _load(mk, tmp, tmp[:, 1, :], din[pre + 'nm_incl'])
    _copy(mk, 'pool', [tmp], [m.nm_incl4], m.nm_incl4[:], tmp[:, 1:2, :].broadcast_to([128, 4, 128]))
    _ts(mk, 'pool', [tmp], [m.strict01], m.strict01[:], tmp[:, 0, :], 0.0, ALU.is_equal)
    _load(mk, mlf, mlf[:], din[pre + 'ml'])
    _copy(mk, 'pool', [mlf], [m.ML], m.ML[:], mlf[:])
    return m


def barrier(mk):
    owners = [o for o in mk.all_owners if o.dcount > 0] if hasattr(mk, 'all_owners') else []
    for E in mk.names:
        for e in mk.names:
            if e != E and e != 'sp' and mk.cnt[e] > 0:
                mk._wait_tok(E, ('e', e, mk.cnt[e]))
        for o in owners:
            mk._wait_tok(E, ('d', o))


def alloc_stage(cx, es):
    mk = cx.mk
    st = Ctx()
    st.f = [mk.tile([128, 4096], F32, f"stg{i}", es=es) for i in range(2)]
    st.b = [mk.tile([128, 4096], BF16, f"stb{i}", es=es) for i in range(2)]
    st.bias = [mk.tile([128, 512], F32, f"adab{i}", es=es) for i in range(2)]
    st.n = 0
    return st


def ada_mod(cx, cT, w_ap, b_ap, ncols, outs, st):
    mk = cx.mk
    wv = w_ap.rearrange("(kc p) n -> p kc n", p=128)
    for ci in range(ncols // 512):
        st.n += 1
        wt, b = st.f[st.n % 2], st.bias[st.n % 2]
        w = wt[:].rearrange("p (k c) -> p k c", k=8)
        ps = cx.ps[ci % 2]
        _load(mk, wt, w, wv[:, :, ci * 512:(ci + 1) * 512])
        _load(mk, b, b[:], b_ap[:, ci * 512:(ci + 1) * 512])
        for kc in range(8):
            _mm(mk, [cT, wt], [ps], ps[:], cT[:, kc, :], w[:, kc, :], start=(kc == 0), stop=(kc == 7))
        o = outs[ci // 2]
        _tt(mk, 'dve', [ps, b], [o], o[:, (ci % 2) * 512:(ci % 2 + 1) * 512], ps[:], b[:], ALU.add)


def prep_w_bf(cx, w_ap, nk, ncols, dst, st, to_dram=True):
    mk = cx.mk
    wv = w_ap.rearrange("(kc p) n -> p kc n", p=128)
    CH = 4096 // nk
    engs = ['pool', 'act']
    for ci, c0 in enumerate(range(0, ncols, CH)):
        w = min(CH, ncols - c0)
        st.n += 1
        sf, sb = st.f[st.n % 2], st.b[st.n % 2]
        s = sf[:].rearrange("p (k c) -> p k c", k=nk)
        b = sb[:].rearrange("p (k c) -> p k c", k=nk)
        _load(mk, sf, s[:, :, :w], wv[:, :, c0:c0 + w])
        _copy(mk, engs[ci % 2], [sf], [sb], b[:, :, :w], s[:, :, :w])
        _store(mk, sb, dst[:, :, c0:c0 + w], b[:, :, :w], writes=[dst], is_output=False)


def rms_mod_T(cx, x, nblk, A, Sh, hT, es_tiles):
    mk = cx.mk
    junk, ss, rstd, hb = es_tiles['junk'], es_tiles['ss'], es_tiles['rstd'], es_tiles['hb']
    for blk in range(nblk):
        _act(mk, [x], [junk, ss], junk[:], x[:, blk, :], AF.Square, accum_out=ss[:, blk:blk + 1])
    _act(mk, [ss, cx.epsc], [rstd], rstd[:, :nblk], ss[:, :nblk], AF.Sqrt, scale=1.0 / D, bias=cx.epsc[:, 0:1])
    mk.op('dve', [rstd], [rstd], lambda e: e.reciprocal(out=rstd[:, :nblk], in_=rstd[:, :nblk]))
    for blk in range(nblk):
        t = es_tiles['t32']
        _stt(mk, [x, rstd, A], [t], t[:], x[:, blk, :], rstd[:, blk:blk + 1], A[:], ALU.mult, ALU.mult)
        _tt(mk, 'pool', [t, Sh], [hb], hb[:], t[:], Sh[:], ALU.add)
        pt = cx.pst[blk % 2]
        for kc in range(8):
            _tr(mk, [hb, cx.identb], [pt], pt[:, kc * 128:(kc + 1) * 128], hb[:, kc * 128:(kc + 1) * 128], cx.identb[:])
        _copy(mk, 'act' if blk % 2 == 0 else 'dve', [pt], [hT],
              hT[:, :, blk * 128:(blk + 1) * 128], pt[:].rearrange("p (k t) -> p k t", k=8))


def nextps(cx):
    cx.psi = (cx.psi + 1) % len(cx.ps)
    return cx.ps[cx.psi]


def nextps4(cx):
    cx.psi4 = (getattr(cx, 'psi4', 0) + 1) % 4
    return cx.ps[cx.psi4]


def nextpst(cx):
    cx.psti = (cx.psti + 1) % len(cx.pst)
    return cx.pst[cx.psti]


def alloc_l0(cx, es, T):
    mk = cx.mk
    L = Ctx()
    nb = T // 128
    L.T, L.nb = T, nb
    L.x = [mk.tile([128, nb, D], F32, 'x', es=es)] * 2
    L.hT = mk.tile([128, 8, T], BF16, 'hT', es=es)
    L.wbuf = [mk.tile([128, 8, 256], BF16, f'wbuf{i}', es=es) for i in range(2)]
    L.raw = [mk.tile([128, 3 + T], BF16, f'raw{i}', es=es) for i in range(2)]
    L.hist = mk.tile([128, 32, 3], BF16, 'hist', es=es)
    L.hist32 = mk.tile([128, 32, 3], F32, 'hist32', es=es)
    L.diag = [mk.tile([128, 4, 128], BF16, f'diag{i}', es=es) for i in range(2)]
    L.qkT = mk.tile([128, 16, T], BF16, 'qkT', es=es)
    L.vT = mk.tile([128, 16, T], BF16, 'vT', es=es)
    L.zg = mk.tile([128, nb, 2048], BF16, 'zg', es=es)
    L.ba = mk.tile([128, nb, 32], F32, 'ba', es=es)
    L.sq = [mk.tile([128, T], BF16, f'sq{i}', es=es) for i in range(2)]
    L.r32 = [mk.tile([128, T], F32, f'r32{i}', es=es) for i in range(2)]
    L.sm = {k: mk.tile([128, 16], F32, 'sm_' + k, es=es) for k in
            ['beta', 'xa', 'ax', 'e1', 'l1', 'sp', 'g', 'gc', 'ngc', 'eg', 's1', 's2', 'egl', 'ss', 'rstd']}
    L.rhsB = mk.tile([128, 16, 128], F32, 'rhsB', es=es)
    L.rhsBeta = mk.tile([128, 16, 128], BF16, 'rhsBeta', es=es)
    L.egB = mk.tile([128, 16, 128], BF16, 'egB', es=es)
    L.QgT = L.egB
    L.DmQ = mk.tile([128, 16, 128], BF16, 'DmQ', es=es)
    L.Ms = mk.tile([128, 16, 128], BF16, 'Ms', es=es)
    L.X = L.Ms
    L.QKT = mk.tile([128, 16, 128], BF16, 'QKT', es=es)
    L.inv = [mk.tile([128, 16, 2, 128], BF16, 'AA', es=es)]
    L.DD = [mk.tile([128, 2, 2, 128], BF16, f'DD{i}', es=es) for i in range(8)]
    L.ZZ = [mk.tile([128, 2, 2, 128], BF16, f'ZZ{i}', es=es) for i in range(8)]
    L.AO = mk.tile([128, 16, 2, 128], BF16, 'AO', es=es)
    L.Kbg = mk.tile([128, 16, 128], BF16, 'Kbg', es=es)
    L.Kd = mk.tile([128, 16, 128], BF16, 'Kd', es=es)
    L.Vb = mk.tile([128, 16, 128], BF16, 'Vb', es=es)
    L.nWT = L.DmQ
    L.vn = L.Ms
    L.o32 = L.rhsB
    L.og = L.rhsBeta
    L.ogT = L.Kbg
    L.x1 = [None, None]
    L.t32 = mk.tile([128, D], F32, 't32', es=es)
    L.hb = mk.tile([128, D], BF16, 'hb', es=es)
    L.junk = L.hb
    L.nss = mk.tile([128, 4], F32, 'nss', es=es)
    L.nrs = mk.tile([128, 4], F32, 'nrs', es=es)
    return L


def gdn_gates(cx, L, blk, M, vm=None):
    mk = cx.mk
    s = L.sm
    bin_ = L.ba[:, blk, 0:16]
    ain = L.ba[:, blk, 16:32]
    _act(mk, [L.ba], [s['beta']], s['beta'][:], bin_, AF.Sigmoid)
    _tt(mk, 'dve', [L.ba, cx.dtb], [s['xa']], s['xa'][:], ain, cx.dtb[:], ALU.add)
    _act(mk, [s['xa']], [s['ax']], s['ax'][:], s['xa'][:], AF.Abs)
    _act(mk, [s['ax']], [s['e1']], s['e1'][:], s['ax'][:], AF.Exp, scale=-1.0)
    _act(mk, [s['e1'], cx.onec], [s['l1']], s['l1'][:], s['e1'][:], AF.Ln, bias=cx.onec[:, 0:1])
    _stt(mk, [s['xa'], s['l1']], [s['sp']], s['sp'][:], s['xa'][:], 0.0, s['l1'][:], ALU.max, ALU.add)
    _tt(mk, 'dve', [s['sp'], cx.nea], [s['g']], s['g'][:], s['sp'][:], cx.nea[:], ALU.mult)
    if vm is not None:
        _ts(mk, 'dve', [s['beta'], vm[0]], [s['beta']], s['beta'][:], s['beta'][:], vm[1], ALU.mult)
        _ts(mk, 'dve', [s['g'], vm[0]], [s['g']], s['g'][:], s['g'][:], vm[1], ALU.mult)
    ps = nextps(cx)
    _mm(mk, [M.m1, s['g']], [ps], ps[:, 0:16], M.m1[:], s['g'][:])
    _mm(mk, [M.m2, s['g']], [ps], ps[:, 16:32], M.m2[:], s['g'][:])
    _mm(mk, [cx.onesf, s['g']], [ps], ps[:, 32:48], cx.onesf[:], s['g'][:])
    _copy(mk, 'dve', [ps], [s['gc']], s['gc'][:], ps[:, 0:16])
    _act(mk, [ps], [s['ngc']], s['ngc'][:], ps[:, 0:16], AF.Copy, scale=-1.0)
    _act(mk, [ps], [s['eg']], s['eg'][:], ps[:, 0:16], AF.Exp)
    _act(mk, [ps], [s['s2']], s['s2'][:], ps[:, 16:32], AF.Exp)
    _act(mk, [ps], [s['egl']], s['egl'][:], ps[:, 32:48], AF.Exp)
    _tt(mk, 'dve', [s['beta'], s['eg']], [s['s1']], s['s1'][:], s['beta'][:], s['eg'][:], ALU.mult)
    _tt(mk, 'dve', [M.m1, s['g']], [L.rhsB], L.rhsB[:],
        M.m1[:].unsqueeze(1).broadcast_to([128, 16, 128]),
        s['g'][:].unsqueeze(2).broadcast_to([128, 16, 128]), ALU.mult)
    _tt(mk, 'pool', [cx.identf, s['beta']], [L.rhsBeta], L.rhsBeta[:],
        cx.identf[:].unsqueeze(1).broadcast_to([128, 16, 128]),
        s['beta'][:].unsqueeze(2).broadcast_to([128, 16, 128]), ALU.mult)


def gdn_block(cx, L, blk, M, S32, Sb, vm=None):
    mk = cx.mk
    s = L.sm
    b0 = blk * 128
    gdn_gates(cx, L, blk, M, vm)
    for hg in range(4):
        hs = slice(hg * 4, hg * 4 + 4)
        pe_ = nextps(cx)
        _mm(mk, [cx.onesf, L.rhsB], [pe_], pe_[:], cx.onesf[:], L.rhsB[:, hs, :].rearrange("p h i -> p (h i)"))
        _act(mk, [pe_], [L.egB], L.egB[:, hs, :].rearrange("p h i -> p (h i)"), pe_[:], AF.Exp)
        pd = nextps(cx)
        _mm(mk, [cx.onesf, L.rhsB], [pd], pd[:], cx.onesf[:], L.rhsB[:, hs, :].rearrange("p h i -> p (h i)"), start=True, stop=False)
        _mm(mk, [cx.identb, M.nm_incl4], [pd], pd[:], cx.identb[:], M.nm_incl4[:].rearrange("p h i -> p (h i)"), start=False, stop=True)
        for hh in range(4):
            h = hg * 4 + hh
            _act(mk, [pd, s['ngc']], [L.DmQ], L.DmQ[:, h, :], pd[:, hh * 128:(hh + 1) * 128], AF.Exp, bias=s['ngc'][:, h:h + 1])
        pb = nextps(cx)
        _mm(mk, [cx.onesb, L.rhsBeta], [pb], pb[:], cx.onesb[:], L.rhsBeta[:, hs, :].rearrange("p h i -> p (h i)"))
        _tt(mk, 'dve', [pb, M.strict01], [L.Ms], L.Ms[:, hs, :], pb[:].rearrange("p (h i) -> p h i", h=4),
            M.strict01[:].unsqueeze(1).broadcast_to([128, 4, 128]), ALU.mult)
    _tt(mk, 'pool', [L.DmQ, L.Ms], [L.X], L.X[:], L.DmQ[:], L.Ms[:], ALU.mult)
    inv0 = L.inv[0]
    for half in range(2):
        pk = nextps(cx)
        pq = nextps(cx)
        for j in range(4):
            hq = half * 4 + j
            kT = L.qkT[:, 8 + hq, b0:b0 + 128]
            qT = L.qkT[:, hq, b0:b0 + 128]
            _mm(mk, [L.qkT], [pk], pk[:, j * 128:(j + 1) * 128], kT, kT)
            _mm(mk, [L.qkT], [pq], pq[:, j * 128:(j + 1) * 128], kT, qT)
        hs = slice(half * 8, half * 8 + 8)
        _tt(mk, 'dve', [pk, L.X], [inv0], inv0[:, hs, 0, :].rearrange("p (a b) i -> p a b i", b=2),
            pk[:].rearrange("p (a i) -> p a i", a=4).unsqueeze(2).broadcast_to([128, 4, 2, 128]),
            L.X[:, hs, :].rearrange("p (a b) i -> p a b i", b=2), ALU.mult)
        _tt(mk, 'dve', [pq, L.DmQ], [L.QKT], L.QKT[:, hs, :].rearrange("p (a b) i -> p a b i", b=2),
            pq[:].rearrange("p (a i) -> p a i", a=4).unsqueeze(2).broadcast_to([128, 4, 2, 128]),
            L.DmQ[:, hs, :].rearrange("p (a b) i -> p a b i", b=2), ALU.mult)
    for half in range(2):
        pt = nextpst(cx)
        for j in range(8):
            h = half * 8 + j
            _tr(mk, [inv0, cx.identb], [pt], pt[:, j * 128:(j + 1) * 128], inv0[:, h, 0, :], cx.identb[:])
        _copy(mk, 'act', [pt], [inv0], inv0[:, half * 8:half * 8 + 8, 1, :], pt[:].rearrange("p (h i) -> p h i", h=8))
    _tt(mk, 'pool', [L.qkT, L.egB], [L.QgT], L.QgT[:].rearrange("p (a b) i -> p a b i", b=2),
        L.qkT[:, 0:8, b0:b0 + 128].unsqueeze(2).broadcast_to([128, 8, 2, 128]),
        L.egB[:].rearrange("p (a b) i -> p a b i", b=2), ALU.mult)
    AA = inv0[:]
    ML = M.ML
    for sl in range(2):
        _tt(mk, 'pool', [inv0, ML], [L.AO], L.AO[:, :, sl, :], inv0[:, :, sl, :], ML[:, 0:1, sl, :].broadcast_to([128, 16, 128]), ALU.mult)
    for hp in range(8):
        _tt(mk, 'pool', [L.AO, cx.identb], [L.DD[hp]], L.DD[hp][:], L.AO[:, hp * 2:hp * 2 + 2, :, :],
            cx.identb[:].unsqueeze(1).unsqueeze(1).broadcast_to([128, 2, 2, 128]), ALU.add)
    nlev = min(M.nlev, INV_LEVELS)
    for lv in range(1, nlev):
        last = (lv == nlev - 1)
        for sl in range(2):
            _tt(mk, 'pool', [inv0, ML], [L.AO], L.AO[:, :, sl, :], inv0[:, :, sl, :], ML[:, lv:lv + 1, sl, :].broadcast_to([128, 16, 128]), ALU.mult)
        for hp in range(8):
            ps = nextps(cx)
            DD, ZZ = L.DD[hp], L.ZZ[hp]
            for hh in range(2):
                h = hp * 2 + hh
                c0 = hh * 256
                if not last:
                    _mm(mk, [L.AO, DD], [ps], ps[:, c0:c0 + 128], L.AO[:, h, 0, :], DD[:, hh, 1, :])
                _mm(mk, [L.AO, DD], [ps], ps[:, c0 + 128:c0 + 256], L.AO[:, h, 1, :], DD[:, hh, 0, :])
            if not last:
                _copy(mk, 'act', [ps], [ZZ], ZZ[:].rearrange("p h s i -> p (h s i)"), ps[:])
            else:
                _copy(mk, 'act', [ps], [ZZ], ZZ[:, :, 1, :], ps[:].rearrange("p (h s i) -> p h s i", h=2, s=2)[:, :, 1, :])
        for hp in range(8):
            ps = nextps(cx)
            DD, ZZ = L.DD[hp], L.ZZ[hp]
            for hh in range(2):
                c0 = hh * 256
                _mm(mk, [ZZ, DD], [ps], ps[:, c0:c0 + 128], DD[:, hh, 1, :], ZZ[:, hh, 1, :])
                if not last:
                    _mm(mk, [ZZ, DD], [ps], ps[:, c0 + 128:c0 + 256], DD[:, hh, 0, :], ZZ[:, hh, 0, :])
            if not last:
                _tt(mk, 'dve', [ps, DD], [DD], DD[:].rearrange("p h s i -> p (h s i)"), ps[:],
                    DD[:].rearrange("p h s i -> p (h s i)"), ALU.add)
            else:
                _tt(mk, 'dve', [ps, DD], [DD], DD[:, :, 0, :], ps[:].rearrange("p (h s i) -> p h s i", h=2, s=2)[:, :, 0, :],
                    DD[:, :, 0, :], ALU.add)

    class _TT:
        def __getitem__(self_, key):
            return None
    TTt = lambda h: L.DD[h // 2]
    TTa = lambda h: L.DD[h // 2][:, h % 2, 0, :]
    pt = nextpst(cx)
    for hq in range(8):
        _tr(mk, [L.qkT, cx.identb], [pt], pt[:, hq * 128:(hq + 1) * 128], L.qkT[:, 8 + hq, b0:b0 + 128], cx.identb[:])
    kview = pt[:].rearrange("p (a i) -> p a i", a=8).unsqueeze(2).broadcast_to([128, 8, 2, 128])
    _tt(mk, 'dve', [pt, s['s1']], [L.Kbg], L.Kbg[:].rearrange("p (a b) i -> p a b i", b=2), kview,
        s['s1'][:].rearrange("p (a b) -> p a b", b=2).unsqueeze(3).broadcast_to([128, 8, 2, 128]), ALU.mult)
    _tt(mk, 'dve', [pt, s['s2']], [L.Kd], L.Kd[:].rearrange("p (a b) i -> p a b i", b=2), kview,
        s['s2'][:].rearrange("p (a b) -> p a b", b=2).unsqueeze(3).broadcast_to([128, 8, 2, 128]), ALU.mult)
    for half in range(2):
        pt = nextpst(cx)
        for j in range(8):
            h = half * 8 + j
            _tr(mk, [L.vT, cx.identb], [pt], pt[:, j * 128:(j + 1) * 128], L.vT[:, h, b0:b0 + 128], cx.identb[:])
        _tt(mk, 'dve', [pt, s['beta']], [L.Vb], L.Vb[:, half * 8:half * 8 + 8, :], pt[:].rearrange("p (a i) -> p a i", a=8),
            s['beta'][:, half * 8:half * 8 + 8].unsqueeze(2).broadcast_to([128, 8, 128]), ALU.mult)
    for hg in range(4):
        ps = nextps(cx)
        for hh in range(4):
            h = hg * 4 + hh
            _mm(mk, [L.Kbg, TTt(h)], [ps], ps[:, hh * 128:(hh + 1) * 128], L.Kbg[:, h, :], TTa(h))
        _act(mk, [ps], [L.nWT], L.nWT[:, hg * 4:hg * 4 + 4, :].rearrange("p h i -> p (h i)"), ps[:], AF.Copy, scale=-1.0)
    for hg in range(4):
        ps = nextps(cx)
        for hh in range(4):
            h = hg * 4 + hh
            _mm(mk, [TTt(h), L.Vb], [ps], ps[:, hh * 128:(hh + 1) * 128], TTa(h), L.Vb[:, h, :], start=True, stop=False)
            _mm(mk, [L.nWT, Sb], [ps], ps[:, hh * 128:(hh + 1) * 128], L.nWT[:, h, :], Sb[:, h, :], start=False, stop=True)
        _copy(mk, 'dve' if hg % 2 else 'act', [ps], [L.vn], L.vn[:, hg * 4:hg * 4 + 4, :].rearrange("p h i -> p (h i)"), ps[:])
    for hg in range(4):
        ps = nextps(cx)
        for hh in range(4):
            h = hg * 4 + hh
            _mm(mk, [L.QgT, Sb], [ps], ps[:, hh * 128:(hh + 1) * 128], L.QgT[:, h, :], Sb[:, h, :], start=True, stop=False)
            _mm(mk, [L.QKT, L.vn], [ps], ps[:, hh * 128:(hh + 1) * 128], L.QKT[:, h, :], L.vn[:, h, :], start=False, stop=True)
        _copy(mk, 'act', [ps], [L.o32], L.o32[:, hg * 4:hg * 4 + 4, :].rearrange("p h i -> p (h i)"), ps[:])
    for hg in range(4):
        ps = nextps(cx)
        for hh in range(4):
            h = hg * 4 + hh
            _mm(mk, [L.Kd, L.vn], [ps], ps[:, hh * 128:(hh + 1) * 128], L.Kd[:, h, :], L.vn[:, h, :])
        for hh in range(4):
            h = hg * 4 + hh
            _stt(mk, [S32, s['egl'], ps], [S32], S32[:, h, :], S32[:, h, :], s['egl'][:, h:h + 1],
                 ps[:, hh * 128:(hh + 1) * 128], ALU.mult, ALU.add)
    _copy(mk, 'pool', [S32], [Sb], Sb[:], S32[:])


def gdn_out(cx, L, blk, x_tile, x1, gate, woa):
    mk = cx.mk
    s = L.sm
    osq = L.AO[:, :, 0, :]
    _tt(mk, 'pool', [L.o32], [L.AO], osq, L.o32[:], L.o32[:], ALU.mult)
    mk.op('dve', [L.AO], [s['ss']], lambda e: e.tensor_reduce(out=s['ss'][:], in_=osq, axis=AX.X, op=ALU.add))
    _act(mk, [s['ss'], cx.epsc], [s['rstd']], s['rstd'][:], s['ss'][:], AF.Sqrt, scale=1.0 / 128, bias=cx.epsc[:, 0:1])
    mk.op('dve', [s['rstd']], [s['rstd']], lambda e: e.reciprocal(out=s['rstd'][:], in_=s['rstd'][:]))
    _tt(mk, 'dve', [L.o32, s['rstd']], [L.o32], L.o32[:], L.o32[:], s['rstd'][:].unsqueeze(2).broadcast_to([128, 16, 128]), ALU.mult)
    _tt(mk, 'pool', [L.o32, L.zg], [L.og], L.og[:].rearrange("p h i -> p (h i)"), L.o32[:].rearrange("p h i -> p (h i)"), L.zg[:, blk, :], ALU.mult)
    for half in range(2):
        pt = nextpst(cx)
        for j in range(8):
            h = half * 8 + j
            _tr(mk, [L.og, cx.identb], [pt], pt[:, j * 128:(j + 1) * 128], L.og[:, h, :], cx.identb[:])
        _copy(mk, 'act' if half else 'dve', [pt], [L.ogT], L.ogT[:, half * 8:half * 8 + 8, :], pt[:].rearrange("p (h i) -> p h i", h=8))
    for half in range(2):
        ps = nextps(cx)
        for h in range(16):
            _mm(mk, [L.ogT, woa], [ps], ps[:], L.ogT[:, h, :], woa[:, h, half * 512:(half + 1) * 512], start=(h == 0), stop=(h == 15))
        cs = slice(half * 512, (half + 1) * 512)
        _tt(mk, 'dve', [ps, gate], [L.t32], L.t32[:, cs], ps[:], gate[:, cs], ALU.mult)
        _tt(mk, 'pool', [L.t32, x_tile], [L.t32], L.t32[:, cs], L.t32[:, cs], x_tile[:, blk, cs], ALU.add)


def conv_state_out(cx, L, es, out_ap, name):
    mk = cx.mk
    h2 = mk.tile([128, 3, 32], F32, 'h2' + name, es=es)
    _copy(mk, 'dve', [L.hist32], [h2], h2[:], L.hist32[:].rearrange("p f j -> p j f"))
    pc = nextps(cx)
    mk.op('pe', [h2, cx.identf], [pc], lambda e: e.transpose(out=pc[0:96, 0:128], in_=h2[:].rearrange("p j f -> p (j f)"), identity=cx.identf[:]))
    h3 = mk.tile([128, 128], F32, 'h3' + name, es=es)
    _copy(mk, 'dve', [pc], [h3], h3[0:96, :], pc[0:96, 0:128])
    _store(mk, h3, out_ap, h3[0:96, :])


def l0_project(cx, L, gi, last, T=None, cpos=None, hpos=0):
    mk = cx.mk
    T = T or L.T
    nb = T // 128
    cpos = T if cpos is None else cpos
    for ct in range(25):
        wb = L.wbuf[ct % 2]
        w = 256 if ct < 24 else 32
        _load(mk, wb, wb[:, :, :w], cx.wina_bf[:, :, ct * 256:ct * 256 + w], reads=[cx.wina_bf])
        if ct < 16:
            for j4 in range(2):
                f = ct * 2 + j4
                pa = nextps(cx)
                for kc in range(8):
                    _mm(mk, [wb, L.hT], [pa], pa[:, :T], wb[:, kc, j4 * 128:(j4 + 1) * 128], L.hT[:, kc, :T],
                        start=(kc == 0), stop=(kc == 7))
                raw = L.raw[f % 2]
                _copy(mk, 'act', [pa], [raw], raw[:, 3:3 + T], pa[:, :T])
                _copy(mk, 'pool', [L.hist], [raw], raw[:, hpos:hpos + 3], L.hist[:, f, :])
                if last:
                    _copy(mk, 'act', [pa], [L.hist32], L.hist32[:, f, :], pa[:, cpos - 3:cpos])
                dg = L.diag[f % 2]
                for j in range(4):
                    _ts(mk, 'pool', [cx.identb, cx.cw], [dg], dg[:, j, :], cx.identb[:], cx.cw[:, f, j:j + 1], ALU.mult)
                pb = nextps(cx)
                for j in range(4):
                    _mm(mk, [dg, raw], [pb], pb[:, :T], dg[:, j, :], raw[:, j:j + T], start=(j == 0), stop=(j == 3))
                dt_, dst = (L.qkT, L.qkT[:, f, :T]) if f < 16 else (L.vT, L.vT[:, f - 16, :T])
                _act(mk, [pb], [dt_], dst, pb[:, :T], AF.Silu)
                _copy(mk, 'pool', [raw], [L.hist], L.hist[:, f, :], raw[:, cpos:cpos + 3])
        elif ct < 24:
            for blk in range(nb):
                pz = nextps(cx)
                for kc in range(8):
                    _mm(mk, [wb, L.hT], [pz], pz[:, :256], L.hT[:, kc, blk * 128:(blk + 1) * 128], wb[:, kc, :],
                        start=(kc == 0), stop=(kc == 7))
                _act(mk, [pz], [L.zg], L.zg[:, blk, (ct - 16) * 256:(ct - 15) * 256], pz[:, :256], AF.Silu)
        else:
            for blk in range(nb):
                pz = nextps(cx)
                for kc in range(8):
                    _mm(mk, [wb, L.hT], [pz], pz[:, 0:32], L.hT[:, kc, blk * 128:(blk + 1) * 128], wb[:, kc, 0:32],
                        start=(kc == 0), stop=(kc == 7))
                _copy(mk, 'dve', [pz], [L.ba], L.ba[:, blk, :], pz[:, 0:32])
    for blk in range(nb):
        _tt(mk, 'pool', [L.zg, cx.ogain], [L.zg], L.zg[:, blk, :].rearrange("p (h i) -> p h i", h=16),
            L.zg[:, blk, :].rearrange("p (h i) -> p h i", h=16),
            cx.ogain[:].unsqueeze(1).broadcast_to([128, 16, 128]), ALU.mult)
    for f in range(16):
        sq, r32 = L.sq[f % 2], L.r32[f % 2]
        _tt(mk, 'dve', [L.qkT], [sq], sq[:, :T], L.qkT[:, f, :T], L.qkT[:, f, :T], ALU.mult)
        ps = nextps(cx)
        _mm(mk, [cx.onesb, sq], [ps], ps[:, :T], cx.onesb[:], sq[:, :T])
        if f < 8:
            _act(mk, [ps, cx.eps128], [r32], r32[:, :T], ps[:, :T], AF.Sqrt, scale=128.0, bias=cx.eps128[:, 0:1])
        else:
            _act(mk, [ps, cx.epsc], [r32], r32[:, :T], ps[:, :T], AF.Sqrt, bias=cx.epsc[:, 0:1])
        mk.op('dve', [r32], [r32], lambda e, r32=r32: e.reciprocal(out=r32[:, :T], in_=r32[:, :T]))
        _tt(mk, 'pool', [L.qkT, r32], [L.qkT], L.qkT[:, f, :T], L.qkT[:, f, :T], r32[:, :T], ALU.mult)


def alloc_l1(cx, es):
    mk = cx.mk
    B = Ctx()
    B.x = mk.tile([128, 1, D], F32, 'bx', es=es)
    B.hT = mk.tile([128, 8, 128], BF16, 'bhT', es=es)
    B.t32 = mk.tile([128, D], F32, 'bt32', es=es)
    B.hb = mk.tile([128, D], BF16, 'bhb', es=es)
    B.nss = mk.tile([128, 4], F32, 'bnss', es=es)
    B.nrs = mk.tile([128, 4], F32, 'bnrs', es=es)
    B.sq = mk.tile([128, D], F32, 'bsq', es=es)
    B.ss = mk.tile([128, 16], F32, 'bss', es=es)
    B.rs = mk.tile([128, 16], F32, 'brs', es=es)
    B.kn = mk.tile([128, 16, 64], F32, 'bkn', es=es)
    B.ko = mk.tile([128, 16, 64], F32, 'bko', es=es)
    B.rt = [mk.tile([128, 16, 8], F32, f'brt{i}', es=es) for i in range(4)]
    B.cs = mk.tile([128, 2, 8], F32, 'bcs', es=es)
    B.kb = mk.tile([128, 8, 128], BF16, 'bkb', es=es)
    B.kT = mk.tile([128, 8, 128], BF16, 'bkT', es=es)
    B.v32 = mk.tile([128, 8, 128], F32, 'bv32', es=es)
    B.vb = mk.tile([128, 8, VW], BF16, 'bvb', es=es)
    B.sc = {'junk': B.hb, 'ss': B.nss, 'rstd': B.nrs, 'hb': B.hb, 't32': B.t32}
    return B


def headnorm_rope(cx, B, src_list, nh, gain, cs_ap):
    mk = cx.mk
    ng = 2 * nh
    _load(mk, B.cs, B.cs[:], cs_ap)
    for i, ps in enumerate(src_list):
        w = min(512, nh * 128 - i * 512)
        _act(mk, [ps], [B.sq], B.sq[:, i * 512:i * 512 + w], ps[:, :w], AF.Square)
    mk.op('dve', [B.sq], [B.ss], lambda e: e.tensor_reduce(
        out=B.ss[:, :ng], in_=B.sq[:, :ng * 64].rearrange("p (g d) -> p g d", d=64), axis=AX.X, op=ALU.add))
    _act(mk, [B.ss, cx.epsc], [B.rs], B.rs[:, :ng], B.ss[:, :ng], AF.Sqrt, scale=1.0 / 64, bias=cx.epsc[:, 0:1])
    mk.op('dve', [B.rs], [B.rs], lambda e: e.reciprocal(out=B.rs[:, :ng], in_=B.rs[:, :ng]))
    for i, ps in enumerate(src_list):
        w = min(512, nh * 128 - i * 512)
        g0, gn = i * 8, w // 64
        _tt(mk, 'dve', [ps, B.rs], [B.kn], B.kn[:, g0:g0 + gn, :], ps[:, :w].rearrange("p (g d) -> p g d", d=64),
            B.rs[:, g0:g0 + gn].unsqueeze(2).broadcast_to([128, gn, 64]), ALU.mult)
    _tt(mk, 'pool', [B.kn, gain], [B.ko], B.ko[:, :ng, :], B.kn[:, :ng, :], gain[:].unsqueeze(1).broadcast_to([128, ng, 64]), ALU.mult)
    cosb = B.cs[:, 0:1, :].broadcast_to([128, ng, 8])
    sinb = B.cs[:, 1:2, :].broadcast_to([128, ng, 8])
    x1, x2 = B.ko[:, :ng, 0:8], B.ko[:, :ng, 8:16]
    t = B.rt
    _tt(mk, 'pool', [B.ko, B.cs], [t[0]], t[0][:, :ng, :], x1, cosb, ALU.mult)
    _tt(mk, 'pool', [B.ko, B.cs], [t[1]], t[1][:, :ng, :], x2, sinb, ALU.mult)
    _tt(mk, 'pool', [B.ko, B.cs], [t[2]], t[2][:, :ng, :], x2, cosb, ALU.mult)
    _tt(mk, 'pool', [B.ko, B.cs], [t[3]], t[3][:, :ng, :], x1, sinb, ALU.mult)
    _tt(mk, 'pool', [t[0], t[1]], [B.ko], x1, t[0][:, :ng, :], t[1][:, :ng, :], ALU.subtract)
    _tt(mk, 'pool', [t[2], t[3]], [B.ko], x2, t[2][:, :ng, :], t[3][:, :ng, :], ALU.add)


def kv_block(cx, B, x_ap, Akv, Shkv, wkv, nh, kgain, cs_ap, k_out_ap, v_out_ap, KTs_ap, Vs_ap, KTs, Vs, x_reads=()):
    mk = cx.mk
    _load(mk, B.x, B.x[:, 0, :], x_ap, reads=x_reads)
    rms_mod_T(cx, B.x, 1, Akv, Shkv, B.hT, B.sc)
    nk = nh * 128
    kps = []
    for i in range((nk + 511) // 512):
        ps = nextps(cx)
        w = min(512, nk - i * 512)
        for kc in range(8):
            _mm(mk, [B.hT, wkv], [ps], ps[:, :w], B.hT[:, kc, :], wkv[:, kc, i * 512:i * 512 + w], start=(kc == 0), stop=(kc == 7))
        kps.append(ps)
    if KVSTEP < 2:
        return
    headnorm_rope(cx, B, kps, nh, kgain, cs_ap)
    if KVSTEP < 3:
        return
    _store(mk, B.ko, k_out_ap, B.ko[:, :2 * nh, :].rearrange("p g d -> p (g d)"))
    _copy(mk, 'act', [B.ko], [B.kb], B.kb[:, :nh, :], B.ko[:, :2 * nh, :].rearrange("p (h c) d -> p h (c d)", c=2))
    pt = nextpst(cx)
    for h in range(nh):
        _tr(mk, [B.kb, cx.identb], [pt], pt[:, h * 128:(h + 1) * 128], B.kb[:, h, :], cx.identb[:])
    _copy(mk, 'dve', [pt], [B.kT], B.kT[:, :nh, :], pt[:, :nh * 128].rearrange("p (h t) -> p h t", h=nh))
    if KVSTEP < 4:
        return
    for h in range(nh):
        _store(mk, B.kT, KTs_ap[h], B.kT[:, h, :], writes=[KTs], is_output=False)
    if KVSTEP < 5:
        return
    for i in range((nk + 511) // 512):
        ps = nextps(cx)
        w = min(512, nk - i * 512)
        for kc in range(8):
            _mm(mk, [B.hT, wkv], [ps], ps[:, :w], B.hT[:, kc, :], wkv[:, kc, nk + i * 512:nk + i * 512 + w], start=(kc == 0), stop=(kc == 7))
        hh = w // 128
        if VVAR != 1:
            _copy(mk, 'act' if VVAR == 0 else 'dve', [ps], [B.v32], B.v32[:, i * 4:i * 4 + hh, :].rearrange("p h e -> p (h e)"), ps[:, :w])
        if VVAR != 2:
            _copy(mk, 'dve', [ps], [B.vb], B.vb[:, i * 4:i * 4 + hh, 0:128], ps[:, :w].rearrange("p (h e) -> p h e", e=128))
    if KVSTEP < 7:
        return
    _store(mk, B.v32, v_out_ap, B.v32[:, :nh, :].rearrange("p h e -> p (h e)"))
    if KVSTEP < 8:
        return
    for h in range(nh):
        _store(mk, B.vb, Vs_ap[h], B.vb[:, h, :], writes=[Vs], is_output=False)


def alloc_attn(cx, es, nh):
    mk = cx.mk
    A = Ctx()
    A.ktile = [mk.tile([128, 4096], BF16, f'akt{i}', es=es) for i in range(2)]
    A.vtile = [mk.tile([128, 32, VW], BF16, f'avt{i}', es=es) for i in range(2)]
    A.QT = mk.tile([128, nh, 2, 128], BF16, 'aQT', es=es)
    A.zb = mk.tile([128, nh * 128], BF16, 'azb', es=es)
    A.PT = [mk.tile([128, 2, 2, 128], BF16, f'aPT{i}', es=es) for i in range(2)]
    A.oatt = mk.tile([128, nh, 128], F32, 'aoatt', es=es)
    A.ot = mk.tile([128, 128], F32, 'aot', es=es)
    A.rr = mk.tile([128, 2], F32, 'arr', es=es)
    A.ss = mk.tile([128, 8], F32, 'ass', es=es)
    A.og = mk.tile([128, nh, 128], BF16, 'aog', es=es)
    A.ogT = mk.tile([128, nh, 128], BF16, 'aogT', es=es)
    A.y = mk.tile([128, D], F32, 'ay', es=es)
    return A


def l1_qz(cx, B, A, Ab, Shb, winb, nh, qgain, cs_ap, zgain):
    mk = cx.mk
    rms_mod_T(cx, B.x, 1, Ab, Shb, B.hT, B.sc)
    nk = nh * 128
    qps = []
    for i in range((nk + 511) // 512):
        ps = nextps(cx)
        w = min(512, nk - i * 512)
        for kc in range(8):
            _mm(mk, [B.hT, winb], [ps], ps[:, :w], B.hT[:, kc, :], winb[:, kc, i * 512:i * 512 + w], start=(kc == 0), stop=(kc == 7))
        qps.append(ps)
    headnorm_rope(cx, B, qps, nh, qgain, cs_ap)
    _copy(mk, 'act', [B.ko], [B.kb], B.kb[:, :nh, :], B.ko[:, :2 * nh, :].rearrange("p (h c) d -> p h (c d)", c=2))
    pt = nextpst(cx)
    for h in range(nh):
        _tr(mk, [B.kb, cx.identb], [pt], pt[:, h * 128:(h + 1) * 128], B.kb[:, h, :], cx.identb[:])
    for c in range(2):
        _ts(mk, 'dve', [pt, cx.cm], [A.QT], A.QT[:, :, c, :], pt[:, :nh * 128].rearrange("p (h t) -> p h t", h=nh), cx.cm[:, c:c + 1], ALU.mult)
    for i in range((nk + 511) // 512):
        ps = nextps(cx)
        w = min(512, nk - i * 512)
        for kc in range(8):
            _mm(mk, [B.hT, winb], [ps], ps[:, :w], B.hT[:, kc, :], winb[:, kc, nk + i * 512:nk + i * 512 + w], start=(kc == 0), stop=(kc == 7))
        _act(mk, [ps], [A.zb], A.zb[:, i * 512:i * 512 + w], ps[:, :w], AF.Silu)
    _tt(mk, 'pool', [A.zb, zgain], [A.zb], A.zb[:].rearrange("p (h e) -> p h e", e=128),
        A.zb[:].rearrange("p (h e) -> p h e", e=128), zgain[:].unsqueeze(1).broadcast_to([128, nh, 128]), ALU.mult)


def attn_head(cx, A, h, kt, vt, nkb, masks, nlam):
    mk = cx.mk
    acc = [cx.ps[4], cx.ps[5]]
    npair = (nkb + 1) // 2
    for kp in range(npair):
        ps = nextps4(cx)
        PT = A.PT[kp % 2]
        nj = min(2, nkb - kp * 2)
        for j in range(nj):
            kb = kp * 2 + j
            _mm(mk, [kt, A.QT], [ps], ps[:, j * 256:(j + 1) * 256], kt[:, kb * 128:(kb + 1) * 128],
                A.QT[:, h, :, :].rearrange("p c q -> p (c q)"))
        _act(mk, [ps, cx.m4c], [PT], PT[:, :nj, :, :].rearrange("p j c q -> p (j c q)"), ps[:, :nj * 256], AF.Exp,
             scale=0.125, bias=cx.m4c[:, 0:1])
        for j in range(nj):
            kb = kp * 2 + j
            if kb in masks:
                mt, map_ = masks[kb]
                for c in range(2):
                    _tt(mk, 'pool', [PT, mt], [PT], PT[:, j, c, :], PT[:, j, c, :], map_, ALU.mult)
        for j in range(nj):
            kb = kp * 2 + j
            for c in range(2):
                _mm(mk, [PT, vt], [acc[c]], acc[c][:, 0:129], PT[:, j, c, :], vt[:, kb, 0:129],
                    start=(kb == 0), stop=(kb == nkb - 1))
    mk.op('dve', [acc[0]], [A.rr], lambda e: e.reciprocal(out=A.rr[:, 0:1], in_=acc[0][:, 128:129]))
    mk.op('dve', [acc[1]], [A.rr], lambda e: e.reciprocal(out=A.rr[:, 1:2], in_=acc[1][:, 128:129]))
    _tt(mk, 'dve', [A.rr, nlam], [A.rr], A.rr[:, 1:2], A.rr[:, 1:2], nlam[:, 0:1], ALU.mult)
    _ts(mk, 'dve', [acc[0], A.rr], [A.ot], A.ot[:], acc[0][:, 0:128], A.rr[:, 0:1], ALU.mult)
    _stt(mk, [acc[1], A.rr, A.ot], [A.oatt], A.oatt[:, h, :], acc[1][:, 0:128], A.rr[:, 1:2], A.ot[:], ALU.mult, ALU.add)


def attn_post(cx, B, A, nh):
    mk = cx.mk
    _tt(mk, 'pool', [A.oatt], [B.sq], B.sq[:, :nh * 128].rearrange("p (h e) -> p h e", e=128), A.oatt[:], A.oatt[:], ALU.mult)
    mk.op('dve', [B.sq], [A.ss], lambda e: e.tensor_reduce(
        out=A.ss[:, :nh], in_=B.sq[:, :nh * 128].rearrange("p (h e) -> p h e", e=128), axis=AX.X, op=ALU.add))
    _act(mk, [A.ss, cx.epsc], [A.ss], A.ss[:, :nh], A.ss[:, :nh], AF.Sqrt, scale=1.0 / 128, bias=cx.epsc[:, 0:1])
    mk.op('dve', [A.ss], [A.ss], lambda e: e.reciprocal(out=A.ss[:, :nh], in_=A.ss[:, :nh]))
    _tt(mk, 'dve', [A.oatt, A.ss], [A.oatt], A.oatt[:], A.oatt[:], A.ss[:, :nh].unsqueeze(2).broadcast_to([128, nh, 128]), ALU.mult)
    _tt(mk, 'pool', [A.oatt, A.zb], [A.og], A.og[:], A.oatt[:], A.zb[:].rearrange("p (h e) -> p h e", e=128), ALU.mult)
    pt = nextpst(cx)
    for h in range(nh):
        _tr(mk, [A.og, cx.identb], [pt], pt[:, h * 128:(h + 1) * 128], A.og[:, h, :], cx.identb[:])
    _copy(mk, 'act', [pt], [A.ogT], A.ogT[:], pt[:, :nh * 128].rearrange("p (h t) -> p h t", h=nh))


def out_proj_res(cx, A, ogT, nh_all, wob, x_tile_ap, x_tile, Gb, y_out_ap):
    mk = cx.mk
    for half in range(2):
        ps = nextps(cx)
        for h in range(nh_all):
            _mm(mk, [ogT, wob], [ps], ps[:], ogT[:, h, :], wob[:, h, half * 512:(half + 1) * 512], start=(h == 0), stop=(h == nh_all - 1))
        cs = slice(half * 512, (half + 1) * 512)
        _tt(mk, 'dve', [ps, Gb], [A.y], A.y[:, cs], ps[:], Gb[:, cs], ALU.mult)
        _tt(mk, 'pool', [A.y, x_tile], [A.y], A.y[:, cs], A.y[:, cs], x_tile_ap[:, cs], ALU.add)
    _store(mk, A.y, y_out_ap, A.y[:])


LAM_INIT = 0.8 - 0.6 * math.exp(-0.3 * 1)
U32 = mybir.dt.uint32


def phase_mods(cx, din, es, nm, w, cT_name):
    mk = cx.mk
    A_ = mk.tile([128, D], BF16, 'A' + nm, es=es); Sh_ = mk.tile([128, D], BF16, 'Sh' + nm, es=es)
    G_ = mk.tile([128, D], F32, 'G' + nm, es=es) if w == 3 else None
    with ExitStack() as est:
        cT = mk.tile([128, 8, 128], F32, 'cT' + nm, es=est); _load(mk, cT, cT[:], din[cT_name])
        gn = mk.tile([128, D], F32, 'gn' + nm, es=est); _load(mk, gn, gn[:], din[f'norm_{nm}'])
        o2 = [mk.tile([128, D], F32, f'ada_o{nm}{i}', es=est) for i in range(2)]
        st = alloc_stage(cx, est)
        ada_mod(cx, cT, din[f'ada_w_{nm}'], din[f'ada_b_{nm}'], w * D, [o2[0], o2[1]] + ([G_] if w == 3 else []), st)
        _copy(mk, 'pool', [o2[0]], [Sh_], Sh_[:], o2[0][:])
        _stt(mk, [o2[1], gn], [A_], A_[:], o2[1][:], 1.0, gn[:], ALU.add, ALU.mult)

    barrier(mk)
    return A_, Sh_, G_


def build_program(NBLK=32, with_sample=False):
    nc = bass.Bass("TRN2", target_bir_lowering=False)
    T = 256
    NT = NBLK * 128
    NG = NT // T
    NSLOT = NBLK // 2
    din = {}

    def inp(name, shape, dt=F32):
        din[name] = nc.dram_tensor(name, list(shape), dt, kind="ExternalInput").ap()

    def outp(name, shape, dt=F32):
        return nc.dram_tensor(name, list(shape), dt, kind="ExternalOutput").ap()

    inp('x_p', [NT, D]); inp('cT_p', [128, 8, 128])
    for nm, w in (('a', 3), ('kv', 2), ('b', 3)):
        inp(f'ada_w_{nm}', [D, w * D]); inp(f'ada_b_{nm}', [128, w * D]); inp(f'norm_{nm}', [128, D])
    inp('w_in_a', [D, NIN]); inp('cw', [128, 32, 4]); inp('a_log', [128, 16]); inp('dt_bias', [128, 16])
    inp('ogain', [128, 128]); inp('w_out_a', [2048, D])
    inp('w_kv', [D, 2048]); inp('kgain', [128, 64])
    inp('w_in_b', [D, 2048]); inp('qgain', [128, 64]); inp('lam', [128, 4, 64]); inp('subgain', [128, 128])
    inp('w_out_b', [D, D])
    inp('cs_p', [NT, 2, 8]); inp('cs_q', [NSLOT * 128, 2, 8]); inp('amask', [128, 2, 128]); inp('qtok', [128, NSLOT], U32)
    for k in ['m1', 'm2', 'nm_strict', 'nm_incl']:
        inp('p_' + k, [128, 128])
    inp('p_ml', [128, 7, 2, 128])
    inp('x_s', [128, D]); inp('cT_s', [128, 8, 128]); inp('S_s', [4, 16, 128, 128]); inp('conv_s_in', [4, 96, 128]); inp('vm_s', [128, 4])
    S_s_out = outp('S_s_out', [4, 16, 128, 128]); conv_s_out = outp('conv_s_out', [4, 96, 128]); x1loc = outp('x1loc', [32, D])
    y_p = outp('y_p', [NSLOT * 128, D]); S_p = outp('S_p', [16, 128, 128]); conv_p = outp('conv_p', [96, 128])
    k_p = outp('k_p', [NT, D]); v_p = outp('v_p', [NT, D])

    with ExitStack() as es:
        mk = MK(nc, es)
        cx = Ctx()
        cx.mk = mk
        cx.ps = [mk.psum([128, 512], F32, f'ps{i}') for i in range(6)]
        cx.pst = [mk.psum([128, 1024], BF16, f'pst{i}') for i in range(2)]
        cx.psi = 0
        cx.psti = 0
        setup_consts(cx, din)
        cx.epsc = mk.tile([128, 1], F32, 'epsc')
        cx.onec = mk.tile([128, 1], F32, 'onec')
        cx.eps128 = mk.tile([128, 1], F32, 'eps128')
        cx.m4c = mk.tile([128, 1], F32, 'm4c')
        cx.cm = mk.tile([128, 2], F32, 'cm')
        mk.op('pool', [], [cx.cm], lambda e: e.memset(cx.cm[:], 0.0))
        mk.op('pool', [cx.cm], [cx.cm], lambda e: e.memset(cx.cm[0:64, 0:1], 1.0))
        mk.op('pool', [cx.cm], [cx.cm], lambda e: e.memset(cx.cm[64:128, 1:2], 1.0))
        mk.op('pool', [], [cx.epsc], lambda e: e.memset(cx.epsc[:], EPS))
        mk.op('pool', [], [cx.onec], lambda e: e.memset(cx.onec[:], 1.0))
        mk.op('pool', [], [cx.eps128], lambda e: e.memset(cx.eps128[:], 128.0 * EPS))
        mk.op('pool', [], [cx.m4c], lambda e: e.memset(cx.m4c[:], -4.0))
        cx.cw = mk.tile([128, 32, 4], F32, 'cw'); _load(mk, cx.cw, cx.cw[:], din['cw'])
        cx.dtb = mk.tile([128, 16], F32, 'dtb'); _load(mk, cx.dtb, cx.dtb[:], din['dt_bias'])
        cx.nea = mk.tile([128, 16], F32, 'nea'); _load(mk, cx.nea, cx.nea[:], din['a_log'])
        _act(mk, [cx.nea], [cx.nea], cx.nea[:], cx.nea[:], AF.Exp)
        _ts(mk, 'dve', [cx.nea], [cx.nea], cx.nea[:], cx.nea[:], -1.0, ALU.mult)
        cx.ogain = mk.tile([128, 128], F32, 'ogain'); _load(mk, cx.ogain, cx.ogain[:], din['ogain'])
        kgain = mk.tile([128, 64], F32, 'kgain'); _load(mk, kgain, kgain[:], din['kgain'])
        qgain = mk.tile([128, 64], F32, 'qgain'); _load(mk, qgain, qgain[:], din['qgain'])
        zgain = mk.tile([128, 128], F32, 'zgain'); _load(mk, zgain, zgain[:], din['subgain'])
        _ts(mk, 'dve', [zgain], [zgain], zgain[:], zgain[:], 1.0 - LAM_INIT, ALU.mult)
        amask = mk.tile([128, 2, 128], BF16, 'amask')
        qtok = mk.tile([128, NSLOT], U32, 'qtok'); _load(mk, qtok, qtok[:], din['qtok'])
        ls = mk.tile([128, 2], F32, 'ls'); nlam = mk.tile([128, 1], F32, 'nlam')
        A0s = mk.tile([128, D], BF16, 'A0s'); Sh0s = mk.tile([128, D], BF16, 'Sh0s'); G0s = mk.tile([128, D], F32, 'G0s')
        vms = mk.tile([128, 4], F32, 'vms'); _load(mk, vms, vms[:], din['vm_s'])
        mods = {}
        for nm in ('a',):
            mods[nm] = (mk.tile([128, D], BF16, 'A' + nm), mk.tile([128, D], BF16, 'Sh' + nm),
                        mk.tile([128, D], F32, 'G' + nm) if nm != 'kv' else None)
        cx.wina_bf = mk.dram('wina_bf', [128, 8, NIN], BF16)
        woa_d = mk.dram('woa_bf', [128, 16, D], BF16)
        wkv_d = mk.dram('wkv_bf', [128, 8, 2048], BF16)
        winb_d = mk.dram('winb_bf', [128, 8, 2048], BF16)
        wob_d = mk.dram('wob_bf', [128, 8, D], BF16)
        x1s = mk.dram('x1s', [NT, D], F32)
        KTs = mk.dram('scr_KTs', [8, 128, NT], BF16, kind='ExternalOutput')
        Vs = mk.dram('scr_Vs', [8, 128, NBLK, VW], BF16, kind='ExternalOutput')
        with ExitStack() as es1:
            Mp = load_masks(cx, din, 'p_', 7, es1)
            amf = mk.tile([128, 2, 128], F32, 'amf', es=es1); _load(mk, amf, amf[:], din['amask'])
            _copy(mk, 'pool', [amf], [amask], amask[:], amf[:])
            lamt = mk.tile([128, 4, 64], F32, 'lamt', es=es1); _load(mk, lamt, lamt[:], din['lam'])
            lp = mk.tile([128, 2, 64], F32, 'lp', es=es1)
            _tt(mk, 'dve', [lamt], [lp], lp[:], lamt[:].rearrange("p (a b) d -> p a b d", b=2)[:, :, 0, :],
                lamt[:].rearrange("p (a b) d -> p a b d", b=2)[:, :, 1, :], ALU.mult)
            mk.op('dve', [lp], [ls], lambda e: e.tensor_reduce(out=ls[:], in_=lp[:], axis=AX.X, op=ALU.add))
            _act(mk, [ls], [ls], ls[:], ls[:], AF.Exp)
            _tt(mk, 'dve', [ls], [nlam], nlam[:], ls[:, 1:2], ls[:, 0:1], ALU.subtract)
            _ts(mk, 'dve', [nlam], [nlam], nlam[:], nlam[:], -LAM_INIT, ALU.add)
            cT = mk.tile([128, 8, 128], F32, 'cT', es=es1); _load(mk, cT, cT[:], din['cT_p'])
            gn = mk.tile([128, D], F32, 'gn', es=es1)
            o3 = [mk.tile([128, D], F32, f'ada_o{i}', es=es1) for i in range(3)]
            st = alloc_stage(cx, es1)
            for nm, w in (('a', 3),):
                A_, Sh_, G_ = mods[nm]
                _load(mk, gn, gn[:], din[f'norm_{nm}'])
                outs = [o3[0], o3[1]] + ([G_] if w == 3 else [])
                ada_mod(cx, cT, din[f'ada_w_{nm}'], din[f'ada_b_{nm}'], w * D, outs, st)
                _copy(mk, 'pool', [o3[0]], [Sh_], Sh_[:], o3[0][:])
                _stt(mk, [o3[1], gn], [A_], A_[:], o3[1][:], 1.0, gn[:], ALU.add, ALU.mult)
            cTs = mk.tile([128, 8, 128], F32, 'cTs', es=es1); _load(mk, cTs, cTs[:], din['cT_s'])
            _load(mk, gn, gn[:], din['norm_a'])
            ada_mod(cx, cTs, din['ada_w_a'], din['ada_b_a'], 3 * D, [o3[0], o3[1], G0s], st)
            _copy(mk, 'pool', [o3[0]], [Sh0s], Sh0s[:], o3[0][:])
            _stt(mk, [o3[1], gn], [A0s], A0s[:], o3[1][:], 1.0, gn[:], ALU.add, ALU.mult)
            prep_w_bf(cx, din['w_in_a'], 8, NIN, cx.wina_bf, st)
            prep_w_bf(cx, din['w_out_a'], 16, D, woa_d, st)
            prep_w_bf(cx, din['w_kv'], 8, 2048, wkv_d, st)
            prep_w_bf(cx, din['w_in_b'], 8, 2048, winb_d, st)
            prep_w_bf(cx, din['w_out_b'], 8, D, wob_d, st)
        barrier(mk)
        A0, Sh0, G0 = mods['a']
        with ExitStack() as es2:
            woa = mk.tile([128, 16, D], BF16, 'woa', es=es2)
            _load(mk, woa, woa[:], woa_d[:], reads=[woa_d])
            L = alloc_l0(cx, es2, T)
            S32 = mk.tile([128, 16, 128], F32, 'S32', es=es2)
            Sb = mk.tile([128, 16, 128], BF16, 'Sb', es=es2)
            mk.op('pool', [], [S32], lambda e: e.memset(S32[:], 0.0))
            mk.op('pool', [], [Sb], lambda e: e.memset(Sb[:], 0.0))
            mk.op('pool', [], [L.hist], lambda e: e.memset(L.hist[:], 0.0))
            sc_t = {'junk': L.junk, 'ss': L.nss, 'rstd': L.nrs, 'hb': L.hb, 't32': L.t32}
            if with_sample:
                xs = L.x[0]
                _load(mk, xs, xs[:, 0, :], din['x_s'])
                rms_mod_T(cx, xs, 1, A0s, Sh0s, L.hT, sc_t)
                hs_in = mk.tile([128, 128], F32, 'hs_in', es=es2)
                mk.op('pool', [], [hs_in], lambda e: e.memset(hs_in[:], 0.0))
                for lb in range(4):
                    _load(mk, hs_in, hs_in[0:96, :], din['conv_s_in'][lb])
                    pc = nextps(cx)
                    mk.op('pe', [hs_in, cx.identf], [pc], lambda e, pc=pc: e.transpose(out=pc[:, 0:128], in_=hs_in[:], identity=cx.identf[:]))
                    _copy(mk, 'dve', [pc], [L.hist], L.hist[:].rearrange("p f j -> p j f"), pc[:, 0:96].rearrange("p (j f) -> p j f", j=3))
                    l0_project(cx, L, 0, True, T=128, cpos=8 * lb + 8, hpos=8 * lb)
                    _load(mk, S32, S32[:], din['S_s'][lb].rearrange("h k v -> k h v"))
                    _copy(mk, 'pool', [S32], [Sb], Sb[:], S32[:])
                    gdn_block(cx, L, 0, Mp, S32, Sb, vm=(vms, vms[:, lb:lb + 1]))
                    gdn_out(cx, L, 0, xs, L.t32, G0s, woa)
                    _store(mk, L.t32, x1loc[8 * lb:8 * lb + 8, :], L.t32[8 * lb:8 * lb + 8, :])
                    _store(mk, S32, S_s_out[lb].rearrange("h k v -> k h v"), S32[:])
                    conv_state_out(cx, L, es2, conv_s_out[lb], f's{lb}')
                mk.op('pool', [], [S32], lambda e: e.memset(S32[:], 0.0))
                mk.op('pool', [], [Sb], lambda e: e.memset(Sb[:], 0.0))
                mk.op('pool', [], [L.hist], lambda e: e.memset(L.hist[:], 0.0))
            for gi in range(NG):
                xt = L.x[0]
                _load(mk, xt, xt[:], din['x_p'][gi * T:(gi + 1) * T, :].rearrange("(b p) d -> p b d", p=128))
                rms_mod_T(cx, xt, L.nb, A0, Sh0, L.hT, sc_t)
                l0_project(cx, L, gi, gi == NG - 1)
                for blk in range(L.nb):
                    gdn_block(cx, L, blk, Mp, S32, Sb)
                    x1 = L.t32
                    gdn_out(cx, L, blk, xt, x1, G0, woa)
                    r0 = gi * T + blk * 128
                    _store(mk, x1, x1s[r0:r0 + 128, :], x1[:], writes=[x1s], is_output=False)
            _store(mk, S32, S_p.rearrange("h k v -> k h v"), S32[:])
            conv_state_out(cx, L, es2, conv_p[:, :], 'p')
        barrier(mk)
        with ExitStack() as es3:
          if MAXPHASE >= 2:
              Akv, Shkv, _ = phase_mods(cx, din, es3, 'kv', 2, 'cT_p')
              wkv = mk.tile([128, 8, 2048], BF16, 'wkv', es=es3)
              _load(mk, wkv, wkv[:], wkv_d[:], reads=[wkv_d])
              B = alloc_l1(cx, es3)
              mk.op('pool', [], [B.vb], lambda e, B=B: e.memset(B.vb[:], 1.0))
              for blk in range(NBLK):
                  r0 = blk * 128
                  kv_block(cx, B, x1s[r0:r0 + 128, :], Akv, Shkv, wkv, 8, kgain, din['cs_p'][r0:r0 + 128, :, :],
                           k_p[r0:r0 + 128, :], v_p[r0:r0 + 128, :], KTs[:, :, r0:r0 + 128], Vs[:, :, blk, :], KTs, Vs, x_reads=[x1s])
        barrier(mk)
        with ExitStack() as es4:
          if MAXPHASE >= 3:
              Ab, Shb, Gb = phase_mods(cx, din, es4, 'b', 3, 'cT_p')
              winb = mk.tile([128, 8, 2048], BF16, 'winb', es=es4)
              wob = mk.tile([128, 8, D], BF16, 'wob', es=es4)
              _load(mk, winb, winb[:], winb_d[:], reads=[winb_d])
              _load(mk, wob, wob[:], wob_d[:], reads=[wob_d])
              B = alloc_l1(cx, es4)
              A = alloc_attn(cx, es4, 8)
              for i in range(NSLOT):
                  nkb = 2 * i + 2
                  mk.dma('pool', B.x, [x1s, qtok], [B.x], lambda e, i=i: e.indirect_dma_start(
                      out=B.x[:, 0, :], out_offset=None, in_=x1s[:, :],
                      in_offset=bass.IndirectOffsetOnAxis(ap=qtok[:, i:i + 1], axis=0)))
                  l1_qz(cx, B, A, Ab, Shb, winb, 8, qgain, din['cs_q'][i * 128:(i + 1) * 128, :, :], zgain)
                  if P3STEP < 2:
                      continue
                  masks = {nkb - 2: (amask, amask[:, 0, :]), nkb - 1: (amask, amask[:, 1, :])}
                  for h in range(8):
                      kt, vt = A.ktile[h % 2], A.vtile[h % 2]
                      _load(mk, kt, kt[:, :nkb * 128], KTs[h, :, 0:nkb * 128], reads=[KTs])
                      _load(mk, vt, vt[:, :nkb, :], Vs[h, :, 0:nkb, :], reads=[Vs])
                      attn_head(cx, A, h, kt, vt, nkb, masks, nlam)
                  if P3STEP < 3:
                      continue
                  attn_post(cx, B, A, 8)
                  if P3STEP < 4:
                      continue
                  out_proj_res(cx, A, A.ogT, 8, wob, B.x[:, 0, :], B.x, Gb, y_p[i * 128:(i + 1) * 128, :])
        mk.finish()
    return nc


def _rep(v, n=128):
    v = np.asarray(v, np.float32).reshape(1, -1)
    return np.ascontiguousarray(np.broadcast_to(v, (n, v.shape[1])))


def _rope_table(pos):
    inv_freq = (500000.0 ** (-np.arange(0, 16, 2, dtype=np.float32) / 16)).astype(np.float32)
    ang = pos.astype(np.float32)[:, None] * inv_freq[None, :]
    return np.stack([np.cos(ang), np.sin(ang)], axis=1).astype(np.float32)


def make_in_maps(inputs, NBLK=32):
    NT = NBLK * 128
    NSLOT = NBLK // 2
    hc = host_consts(False)
    g = lambda k: np.asarray(inputs[k])
    shared = {
        'ada_w_a': g('ada_w_a')[0], 'ada_b_a': _rep(g('ada_b_a')[0]), 'norm_a': _rep(g('norm_a')[0]),
        'ada_w_kv': g('ada_w_kv'), 'ada_b_kv': _rep(g('ada_b_kv')), 'norm_kv': _rep(g('norm_kv')),
        'ada_w_b': g('ada_w_b')[0], 'ada_b_b': _rep(g('ada_b_b')[0]), 'norm_b': _rep(g('norm_b')[0]),
        'w_in_a': g('w_in_a')[0],
        'cw': np.ascontiguousarray(g('conv_w_a')[0].reshape(4, 32, 128).transpose(2, 1, 0)),
        'a_log': _rep(g('a_log')[0]), 'dt_bias': _rep(g('dt_bias')[0]), 'ogain': _rep(g('gdn_out_gain')[0]),
        'w_out_a': g('w_out_a')[0], 'w_kv': g('w_kv'), 'kgain': _rep(g('k_gain')),
        'w_in_b': g('w_in_b')[0], 'qgain': _rep(g('q_gain')[0]),
        'lam': np.ascontiguousarray(np.broadcast_to(g('lam_params')[0][None], (128, 4, 64))).astype(np.float32),
        'subgain': _rep(g('subln_gain')[0]), 'w_out_b': g('w_out_b')[0],
        'cs_p': _rope_table(np.arange(NT)),
        'p_m1': hc['m1'], 'p_m2': hc['m2'], 'p_nm_strict': hc['nm_strict'], 'p_nm_incl': hc['nm_incl'], 'p_ml': hc['ml'],
    }
    i128 = np.arange(128)
    tri = (i128[:, None] <= i128[None, :]).astype(np.float32)
    maps = []
    for c in range(8):
        b, r = c // 2, c % 2
        m = dict(shared)
        m['x_p'] = np.ascontiguousarray(g('x_prompt')[b, :NT])
        cvec = g('c_prompt')[b]
        m['cT_p'] = np.ascontiguousarray(np.broadcast_to(cvec.reshape(8, 128).T[:, :, None], (128, 8, 128))).astype(np.float32)
        qblk = 2 * np.arange(NSLOT) + r
        qpos = (qblk[:, None] * 128 + i128[None, :]).reshape(-1)
        m['cs_q'] = _rope_table(qpos)
        m['qtok'] = np.ascontiguousarray((qblk[None, :] * 128 + i128[:, None]).astype(np.uint32))
        am = np.zeros((128, 2, 128), np.float32)
        if r == 0:
            am[:, 0, :] = tri
        else:
            am[:, 0, :] = 1.0
            am[:, 1, :] = tri
        m['amask'] = am
        if 'x_sample' in inputs:
            xs = np.zeros((128, D), np.float32)
            xs[:32] = g('x_sample')[4 * c:4 * c + 4].reshape(32, D)
            m['x_s'] = xs
            cs_ = np.zeros((D, 128), np.float32)
            cs_[:, :32] = np.repeat(g('c_sample')[4 * c:4 * c + 4], 8, axis=0).T
            m['cT_s'] = np.ascontiguousarray(cs_.reshape(8, 128, 128).transpose(1, 0, 2))
            m['S_s'] = np.ascontiguousarray(g('state_gdn')[0, 4 * c:4 * c + 4])
            m['conv_s_in'] = np.ascontiguousarray(g('state_conv')[0, 4 * c:4 * c + 4].reshape(4, 96, 128))
            vm = np.zeros((128, 4), np.float32)
            for lb in range(4):
                vm[8 * lb:8 * lb + 8, lb] = 1.0
            m['vm_s'] = vm
        maps.append(m)
    return maps


def _common_ctx(nc, es):
    mk = MK(nc, es)
    cx = Ctx()
    cx.mk = mk
    cx.ps = [mk.psum([128, 512], F32, f'ps{i}') for i in range(6)]
    cx.pst = [mk.psum([128, 1024], BF16, f'pst{i}') for i in range(2)]
    cx.psi = 0
    cx.psti = 0
    setup_consts(cx, {})
    cx.epsc = mk.tile([128, 1], F32, 'epsc')
    cx.m4c = mk.tile([128, 1], F32, 'm4c')
    cx.cm = mk.tile([128, 2], F32, 'cm')
    mk.op('pool', [], [cx.epsc], lambda e: e.memset(cx.epsc[:], EPS))
    mk.op('pool', [], [cx.m4c], lambda e: e.memset(cx.m4c[:], -4.0))
    mk.op('pool', [], [cx.cm], lambda e: e.memset(cx.cm[:], 0.0))
    mk.op('pool', [cx.cm], [cx.cm], lambda e: e.memset(cx.cm[0:64, 0:1], 1.0))
    mk.op('pool', [cx.cm], [cx.cm], lambda e: e.memset(cx.cm[64:128, 1:2], 1.0))
    return mk, cx


def build_sample_attn(NB=32, NPG=64):
    nc = bass.Bass("TRN2", target_bir_lowering=False)
    NTK = NB * 8
    NTILE = NTK // 128
    NGRP = NPG // 16
    din = {}

    def inp(name, shape, dt=F32):
        din[name] = nc.dram_tensor(name, list(shape), dt, kind="ExternalInput").ap()

    def outp(name, shape, dt=F32):
        return nc.dram_tensor(name, list(shape), dt, kind="ExternalOutput").ap()

    inp('x1all', [NTK, D])
    for j in range(NTILE):
        inp(f'cT_all{j}', [128, 8, 128])
    for nm, w in (('kv', 2), ('b', 3)):
        inp(f'ada_w_{nm}', [D, w * D]); inp(f'ada_b_{nm}', [128, w * D]); inp(f'norm_{nm}', [128, D])
    inp('w_kv_h', [D, 256]); inp('kgain', [128, 64]); inp('w_in_b_h', [D, 256]); inp('qgain', [128, 64])
    inp('subgain', [128, 128]); inp('lam', [128, 4, 64]); inp('cs_s', [NTK, 2, 8])
    inp('pool_k', [2560 * 8, 2048]); inp('pool_v', [2560 * 8, 2048])
    inp('pt_rep', [128, NB * NGRP], I32); inp('sub8', [128, 1]); inp('nmask', [128, 16, 8])
    og_h = outp('og_h', [NTK, 128]); k_s = outp('k_s_h', [NTK, 128]); v_s = outp('v_s_h', [NTK, 128])
    oat = outp('scr_oatt', [NTK, 128])
    KTs = nc.dram_tensor('scr_KTn', [1, 128, NTK], BF16, kind="ExternalOutput").ap()
    Vs = nc.dram_tensor('scr_Vn', [1, 128, NTILE, VW], BF16, kind="ExternalOutput").ap()
    with ExitStack() as es:
        mk, cx = _common_ctx(nc, es)
        KTs_t, Vs_t, oat_t = Tile('KTn', KTs), Tile('Vn', Vs), Tile('oat', oat)
        kgain = mk.tile([128, 64], F32, 'kgain'); _load(mk, kgain, kgain[:], din['kgain'])
        qgain = mk.tile([128, 64], F32, 'qgain'); _load(mk, qgain, qgain[:], din['qgain'])
        zgain = mk.tile([128, 128], F32, 'zgain'); _load(mk, zgain, zgain[:], din['subgain'])
        _ts(mk, 'dve', [zgain], [zgain], zgain[:], zgain[:], 1.0 - LAM_INIT, ALU.mult)
        nmf = mk.tile([128, 16, 8], F32, 'nmf'); _load(mk, nmf, nmf[:], din['nmask'])
        nmask = mk.tile([128, 16, 8], BF16, 'nmask'); _copy(mk, 'pool', [nmf], [nmask], nmask[:], nmf[:])
        lamt = mk.tile([128, 4, 64], F32, 'lamt'); _load(mk, lamt, lamt[:], din['lam'])
        lp = mk.tile([128, 2, 64], F32, 'lp'); ls = mk.tile([128, 2], F32, 'ls'); nlam = mk.tile([128, 1], F32, 'nlam')
        _tt(mk, 'dve', [lamt], [lp], lp[:], lamt[:].rearrange("p (a b) d -> p a b d", b=2)[:, :, 0, :],
            lamt[:].rearrange("p (a b) d -> p a b d", b=2)[:, :, 1, :], ALU.mult)
        mk.op('dve', [lp], [ls], lambda e: e.tensor_reduce(out=ls[:], in_=lp[:], axis=AX.X, op=ALU.add))
        _act(mk, [ls], [ls], ls[:], ls[:], AF.Exp)
        _tt(mk, 'dve', [ls], [nlam], nlam[:], ls[:, 1:2], ls[:, 0:1], ALU.subtract)
        _ts(mk, 'dve', [nlam], [nlam], nlam[:], nlam[:], -LAM_INIT, ALU.add)
        pti = mk.tile([128, NB * NGRP], I32, 'pti'); _load(mk, pti, pti[:], din['pt_rep'])
        sub8 = mk.tile([128, 1], F32, 'sub8'); _load(mk, sub8, sub8[:], din['sub8'])
        ptf = mk.tile([128, NB * NGRP], F32, 'ptf'); _copy(mk, 'dve', [pti], [ptf], ptf[:], pti[:])
        _ts(mk, 'dve', [ptf, sub8], [ptf], ptf[:], ptf[:], 8.0, ALU.mult, sub8[:, 0:1], ALU.add)
        idx = mk.tile([128, NB * NGRP], U32, 'idx'); _copy(mk, 'dve', [ptf], [idx], idx[:], ptf[:])
        wkv = mk.tile([128, 8, 256], BF16, 'wkvh'); winb = mk.tile([128, 8, 256], BF16, 'winbh')
        QTall = mk.tile([128, NTILE, 2, 128], BF16, 'QTall')
        zball = mk.tile([128, NTILE, 128], BF16, 'zball')
        KTn = mk.tile([128, NTILE, 128], BF16, 'KTnew')
        Vn = mk.tile([128, NTILE, VW], BF16, 'Vnew')
        with ExitStack() as es1:
            wf = mk.tile([128, 8, 256], F32, 'wf', es=es1)
            _load(mk, wf, wf[:], din['w_kv_h'].rearrange("(kc p) n -> p kc n", p=128))
            _copy(mk, 'pool', [wf], [wkv], wkv[:], wf[:])
            _load(mk, wf, wf[:], din['w_in_b_h'].rearrange("(kc p) n -> p kc n", p=128))
            _copy(mk, 'pool', [wf], [winb], winb[:], wf[:])
            B = alloc_l1(cx, es1)
            mk.op('pool', [], [B.vb], lambda e, B=B: e.memset(B.vb[:], 1.0))
            A = alloc_attn_small(cx, es1)
            for j in range(NTILE):
                with ExitStack() as esm:
                    Akv, Shkv, _ = phase_mods(cx, din, esm, 'kv', 2, f'cT_all{j}')
                    Ab, Shb, _g = phase_mods(cx, din, esm, 'b', 2, f'cT_all{j}')
                    r0 = j * 128
                    kv_block(cx, B, din['x1all'][r0:r0 + 128, :], Akv, Shkv, wkv, 1, kgain, din['cs_s'][r0:r0 + 128, :, :],
                             k_s[r0:r0 + 128, :], v_s[r0:r0 + 128, :], KTs[:, :, r0:r0 + 128], Vs[:, :, j, :], KTs_t, Vs_t)
                    _copy(mk, 'pool', [B.kT], [KTn], KTn[:, j, :], B.kT[:, 0, :])
                    _copy(mk, 'pool', [B.vb], [Vn], Vn[:, j, :], B.vb[:, 0, :])
                    l1_qz(cx, B, A, Ab, Shb, winb, 1, qgain, din['cs_s'][r0:r0 + 128, :, :], zgain)
                    _copy(mk, 'pool', [A.QT], [QTall], QTall[:, j, :, :], A.QT[:, 0, :, :])
                    _copy(mk, 'pool', [A.zb], [zball], zball[:, j, :], A.zb[:])
                    barrier(mk)
        barrier(mk)
        with ExitStack() as es2:
            kf = [mk.tile([128, 2048], F32, f'kf{i}', es=es2) for i in range(2)]
            vf = [mk.tile([128, 2048], F32, f'vf{i}', es=es2) for i in range(2)]
            kb16 = mk.tile([128, 16, 128], BF16, 'kb16', es=es2)
            vb16 = [mk.tile([128, 16, VW], BF16, f'vb16{i}', es=es2) for i in range(2)]
            KTg = mk.tile([128, 16, 128], BF16, 'KTg', es=es2)
            PT = [mk.tile([128, 16, 2, 8], BF16, f'PTs{i}', es=es2) for i in range(2)]
            PTn = mk.tile([128, 2, 8], BF16, 'PTn', es=es2)
            rr = mk.tile([128, 2], F32, 'rrs', es=es2); ot = mk.tile([128, 128], F32, 'ots', es=es2)
            ob = mk.tile([128, 128], F32, 'obs', es=es2)
            for t_ in vb16:
                mk.op('pool', [], [t_], lambda e, t_=t_: e.memset(t_[:], 1.0))
            pk = Tile('pool_k', din['pool_k']); pv = Tile('pool_v', din['pool_v'])
            n = 0
            for b in range(NB):
                j, bb = b // 16, b % 16
                qT = QTall[:, j, :, 8 * bb:8 * bb + 8]
                acc = [cx.ps[4], cx.ps[5]]
                for g in range(NGRP):
                    n += 1
                    kf_, vf_, vb_, PT_ = kf[n % 2], vf[n % 2], vb16[n % 2], PT[n % 2]
                    col = b * NGRP + g
                    mk.dma('pool', kf_, [idx], [kf_], lambda e, kf_=kf_, col=col: e.indirect_dma_start(
                        out=kf_[:], out_offset=None, in_=din['pool_k'],
                        in_offset=bass.IndirectOffsetOnAxis(ap=idx[:, col:col + 1], axis=0)))
                    mk.dma('pool', vf_, [idx], [vf_], lambda e, vf_=vf_, col=col: e.indirect_dma_start(
                        out=vf_[:], out_offset=None, in_=din['pool_v'],
                        in_offset=bass.IndirectOffsetOnAxis(ap=idx[:, col:col + 1], axis=0)))
                    _copy(mk, 'dve', [kf_], [kb16], kb16[:].rearrange("p t d -> p (t d)"), kf_[:])
                    _copy(mk, 'act', [vf_], [vb_], vb_[:, :, 0:128], vf_[:].rearrange("p (t d) -> p t d", d=128))
                    for half in range(2):
                        pt = nextpst(cx)
                        for t8 in range(8):
                            _tr(mk, [kb16, cx.identb], [pt], pt[:, t8 * 128:(t8 + 1) * 128], kb16[:, half * 8 + t8, :], cx.identb[:])
                        _copy(mk, 'dve', [pt], [KTg], KTg[:, half * 8:half * 8 + 8, :], pt[:].rearrange("p (t k) -> p t k", t=8))
                    ps = nextps4(cx)
                    for tl in range(16):
                        _mm(mk, [KTg, QTall], [ps], ps[:, tl * 16:(tl + 1) * 16], KTg[:, tl, :], qT)
                    _act(mk, [ps, cx.m4c], [PT_], PT_[:].rearrange("p t c q -> p (t c q)"), ps[:, 0:256], AF.Exp,
                         scale=0.125, bias=cx.m4c[:, 0:1])
                    for tl in range(16):
                        for c in range(2):
                            _mm(mk, [PT_, vb_], [acc[c]], acc[c][0:8, 0:129], PT_[:, tl, c, :], vb_[:, tl, 0:129],
                                start=(g == 0 and tl == 0), stop=False)
                ps = nextps4(cx)
                _mm(mk, [KTn, QTall], [ps], ps[:, 0:16], KTn[:, j, :], qT)
                _act(mk, [ps, cx.m4c], [PTn], PTn[:].rearrange("p c q -> p (c q)"), ps[:, 0:16], AF.Exp, scale=0.125, bias=cx.m4c[:, 0:1])
                _tt(mk, 'pool', [PTn, nmask], [PTn], PTn[:], PTn[:], nmask[:, bb:bb + 1, :].broadcast_to([128, 2, 8]), ALU.mult)
                for c in range(2):
                    _mm(mk, [PTn, Vn], [acc[c]], acc[c][0:8, 0:129], PTn[:, c, :], Vn[:, j, 0:129], start=False, stop=True)
                mk.op('dve', [acc[0]], [rr], lambda e, acc=acc: e.reciprocal(out=rr[0:8, 0:1], in_=acc[0][0:8, 128:129]))
                mk.op('dve', [acc[1]], [rr], lambda e, acc=acc: e.reciprocal(out=rr[0:8, 1:2], in_=acc[1][0:8, 128:129]))
                _tt(mk, 'dve', [rr, nlam], [rr], rr[0:8, 1:2], rr[0:8, 1:2], nlam[0:8, 0:1], ALU.mult)
                _ts(mk, 'dve', [acc[0], rr], [ot], ot[0:8, :], acc[0][0:8, 0:128], rr[0:8, 0:1], ALU.mult)
                _stt(mk, [acc[1], rr, ot], [ob], ob[0:8, :], acc[1][0:8, 0:128], rr[0:8, 1:2], ot[0:8, :], ALU.mult, ALU.add)
                _store(mk, ob, oat[8 * b:8 * b + 8, :], ob[0:8, :], writes=[oat_t])
            A2 = alloc_attn_small(cx, es2)
            sqt = mk.tile([128, D], F32, 'sqt', es=es2)
            B2 = Ctx(); B2.sq = sqt
            for j in range(NTILE):
                _load(mk, A2.oatt, A2.oatt[:, 0, :], oat[j * 128:(j + 1) * 128, :], reads=[oat_t])
                _copy(mk, 'pool', [zball], [A2.zb], A2.zb[:], zball[:, j, :])
                attn_post_tok(cx, B2, A2)
                _store(mk, A2.y, og_h[j * 128:(j + 1) * 128, :], A2.y[:, 0:128])
        mk.finish()
    return nc


def alloc_attn_small(cx, es):
    mk = cx.mk
    A = Ctx()
    A.QT = mk.tile([128, 1, 2, 128], BF16, 'sQT', es=es)
    A.zb = mk.tile([128, 128], BF16, 'szb', es=es)
    A.oatt = mk.tile([128, 1, 128], F32, 'soatt', es=es)
    A.ss = mk.tile([128, 8], F32, 'sss', es=es)
    A.y = mk.tile([128, D], F32, 'sy', es=es)
    return A


def attn_post_tok(cx, B, A):
    mk = cx.mk
    _tt(mk, 'pool', [A.oatt], [B.sq], B.sq[:, 0:128], A.oatt[:, 0, :], A.oatt[:, 0, :], ALU.mult)
    mk.op('dve', [B.sq], [A.ss], lambda e: e.tensor_reduce(out=A.ss[:, 0:1], in_=B.sq[:, 0:128], axis=AX.X, op=ALU.add))
    _act(mk, [A.ss, cx.epsc], [A.ss], A.ss[:, 0:1], A.ss[:, 0:1], AF.Sqrt, scale=1.0 / 128, bias=cx.epsc[:, 0:1])
    mk.op('dve', [A.ss], [A.ss], lambda e: e.reciprocal(out=A.ss[:, 0:1], in_=A.ss[:, 0:1]))
    _ts(mk, 'dve', [A.oatt, A.ss], [A.oatt], A.oatt[:, 0, :], A.oatt[:, 0, :], A.ss[:, 0:1], ALU.mult)
    _tt(mk, 'pool', [A.oatt, A.zb], [A.y], A.y[:, 0:128], A.oatt[:, 0, :], A.zb[:], ALU.mult)


def build_sample_out():
    nc = bass.Bass("TRN2", target_bir_lowering=False)
    din = {}

    def inp(name, shape, dt=F32):
        din[name] = nc.dram_tensor(name, list(shape), dt, kind="ExternalInput").ap()

    inp('x1l', [128, D]); inp('og_l', [128, D]); inp('cT_s', [128, 8, 128])
    inp('ada_w_b', [D, 3 * D]); inp('ada_b_b', [128, 3 * D]); inp('norm_b', [128, D]); inp('w_out_b', [D, D])
    y_s = nc.dram_tensor('y_s', [128, D], F32, kind="ExternalOutput").ap()
    with ExitStack() as es:
        mk, cx = _common_ctx(nc, es)
        wob = mk.tile([128, 8, D], BF16, 'wob')
        A = Ctx()
        A.y = mk.tile([128, D], F32, 'y')
        x1 = mk.tile([128, D], F32, 'x1'); _load(mk, x1, x1[:], din['x1l'])
        og = mk.tile([128, D], F32, 'og'); _load(mk, og, og[:], din['og_l'])
        ogb = mk.tile([128, 8, 128], BF16, 'ogb'); _copy(mk, 'dve', [og], [ogb], ogb[:].rearrange("p h e -> p (h e)"), og[:])
        ogT = mk.tile([128, 8, 128], BF16, 'ogT')
        _Ab, _Shb, Gb = phase_mods(cx, din, es, 'b', 3, 'cT_s')
        with ExitStack() as es1:
            wf = mk.tile([128, 8, D], F32, 'wf', es=es1)
            _load(mk, wf, wf[:], din['w_out_b'].rearrange("(kc p) n -> p kc n", p=128))
            _copy(mk, 'pool', [wf], [wob], wob[:], wf[:])
            pt = nextpst(cx)
            for h in range(8):
                _tr(mk, [ogb, cx.identb], [pt], pt[:, h * 128:(h + 1) * 128], ogb[:, h, :], cx.identb[:])
            _copy(mk, 'act', [pt], [ogT], ogT[:], pt[:].rearrange("p (h t) -> p h t", h=8))
            out_proj_res(cx, A, ogT, 8, wob, x1[:], x1, Gb, y_s[:, :])
        mk.finish()
    return nc


_PROG_CACHE = {}
DBG = {}


def kernel_impl(inputs, NBLK=32):
    g = lambda k: np.asarray(inputs[k])
    NT = NBLK * 128
    if ('p1', NBLK) not in _PROG_CACHE:
        _PROG_CACHE[('p1', NBLK)] = build_program(NBLK, True)
        _PROG_CACHE['p2'] = build_sample_attn(32, 64)
        _PROG_CACHE['p3'] = build_sample_out()
    maps = make_in_maps(inputs, NBLK)
    res1 = run_bass_kernel_spmd(_PROG_CACHE[('p1', NBLK)], maps, core_ids=list(range(8))).results
    y_p = np.zeros((4, NT, D), np.float32)
    for b in range(4):
        yv = y_p[b].reshape(NBLK // 2, 2, 128, D)
        yv[:, 0] = res1[2 * b]['y_p'].reshape(-1, 128, D)
        yv[:, 1] = res1[2 * b + 1]['y_p'].reshape(-1, 128, D)
    st_p = np.stack([res1[2 * b]['S_p'] for b in range(4)])[None]
    conv_p = np.stack([res1[2 * b]['conv_p'].reshape(3, 4096) for b in range(4)])[None]
    k_p = np.stack([res1[2 * b]['k_p'].reshape(NT, 8, 128) for b in range(4)])
    v_p = np.stack([res1[2 * b]['v_p'].reshape(NT, 8, 128) for b in range(4)])
    st_s = np.concatenate([res1[c]['S_s_out'] for c in range(8)])[None]
    conv_s = np.concatenate([res1[c]['conv_s_out'].reshape(4, 3, 4096) for c in range(8)])[None]
    x1all = np.concatenate([res1[c]['x1loc'] for c in range(8)])
    cs = np.repeat(g('c_sample'), 8, axis=0)
    cT_all = [np.ascontiguousarray(cs[j * 128:(j + 1) * 128].T.reshape(8, 128, 128).transpose(1, 0, 2)) for j in range(2)]
    pt = g('page_table').astype(np.int32)
    pt_rep = np.ascontiguousarray(np.repeat(pt.reshape(32, 4, 16), 8, axis=2).transpose(2, 0, 1).reshape(128, 128))
    i128 = np.arange(128)
    nmask = np.zeros((128, 16, 8), np.float32)
    for bb in range(16):
        for q in range(8):
            nmask[8 * bb:8 * bb + q + 1, bb, q] = 1.0
    pos = 8192 + (np.arange(256) % 8)
    shared2 = {
        'x1all': x1all, 'cT_all0': cT_all[0], 'cT_all1': cT_all[1],
        'ada_w_kv': g('ada_w_kv'), 'ada_b_kv': _rep(g('ada_b_kv')), 'norm_kv': _rep(g('norm_kv')),
        'ada_w_b': g('ada_w_b')[0], 'ada_b_b': _rep(g('ada_b_b')[0]), 'norm_b': _rep(g('norm_b')[0]),
        'kgain': _rep(g('k_gain')), 'qgain': _rep(g('q_gain')[0]), 'subgain': _rep(g('subln_gain')[0]),
        'lam': np.ascontiguousarray(np.broadcast_to(g('lam_params')[0][None], (128, 4, 64))).astype(np.float32),
        'cs_s': _rope_table(pos), 'pt_rep': pt_rep, 'sub8': (i128 % 8).astype(np.float32).reshape(128, 1), 'nmask': nmask,
    }
    maps2 = []
    wkv, winb = g('w_kv'), g('w_in_b')[0]
    ck, cv = g('cache_k'), g('cache_v')
    for h in range(8):
        m = dict(shared2)
        m['w_kv_h'] = np.ascontiguousarray(np.concatenate([wkv[:, h * 128:(h + 1) * 128], wkv[:, 1024 + h * 128:1024 + (h + 1) * 128]], axis=1))
        m['w_in_b_h'] = np.ascontiguousarray(np.concatenate([winb[:, h * 128:(h + 1) * 128], winb[:, 1024 + h * 128:1024 + (h + 1) * 128]], axis=1))
        m['pool_k'] = np.ascontiguousarray(ck[:, :, h, :]).reshape(2560 * 8, 2048)
        m['pool_v'] = np.ascontiguousarray(cv[:, :, h, :]).reshape(2560 * 8, 2048)
        maps2.append(m)
    res2 = run_bass_kernel_spmd(_PROG_CACHE['p2'], maps2, core_ids=list(range(8))).results
    k_s = np.stack([res2[h]['k_s_h'] for h in range(8)], axis=1).reshape(32, 8, 8, 128)
    v_s = np.stack([res2[h]['v_s_h'] for h in range(8)], axis=1).reshape(32, 8, 8, 128)
    og_all = np.stack([res2[h]['og_h'] for h in range(8)], axis=1).reshape(256, D)
    DBG['x1all'] = x1all; DBG['og_all'] = og_all; DBG['oatt'] = np.stack([res2[h]['scr_oatt'] for h in range(8)], axis=1)
    maps3 = []
    for c in range(8):
        x1l = np.zeros((128, D), np.float32); x1l[:32] = x1all[32 * c:32 * c + 32]
        ogl = np.zeros((128, D), np.float32); ogl[:32] = og_all[32 * c:32 * c + 32]
        maps3.append({'x1l': x1l, 'og_l': ogl, 'cT_s': maps[c]['cT_s'], 'ada_w_b': g('ada_w_b')[0],
                      'ada_b_b': _rep(g('ada_b_b')[0]), 'norm_b': _rep(g('norm_b')[0]), 'w_out_b': g('w_out_b')[0]})
    res3 = run_bass_kernel_spmd(_PROG_CACHE['p3'], maps3, core_ids=list(range(8))).results
    y_s = np.concatenate([res3[c]['y_s'][:32] for c in range(8)]).reshape(32, 8, D)
    return (y_p, y_s, st_p, conv_p, k_p, v_p, st_s, conv_s, k_s, v_s)


def kernel(**inputs):
    return kernel_impl(inputs, 32)
```

```python
import math
from contextlib import ExitStack

import numpy as np
import concourse.bass as bass
import concourse.mybir as mybir
from concourse.bass_utils import run_bass_kernel_spmd

F32 = mybir.dt.float32
BF16 = mybir.dt.bfloat16
I32 = mybir.dt.int32
AF = mybir.ActivationFunctionType
ALU = mybir.AluOpType
AX = mybir.AxisListType
EPOCH = 1 << 28
EPS = 1e-6
DEBUG = False
P3STEP = 9
VVAR = 3
NDSEM = 24
KVSTEP = 9
VW = 160
MAXPHASE = 3
INV_LEVELS = 7
NEG = -30000.0

D = 1024
HV = 16
HQ = 8
NQKV = 4096
NIN = 6176
DH = 64


class Buf:
    def __init__(self, name):
        self.name = name
        self.writer = None
        self.readers = []
        self.dsem = None
        self.dcount = 0


class Tile(Buf):
    def __init__(self, name, t):
        super().__init__(name)
        self.t = t

    def __getitem__(self, k):
        return self.t[k]


class MK:
    def __init__(self, nc, es):
        self.nc = nc
        self.es = es
        self.names = ['pe', 'dve', 'act', 'pool', 'sp']
        self.cnt = {k: 0 for k in self.names}
        self.sems = {k: [] for k in self.names}
        self.waited = {k: {} for k in self.names}
        self.q = {k: [] for k in self.names}
        self.n = 0
        self.out_owners = []
        self.all_owners = []
        self.groups = []
        self.ngrp = 0
        self.same_sync = {'pe': False, 'dve': True, 'act': True, 'pool': True, 'sp': True}

    def tile(self, shape, dt, name=None, es=None):
        self.n += 1
        name = name or f"t{self.n}"
        t = (es or self.es).enter_context(self.nc.sbuf_tensor(f"{name}_{self.n}", list(shape), dt))
        return Tile(name, t)

    def psum(self, shape, dt, name=None, es=None):
        self.n += 1
        name = name or f"p{self.n}"
        t = (es or self.es).enter_context(self.nc.psum_tensor(f"{name}_{self.n}", list(shape), dt))
        return Tile(name, t)

    def dram(self, name, shape, dt, kind="Internal"):
        t = self.nc.dram_tensor(name, list(shape), dt, kind=kind)
        return Tile(name, t.ap())

    def _sem(self, e, epoch):
        while len(self.sems[e]) <= epoch:
            self.sems[e].append(self.es.enter_context(self.nc.semaphore(f"s_{e}_{len(self.sems[e])}")))
        return self.sems[e][epoch]

    def _wait_tok(self, E, tok):
        if tok[0] == 'e':
            _, e, count = tok
            if e == E and not self.same_sync[E]:
                return
            key = ('e', e)
            if self.waited[E].get(key, 0) >= count:
                return
            self.waited[E][key] = count
            ep, v = (count - 1) // EPOCH, (count - 1) % EPOCH + 1
            self.q[E].append(('w', self._sem(e, ep), v))
        else:
            g = tok[1].grp
            key = ('d', id(g))
            if self.waited[E].get(key, 0) >= g.dcount:
                return
            self.waited[E][key] = g.dcount
            self.q[E].append(('w', g.dsem, 16 * g.dcount))

    def deps(self, E, reads, writes):
        for b in reads:
            if b.writer is not None:
                self._wait_tok(E, b.writer)
        for b in writes:
            if b.writer is not None:
                self._wait_tok(E, b.writer)
            for r in b.readers:
                self._wait_tok(E, r)

    def _record(self, tok, reads, writes):
        for b in writes:
            b.writer = tok
            b.readers = []
        for b in reads:
            if b not in writes:
                b.readers.append(tok)
                if len(b.readers) > 24:
                    b.readers = b.readers[-24:] if False else b.readers

    def op(self, E, reads, writes, fn):
        self.deps(E, reads, writes)
        self.cnt[E] += 1
        c = self.cnt[E]
        self.q[E].append(('i', fn, self._sem(E, (c - 1) // EPOCH), 1))
        self._record(('e', E, c), reads, writes)

    def dma(self, Q, owner, reads, writes, fn, is_output=False):
        self.deps(Q, reads, writes)
        if getattr(owner, 'grp', None) is None:
            if len(self.groups) < NDSEM:
                g = Buf(f"grp{len(self.groups)}")
                g.dsem = self.es.enter_context(self.nc.semaphore(f"dgrp_{len(self.groups)}"))
                self.groups.append(g)
            owner.grp = self.groups[self.ngrp % NDSEM]
            self.ngrp += 1
        g = owner.grp
        g.dcount += 1
        owner.dcount += 1
        if owner not in self.all_owners:
            self.all_owners.append(owner)
        self.q[Q].append(('i', fn, g.dsem, 16))
        self._record(('d', owner), reads, writes)
        if is_output and owner not in self.out_owners:
            self.out_owners.append(owner)

    def finish(self):
        for g in self.groups:
            if g.dcount:
                self.q['sp'].append(('w', g.dsem, 16 * g.dcount))
        nc = self.nc
        with nc.Block() as block:
            regs = {'pe': block.tensor, 'dve': block.vector, 'act': block.scalar,
                    'pool': block.gpsimd, 'sp': block.sync}
            for E in self.names:
                items = self.q[E]

                def body(e, items=items):
                    for it in items:
                        if it[0] == 'w':
                            e.wait_ge(it[1], it[2])
                        else:
                            ins = it[1](e)
                            if it[2] is not None:
                                ins.then_inc(it[2], it[3])
                regs[E](body)


def host_consts(sample):
    i = np.arange(128)
    if sample:
        same = (i[:, None] // 8) == (i[None, :] // 8)
    else:
        same = np.ones((128, 128), bool)
    c = {}
    c['m1'] = ((i[:, None] <= i[None, :]) & same).astype(np.float32)
    c['m2'] = ((i[:, None] > i[None, :]) & same).astype(np.float32)
    c['nm_strict'] = np.where((i[:, None] < i[None, :]) & same, 0.0, NEG).astype(np.float32)
    c['nm_incl'] = np.where((i[:, None] <= i[None, :]) & same, 0.0, NEG).astype(np.float32)
    c['same'] = same.astype(np.float32)
    nlev = 3 if sample else 7
    ml = np.zeros((128, nlev, 2, 128), np.float32)
    for lv in range(nlev):
        b = 1 << lv
        r, cc = i[:, None], i[None, :]
        low = ((r // (2 * b)) == (cc // (2 * b))) & ((r // b) % 2 == 1) & ((cc // b) % 2 == 0)
        ml[:, lv, 1, :] = -low.astype(np.float32)
        ml[:, lv, 0, :] = -low.T.astype(np.float32)
    c['ml'] = ml
    return c


class Ctx:
    pass


def _act(mk, reads, writes, out, in_, func, **kw):
    mk.op('act', reads, writes, lambda e: e.activation(out=out, in_=in_, func=func, **kw))


def _tt(mk, E, reads, writes, out, in0, in1, op):
    mk.op(E, reads, writes, lambda e: e.tensor_tensor(out=out, in0=in0, in1=in1, op=op))


def _ts(mk, E, reads, writes, out, in0, s1, op0, s2=None, op1=None):
    if op1 is None:
        mk.op(E, reads, writes, lambda e: e.tensor_scalar(out=out, in0=in0, scalar1=s1, scalar2=None, op0=op0))
    else:
        mk.op(E, reads, writes, lambda e: e.tensor_scalar(out=out, in0=in0, scalar1=s1, scalar2=s2, op0=op0, op1=op1))


def _stt(mk, reads, writes, out, in0, scalar, in1, op0, op1, E='dve'):
    mk.op(E, reads, writes, lambda e: e.scalar_tensor_tensor(out=out, in0=in0, scalar=scalar, in1=in1, op0=op0, op1=op1))


def _mm(mk, reads, writes, out, lhsT, rhs, start=True, stop=True):
    mk.op('pe', reads, writes, lambda e: e.matmul(out, lhsT=lhsT, rhs=rhs, start=start, stop=stop))


def _tr(mk, reads, writes, out, in_, ident):
    mk.op('pe', reads, writes, lambda e: e.transpose(out=out, in_=in_, identity=ident))


def _copy(mk, E, reads, writes, out, in_):
    if E == 'act':
        mk.op('act', reads, writes, lambda e: e.activation(out=out, in_=in_, func=AF.Copy))
    else:
        mk.op(E, reads, writes, lambda e: e.tensor_copy(out=out, in_=in_))


def _load(mk, tile, out_ap, in_ap, Q='sp', reads=()):
    mk.dma(Q, tile, list(reads), [tile], lambda e: e.dma_start(out=out_ap, in_=in_ap))


def _store(mk, tile, out_ap, in_ap, Q='sp', writes=(), is_output=True):
    mk.dma(Q, tile, [tile], list(writes), lambda e: e.dma_start(out=out_ap, in_=in_ap), is_output=is_output)


def setup_consts(cx, din):
    mk = cx.mk
    cx.identb = mk.tile([128, 128], BF16, 'identb')
    cx.identf = mk.tile([128, 128], F32, 'identf')
    cx.onesf = mk.tile([128, 128], F32, 'onesf')
    cx.onesb = mk.tile([128, 128], BF16, 'onesb')
    mk.op('pool', [], [cx.onesf], lambda e: e.memset(cx.onesf[:], 1.0))
    mk.op('pool', [], [cx.onesb], lambda e: e.memset(cx.onesb[:], 1.0))
    mk.op('pool', [], [cx.identf], lambda e: e.memset(cx.identf[:], 1.0))
    mk.op('pool', [cx.identf], [cx.identf], lambda e: e.affine_select(
        out=cx.identf[:], in_=cx.identf[:], pattern=[[-1, 128]], compare_op=ALU.is_equal, fill=0.0,
        base=0, channel_multiplier=1))
    _copy(mk, 'pool', [cx.identf], [cx.identb], cx.identb[:], cx.identf[:])


def load_masks(cx, din, pre, nlev=7, es_tmp=None):
    mk = cx.mk
    m = Ctx()
    m.nlev = nlev
    m.m1 = mk.tile([128, 128], F32, pre + 'm1')
    m.m2 = mk.tile([128, 128], F32, pre + 'm2')
    m.nm_incl4 = mk.tile([128, 4, 128], BF16, pre + 'nmi4')
    m.strict01 = mk.tile([128, 128], F32, pre + 's01')
    m.ML = mk.tile([128, nlev, 2, 128], BF16, pre + 'ML')
    tmp = mk.tile([128, 2, 128], F32, pre + 'nmtmp', es=es_tmp)
    mlf = mk.tile([128, nlev, 2, 128], F32, pre + 'mlf', es=es_tmp)
    _load(mk, m.m1, m.m1[:], din[pre + 'm1'])
    _load(mk, m.m2, m.m2[:], din[pre + 'm2'])
    _load(mk, tmp, tmp[:, 0, :], din[pre + 'nm_strict'])
    _load(mk, tmp, tmp[:, 1, :], din[pre + 'nm_incl'])
    _copy(mk, 'pool', [tmp], [m.nm_incl4], m.nm_incl4[:], tmp[:, 1:2, :].broadcast_to([128, 4, 128]))
    _ts(mk, 'pool', [tmp], [m.strict01], m.strict01[:], tmp[:, 0, :], 0.0, ALU.is_equal)
    _load(mk, mlf, mlf[:], din[pre + 'ml'])
    _copy(mk, 'pool', [mlf], [m.ML], m.ML[:], mlf[:])
    return m


def barrier(mk):
    owners = [o for o in mk.all_owners if o.dcount > 0] if hasattr(mk, 'all_owners') else []
    for E in mk.names:
        for e in mk.names:
            if e != E and e != 'sp' and mk.cnt[e] > 0:
                mk._wait_tok(E, ('e', e, mk.cnt[e]))
        for o in owners:
            mk._wait_tok(E, ('d', o))


def alloc_stage(cx, es):
    mk = cx.mk
    st = Ctx()
    st.f = [mk.tile([128, 4096], F32, f"stg{i}", es=es) for i in range(2)]
    st.b = [mk.tile([128, 4096], BF16, f"stb{i}", es=es) for i in range(2)]
    st.bias = [mk.tile([128, 512], F32, f"adab{i}", es=es) for i in range(2)]
    st.n = 0
    return st


def ada_mod(cx, cT, w_ap, b_ap, ncols, outs, st):
    mk = cx.mk
    wv = w_ap.rearrange("(kc p) n -> p kc n", p=128)
    for ci in range(ncols // 512):
        st.n += 1
        wt, b = st.f[st.n % 2], st.bias[st.n % 2]
        w = wt[:].rearrange("p (k c) -> p k c", k=8)
        ps = cx.ps[ci % 2]
        _load(mk, wt, w, wv[:, :, ci * 512:(ci + 1) * 512])
        _load(mk, b, b[:], b_ap[:, ci * 512:(ci + 1) * 512])
        for kc in range(8):
            _mm(mk, [cT, wt], [ps], ps[:], cT[:, kc, :], w[:, kc, :], start=(kc == 0), stop=(kc == 7))
        o = outs[ci // 2]
        _tt(mk, 'dve', [ps, b], [o], o[:, (ci % 2) * 512:(ci % 2 + 1) * 512], ps[:], b[:], ALU.add)


def prep_w_bf(cx, w_ap, nk, ncols, dst, st, to_dram=True):
    mk = cx.mk
    wv = w_ap.rearrange("(kc p) n -> p kc n", p=128)
    CH = 4096 // nk
    engs = ['pool', 'act']
    for ci, c0 in enumerate(range(0, ncols, CH)):
        w = min(CH, ncols - c0)
        st.n += 1
        sf, sb = st.f[st.n % 2], st.b[st.n % 2]
        s = sf[:].rearrange("p (k c) -> p k c", k=nk)
        b = sb[:].rearrange("p (k c) -> p k c", k=nk)
        _load(mk, sf, s[:, :, :w], wv[:, :, c0:c0 + w])
        _copy(mk, engs[ci % 2], [sf], [sb], b[:, :, :w], s[:, :, :w])
        _store(mk, sb, dst[:, :, c0:c0 + w], b[:, :, :w], writes=[dst], is_output=False)


def rms_mod_T(cx, x, nblk, A, Sh, hT, es_tiles):
    mk = cx.mk
    junk, ss, rstd, hb = es_tiles['junk'], es_tiles['ss'], es_tiles['rstd'], es_tiles['hb']
    for blk in range(nblk):
        _act(mk, [x], [junk, ss], junk[:], x[:, blk, :], AF.Square, accum_out=ss[:, blk:blk + 1])
    _act(mk, [ss, cx.epsc], [rstd], rstd[:, :nblk], ss[:, :nblk], AF.Sqrt, scale=1.0 / D, bias=cx.epsc[:, 0:1])
    mk.op('dve', [rstd], [rstd], lambda e: e.reciprocal(out=rstd[:, :nblk], in_=rstd[:, :nblk]))
    for blk in range(nblk):
        t = es_tiles['t32']
        _stt(mk, [x, rstd, A], [t], t[:], x[:, blk, :], rstd[:, blk:blk + 1], A[:], ALU.mult, ALU.mult)
        _tt(mk, 'pool', [t, Sh], [hb], hb[:], t[:], Sh[:], ALU.add)
        pt = cx.pst[blk % 2]
        for kc in range(8):
            _tr(mk, [hb, cx.identb], [pt], pt[:, kc * 128:(kc + 1) * 128], hb[:, kc * 128:(kc + 1) * 128], cx.identb[:])
        _copy(mk, 'act' if blk % 2 == 0 else 'dve', [pt], [hT],
              hT[:, :, blk * 128:(blk + 1) * 128], pt[:].rearrange("p (k t) -> p k t", k=8))


def nextps(cx):
    cx.psi = (cx.psi + 1) % len(cx.ps)
    return cx.ps[cx.psi]


def nextps4(cx):
    cx.psi4 = (getattr(cx, 'psi4', 0) + 1) % 4
    return cx.ps[cx.psi4]


def nextpst(cx):
    cx.psti = (cx.psti + 1) % len(cx.pst)
    return cx.pst[cx.psti]


def alloc_l0(cx, es, T):
    mk = cx.mk
    L = Ctx()
    nb = T // 128
    L.T, L.nb = T, nb
    L.x = [mk.tile([128, nb, D], F32, 'x', es=es)] * 2
    L.hT = mk.tile([128, 8, T], BF16, 'hT', es=es)
    L.wbuf = [mk.tile([128, 8, 256], BF16, f'wbuf{i}', es=es) for i in range(2)]
    L.raw = [mk.tile([128, 3 + T], BF16, f'raw{i}', es=es) for i in range(2)]
    L.hist = mk.tile([128, 32, 3], BF16, 'hist', es=es)
    L.hist32 = mk.tile([128, 32, 3], F32, 'hist32', es=es)
    L.diag = [mk.tile([128, 4, 128], BF16, f'diag{i}', es=es) for i in range(2)]
    L.qkT = mk.tile([128, 16, T], BF16, 'qkT', es=es)
    L.vT = mk.tile([128, 16, T], BF16, 'vT', es=es)
    L.zg = mk.tile([128, nb, 2048], BF16, 'zg', es=es)
    L.ba = mk.tile([128, nb, 32], F32, 'ba', es=es)
    L.sq = [mk.tile([128, T], BF16, f'sq{i}', es=es) for i in range(2)]
    L.r32 = [mk.tile([128, T], F32, f'r32{i}', es=es) for i in range(2)]
    L.sm = {k: mk.tile([128, 16], F32, 'sm_' + k, es=es) for k in
            ['beta', 'xa', 'ax', 'e1', 'l1', 'sp', 'g', 'gc', 'ngc', 'eg', 's1', 's2', 'egl', 'ss', 'rstd']}
    L.rhsB = mk.tile([128, 16, 128], F32, 'rhsB', es=es)
    L.rhsBeta = mk.tile([128, 16, 128], BF16, 'rhsBeta', es=es)
    L.egB = mk.tile([128, 16, 128], BF16, 'egB', es=es)
    L.QgT = L.egB
    L.DmQ = mk.tile([128, 16, 128], BF16, 'DmQ', es=es)
    L.Ms = mk.tile([128, 16, 128], BF16, 'Ms', es=es)
    L.X = L.Ms
    L.QKT = mk.tile([128, 16, 128], BF16, 'QKT', es=es)
    L.inv = [mk.tile([128, 16, 2, 128], BF16, 'AA', es=es)]
    L.DD = [mk.tile([128, 2, 2, 128], BF16, f'DD{i}', es=es) for i in range(8)]
    L.ZZ = [mk.tile([128, 2, 2, 128], BF16, f'ZZ{i}', es=es) for i in range(8)]
    L.AO = mk.tile([128, 16, 2, 128], BF16, 'AO', es=es)
    L.Kbg = mk.tile([128, 16, 128], BF16, 'Kbg', es=es)
    L.Kd = mk.tile([128, 16, 128], BF16, 'Kd', es=es)
    L.Vb = mk.tile([128, 16, 128], BF16, 'Vb', es=es)
    L.nWT = L.DmQ
    L.vn = L.Ms
    L.o32 = L.rhsB
    L.og = L.rhsBeta
    L.ogT = L.Kbg
    L.x1 = [None, None]
    L.t32 = mk.tile([128, D], F32, 't32', es=es)
    L.hb = mk.tile([128, D], BF16, 'hb', es=es)
    L.junk = L.hb
    L.nss = mk.tile([128, 4], F32, 'nss', es=es)
    L.nrs = mk.tile([128, 4], F32, 'nrs', es=es)
    return L


def gdn_gates(cx, L, blk, M, vm=None):
    mk = cx.mk
    s = L.sm
    bin_ = L.ba[:, blk, 0:16]
    ain = L.ba[:, blk, 16:32]
    _act(mk, [L.ba], [s['beta']], s['beta'][:], bin_, AF.Sigmoid)
    _tt(mk, 'dve', [L.ba, cx.dtb], [s['xa']], s['xa'][:], ain, cx.dtb[:], ALU.add)
    _act(mk, [s['xa']], [s['ax']], s['ax'][:], s['xa'][:], AF.Abs)
    _act(mk, [s['ax']], [s['e1']], s['e1'][:], s['ax'][:], AF.Exp, scale=-1.0)
    _act(mk, [s['e1'], cx.onec], [s['l1']], s['l1'][:], s['e1'][:], AF.Ln, bias=cx.onec[:, 0:1])
    _stt(mk, [s['xa'], s['l1']], [s['sp']], s['sp'][:], s['xa'][:], 0.0, s['l1'][:], ALU.max, ALU.add)
    _tt(mk, 'dve', [s['sp'], cx.nea], [s['g']], s['g'][:], s['sp'][:], cx.nea[:], ALU.mult)
    if vm is not None:
        _ts(mk, 'dve', [s['beta'], vm[0]], [s['beta']], s['beta'][:], s['beta'][:], vm[1], ALU.mult)
        _ts(mk, 'dve', [s['g'], vm[0]], [s['g']], s['g'][:], s['g'][:], vm[1], ALU.mult)
    ps = nextps(cx)
    _mm(mk, [M.m1, s['g']], [ps], ps[:, 0:16], M.m1[:], s['g'][:])
    _mm(mk, [M.m2, s['g']], [ps], ps[:, 16:32], M.m2[:], s['g'][:])
    _mm(mk, [cx.onesf, s['g']], [ps], ps[:, 32:48], cx.onesf[:], s['g'][:])
    _copy(mk, 'dve', [ps], [s['gc']], s['gc'][:], ps[:, 0:16])
    _act(mk, [ps], [s['ngc']], s['ngc'][:], ps[:, 0:16], AF.Copy, scale=-1.0)
    _act(mk, [ps], [s['eg']], s['eg'][:], ps[:, 0:16], AF.Exp)
    _act(mk, [ps], [s['s2']], s['s2'][:], ps[:, 16:32], AF.Exp)
    _act(mk, [ps], [s['egl']], s['egl'][:], ps[:, 32:48], AF.Exp)
    _tt(mk, 'dve', [s['beta'], s['eg']], [s['s1']], s['s1'][:], s['beta'][:], s['eg'][:], ALU.mult)
    _tt(mk, 'dve', [M.m1, s['g']], [L.rhsB], L.rhsB[:],
        M.m1[:].unsqueeze(1).broadcast_to([128, 16, 128]),
        s['g'][:].unsqueeze(2).broadcast_to([128, 16, 128]), ALU.mult)
    _tt(mk, 'pool', [cx.identf, s['beta']], [L.rhsBeta], L.rhsBeta[:],
        cx.identf[:].unsqueeze(1).broadcast_to([128, 16, 128]),
        s['beta'][:].unsqueeze(2).broadcast_to([128, 16, 128]), ALU.mult)


def gdn_block(cx, L, blk, M, S32, Sb, vm=None):
    mk = cx.mk
    s = L.sm
    b0 = blk * 128
    gdn_gates(cx, L, blk, M, vm)
    for hg in range(4):
        hs = slice(hg * 4, hg * 4 + 4)
        pe_ = nextps(cx)
        _mm(mk, [cx.onesf, L.rhsB], [pe_], pe_[:], cx.onesf[:], L.rhsB[:, hs, :].rearrange("p h i -> p (h i)"))
        _act(mk, [pe_], [L.egB], L.egB[:, hs, :].rearrange("p h i -> p (h i)"), pe_[:], AF.Exp)
        pd = nextps(cx)
        _mm(mk, [cx.onesf, L.rhsB], [pd], pd[:], cx.onesf[:], L.rhsB[:, hs, :].rearrange("p h i -> p (h i)"), start=True, stop=False)
        _mm(mk, [cx.identb, M.nm_incl4], [pd], pd[:], cx.identb[:], M.nm_incl4[:].rearrange("p h i -> p (h i)"), start=False, stop=True)
        for hh in range(4):
            h = hg * 4 + hh
            _act(mk, [pd, s['ngc']], [L.DmQ], L.DmQ[:, h, :], pd[:, hh * 128:(hh + 1) * 128], AF.Exp, bias=s['ngc'][:, h:h + 1])
        pb = nextps(cx)
        _mm(mk, [cx.onesb, L.rhsBeta], [pb], pb[:], cx.onesb[:], L.rhsBeta[:, hs, :].rearrange("p h i -> p (h i)"))
        _tt(mk, 'dve', [pb, M.strict01], [L.Ms], L.Ms[:, hs, :], pb[:].rearrange("p (h i) -> p h i", h=4),
            M.strict01[:].unsqueeze(1).broadcast_to([128, 4, 128]), ALU.mult)
    _tt(mk, 'pool', [L.DmQ, L.Ms], [L.X], L.X[:], L.DmQ[:], L.Ms[:], ALU.mult)
    inv0 = L.inv[0]
    for half in range(2):
        pk = nextps(cx)
        pq = nextps(cx)
        for j in range(4):
            hq = half * 4 + j
            kT = L.qkT[:, 8 + hq, b0:b0 + 128]
            qT = L.qkT[:, hq, b0:b0 + 128]
            _mm(mk, [L.qkT], [pk], pk[:, j * 128:(j + 1) * 128], kT, kT)
            _mm(mk, [L.qkT], [pq], pq[:, j * 128:(j + 1) * 128], kT, qT)
        hs = slice(half * 8, half * 8 + 8)
        _tt(mk, 'dve', [pk, L.X], [inv0], inv0[:, hs, 0, :].rearrange("p (a b) i -> p a b i", b=2),
            pk[:].rearrange("p (a i) -> p a i", a=4).unsqueeze(2).broadcast_to([128, 4, 2, 128]),
            L.X[:, hs, :].rearrange("p (a b) i -> p a b i", b=2), ALU.mult)
        _tt(mk, 'dve', [pq, L.DmQ], [L.QKT], L.QKT[:, hs, :].rearrange("p (a b) i -> p a b i", b=2),
            pq[:].rearrange("p (a i) -> p a i", a=4).unsqueeze(2).broadcast_to([128, 4, 2, 128]),
            L.DmQ[:, hs, :].rearrange("p (a b) i -> p a b i", b=2), ALU.mult)
    for half in range(2):
        pt = nextpst(cx)
        for j in range(8):
            h = half * 8 + j
            _tr(mk, [inv0, cx.identb], [pt], pt[:, j * 128:(j + 1) * 128], inv0[:, h, 0, :], cx.identb[:])
        _copy(mk, 'act', [pt], [inv0], inv0[:, half * 8:half * 8 + 8, 1, :], pt[:].rearrange("p (h i) -> p h i", h=8))
    _tt(mk, 'pool', [L.qkT, L.egB], [L.QgT], L.QgT[:].rearrange("p (a b) i -> p a b i", b=2),
        L.qkT[:, 0:8, b0:b0 + 128].unsqueeze(2).broadcast_to([128, 8, 2, 128]),
        L.egB[:].rearrange("p (a b) i -> p a b i", b=2), ALU.mult)
    AA = inv0[:]
    ML = M.ML
    for sl in range(2):
        _tt(mk, 'pool', [inv0, ML], [L.AO], L.AO[:, :, sl, :], inv0[:, :, sl, :], ML[:, 0:1, sl, :].broadcast_to([128, 16, 128]), ALU.mult)
    for hp in range(8):
        _tt(mk, 'pool', [L.AO, cx.identb], [L.DD[hp]], L.DD[hp][:], L.AO[:, hp * 2:hp * 2 + 2, :, :],
            cx.identb[:].unsqueeze(1).unsqueeze(1).broadcast_to([128, 2, 2, 128]), ALU.add)
    nlev = min(M.nlev, INV_LEVELS)
    for lv in range(1, nlev):
        last = (lv == nlev - 1)
        for sl in range(2):
            _tt(mk, 'pool', [inv0, ML], [L.AO], L.AO[:, :, sl, :], inv0[:, :, sl, :], ML[:, lv:lv + 1, sl, :].broadcast_to([128, 16, 128]), ALU.mult)
        for hp in range(8):
            ps = nextps(cx)
            DD, ZZ = L.DD[hp], L.ZZ[hp]
            for hh in range(2):
                h = hp * 2 + hh
                c0 = hh * 256
                if not last:
                    _mm(mk, [L.AO, DD], [ps], ps[:, c0:c0 + 128], L.AO[:, h, 0, :], DD[:, hh, 1, :])
                _mm(mk, [L.AO, DD], [ps], ps[:, c0 + 128:c0 + 256], L.AO[:, h, 1, :], DD[:, hh, 0, :])
            if not last:
                _copy(mk, 'act', [ps], [ZZ], ZZ[:].rearrange("p h s i -> p (h s i)"), ps[:])
            else:
                _copy(mk, 'act', [ps], [ZZ], ZZ[:, :, 1, :], ps[:].rearrange("p (h s i) -> p h s i", h=2, s=2)[:, :, 1, :])
        for hp in range(8):
            ps = nextps(cx)
            DD, ZZ = L.DD[hp], L.ZZ[hp]
            for hh in range(2):
                c0 = hh * 256
                _mm(mk, [ZZ, DD], [ps], ps[:, c0:c0 + 128], DD[:, hh, 1, :], ZZ[:, hh, 1, :])
                if not last:
                    _mm(mk, [ZZ, DD], [ps], ps[:, c0 + 128:c0 + 256], DD[:, hh, 0, :], ZZ[:, hh, 0, :])
            if not last:
                _tt(mk, 'dve', [ps, DD], [DD], DD[:].rearrange("p h s i -> p (h s i)"), ps[:],
                    DD[:].rearrange("p h s i -> p (h s i)"), ALU.add)
            else:
                _tt(mk, 'dve', [ps, DD], [DD], DD[:, :, 0, :], ps[:].rearrange("p (h s i) -> p h s i", h=2, s=2)[:, :, 0, :],
                    DD[:, :, 0, :], ALU.add)

    class _TT:
        def __getitem__(self_, key):
            return None
    TTt = lambda h: L.DD[h // 2]
    TTa = lambda h: L.DD[h // 2][:, h % 2, 0, :]
    pt = nextpst(cx)
    for hq in range(8):
        _tr(mk, [L.qkT, cx.identb], [pt], pt[:, hq * 128:(hq + 1) * 128], L.qkT[:, 8 + hq, b0:b0 + 128], cx.identb[:])
    kview = pt[:].rearrange("p (a i) -> p a i", a=8).unsqueeze(2).broadcast_to([128, 8, 2, 128])
    _tt(mk, 'dve', [pt, s['s1']], [L.Kbg], L.Kbg[:].rearrange("p (a b) i -> p a b i", b=2), kview,
        s['s1'][:].rearrange("p (a b) -> p a b", b=2).unsqueeze(3).broadcast_to([128, 8, 2, 128]), ALU.mult)
    _tt(mk, 'dve', [pt, s['s2']], [L.Kd], L.Kd[:].rearrange("p (a b) i -> p a b i", b=2), kview,
        s['s2'][:].rearrange("p (a b) -> p a b", b=2).unsqueeze(3).broadcast_to([128, 8, 2, 128]), ALU.mult)
    for half in range(2):
        pt = nextpst(cx)
        for j in range(8):
            h = half * 8 + j
            _tr(mk, [L.vT, cx.identb], [pt], pt[:, j * 128:(j + 1) * 128], L.vT[:, h, b0:b0 + 128], cx.identb[:])
        _tt(mk, 'dve', [pt, s['beta']], [L.Vb], L.Vb[:, half * 8:half * 8 + 8, :], pt[:].rearrange("p (a i) -> p a i", a=8),
            s['beta'][:, half * 8:half * 8 + 8].unsqueeze(2).broadcast_to([128, 8, 128]), ALU.mult)
    for hg in range(4):
        ps = nextps(cx)
        for hh in range(4):
            h = hg * 4 + hh
            _mm(mk, [L.Kbg, TTt(h)], [ps], ps[:, hh * 128:(hh + 1) * 128], L.Kbg[:, h, :], TTa(h))
        _act(mk, [ps], [L.nWT], L.nWT[:, hg * 4:hg * 4 + 4, :].rearrange("p h i -> p (h i)"), ps[:], AF.Copy, scale=-1.0)
    for hg in range(4):
        ps = nextps(cx)
        for hh in range(4):
            h = hg * 4 + hh
            _mm(mk, [TTt(h), L.Vb], [ps], ps[:, hh * 128:(hh + 1) * 128], TTa(h), L.Vb[:, h, :], start=True, stop=False)
            _mm(mk, [L.nWT, Sb], [ps], ps[:, hh * 128:(hh + 1) * 128], L.nWT[:, h, :], Sb[:, h, :], start=False, stop=True)
        _copy(mk, 'dve' if hg % 2 else 'act', [ps], [L.vn], L.vn[:, hg * 4:hg * 4 + 4, :].rearrange("p h i -> p (h i)"), ps[:])
    for hg in range(4):
        ps = nextps(cx)
        for hh in range(4):
            h = hg * 4 + hh
            _mm(mk, [L.QgT, Sb], [ps], ps[:, hh * 128:(hh + 1) * 128], L.QgT[:, h, :], Sb[:, h, :], start=True, stop=False)
            _mm(mk, [L.QKT, L.vn], [ps], ps[:, hh * 128:(hh + 1) * 128], L.QKT[:, h, :], L.vn[:, h, :], start=False, stop=True)
        _copy(mk, 'act', [ps], [L.o32], L.o32[:, hg * 4:hg * 4 + 4, :].rearrange("p h i -> p (h i)"), ps[:])
    for hg in range(4):
        ps = nextps(cx)
        for hh in range(4):
            h = hg * 4 + hh
            _mm(mk, [L.Kd, L.vn], [ps], ps[:, hh * 128:(hh + 1) * 128], L.Kd[:, h, :], L.vn[:, h, :])
        for hh in range(4):
            h = hg * 4 + hh
            _stt(mk, [S32, s['egl'], ps], [S32], S32[:, h, :], S32[:, h, :], s['egl'][:, h:h + 1],
                 ps[:, hh * 128:(hh + 1) * 128], ALU.mult, ALU.add)
    _copy(mk, 'pool', [S32], [Sb], Sb[:], S32[:])


def gdn_out(cx, L, blk, x_tile, x1, gate, woa):
    mk = cx.mk
    s = L.sm
    osq = L.AO[:, :, 0, :]
    _tt(mk, 'pool', [L.o32], [L.AO], osq, L.o32[:], L.o32[:], ALU.mult)
    mk.op('dve', [L.AO], [s['ss']], lambda e: e.tensor_reduce(out=s['ss'][:], in_=osq, axis=AX.X, op=ALU.add))
    _act(mk, [s['ss'], cx.epsc], [s['rstd']], s['rstd'][:], s['ss'][:], AF.Sqrt, scale=1.0 / 128, bias=cx.epsc[:, 0:1])
    mk.op('dve', [s['rstd']], [s['rstd']], lambda e: e.reciprocal(out=s['rstd'][:], in_=s['rstd'][:]))
    _tt(mk, 'dve', [L.o32, s['rstd']], [L.o32], L.o32[:], L.o32[:], s['rstd'][:].unsqueeze(2).broadcast_to([128, 16, 128]), ALU.mult)
    _tt(mk, 'pool', [L.o32, L.zg], [L.og], L.og[:].rearrange("p h i -> p (h i)"), L.o32[:].rearrange("p h i -> p (h i)"), L.zg[:, blk, :], ALU.mult)
    for half in range(2):
        pt = nextpst(cx)
        for j in range(8):
            h = half * 8 + j
            _tr(mk, [L.og, cx.identb], [pt], pt[:, j * 128:(j + 1) * 128], L.og[:, h, :], cx.identb[:])
        _copy(mk, 'act' if half else 'dve', [pt], [L.ogT], L.ogT[:, half * 8:half * 8 + 8, :], pt[:].rearrange("p (h i) -> p h i", h=8))
    for half in range(2):
        ps = nextps(cx)
        for h in range(16):
            _mm(mk, [L.ogT, woa], [ps], ps[:], L.ogT[:, h, :], woa[:, h, half * 512:(half + 1) * 512], start=(h == 0), stop=(h == 15))
        cs = slice(half * 512, (half + 1) * 512)
        _tt(mk, 'dve', [ps, gate], [L.t32], L.t32[:, cs], ps[:], gate[:, cs], ALU.mult)
        _tt(mk, 'pool', [L.t32, x_tile], [L.t32], L.t32[:, cs], L.t32[:, cs], x_tile[:, blk, cs], ALU.add)


def conv_state_out(cx, L, es, out_ap, name):
    mk = cx.mk
    h2 = mk.tile([128, 3, 32], F32, 'h2' + name, es=es)
    _copy(mk, 'dve', [L.hist32], [h2], h2[:], L.hist32[:].rearrange("p f j -> p j f"))
    pc = nextps(cx)
    mk.op('pe', [h2, cx.identf], [pc], lambda e: e.transpose(out=pc[0:96, 0:128], in_=h2[:].rearrange("p j f -> p (j f)"), identity=cx.identf[:]))
    h3 = mk.tile([128, 128], F32, 'h3' + name, es=es)
    _copy(mk, 'dve', [pc], [h3], h3[0:96, :], pc[0:96, 0:128])
    _store(mk, h3, out_ap, h3[0:96, :])


def l0_project(cx, L, gi, last, T=None, cpos=None, hpos=0):
    mk = cx.mk
    T = T or L.T
    nb = T // 128
    cpos = T if cpos is None else cpos
    for ct in range(25):
        wb = L.wbuf[ct % 2]
        w = 256 if ct < 24 else 32
        _load(mk, wb, wb[:, :, :w], cx.wina_bf[:, :, ct * 256:ct * 256 + w], reads=[cx.wina_bf])
        if ct < 16:
            for j4 in range(2):
                f = ct * 2 + j4
                pa = nextps(cx)
                for kc in range(8):
                    _mm(mk, [wb, L.hT], [pa], pa[:, :T], wb[:, kc, j4 * 128:(j4 + 1) * 128], L.hT[:, kc, :T],
                        start=(kc == 0), stop=(kc == 7))
                raw = L.raw[f % 2]
                _copy(mk, 'act', [pa], [raw], raw[:, 3:3 + T], pa[:, :T])
                _copy(mk, 'pool', [L.hist], [raw], raw[:, hpos:hpos + 3], L.hist[:, f, :])
                if last:
                    _copy(mk, 'act', [pa], [L.hist32], L.hist32[:, f, :], pa[:, cpos - 3:cpos])
                dg = L.diag[f % 2]
                for j in range(4):
                    _ts(mk, 'pool', [cx.identb, cx.cw], [dg], dg[:, j, :], cx.identb[:], cx.cw[:, f, j:j + 1], ALU.mult)
                pb = nextps(cx)
                for j in range(4):
                    _mm(mk, [dg, raw], [pb], pb[:, :T], dg[:, j, :], raw[:, j:j + T], start=(j == 0), stop=(j == 3))
                dt_, dst = (L.qkT, L.qkT[:, f, :T]) if f < 16 else (L.vT, L.vT[:, f - 16, :T])
                _act(mk, [pb], [dt_], dst, pb[:, :T], AF.Silu)
                _copy(mk, 'pool', [raw], [L.hist], L.hist[:, f, :], raw[:, cpos:cpos + 3])
        elif ct < 24:
            for blk in range(nb):
                pz = nextps(cx)
                for kc in range(8):
                    _mm(mk, [wb, L.hT], [pz], pz[:, :256], L.hT[:, kc, blk * 128:(blk + 1) * 128], wb[:, kc, :],
                        start=(kc == 0), stop=(kc == 7))
                _act(mk, [pz], [L.zg], L.zg[:, blk, (ct - 16) * 256:(ct - 15) * 256], pz[:, :256], AF.Silu)
        else:
            for blk in range(nb):
                pz = nextps(cx)
                for kc in range(8):
                    _mm(mk, [wb, L.hT], [pz], pz[:, 0:32], L.hT[:, kc, blk * 128:(blk + 1) * 128], wb[:, kc, 0:32],
                        start=(kc == 0), stop=(kc == 7))
                _copy(mk, 'dve', [pz], [L.ba], L.ba[:, blk, :], pz[:, 0:32])
    for blk in range(nb):
        _tt(mk, 'pool', [L.zg, cx.ogain], [L.zg], L.zg[:, blk, :].rearrange("p (h i) -> p h i", h=16),
            L.zg[:, blk, :].rearrange("p (h i) -> p h i", h=16),
            cx.ogain[:].unsqueeze(1).broadcast_to([128, 16, 128]), ALU.mult)
    for f in range(16):
        sq, r32 = L.sq[f % 2], L.r32[f % 2]
        _tt(mk, 'dve', [L.qkT], [sq], sq[:, :T], L.qkT[:, f, :T], L.qkT[:, f, :T], ALU.mult)
        ps = nextps(cx)
        _mm(mk, [cx.onesb, sq], [ps], ps[:, :T], cx.onesb[:], sq[:, :T])
        if f < 8:
            _act(mk, [ps, cx.eps128], [r32], r32[:, :T], ps[:, :T], AF.Sqrt, scale=128.0, bias=cx.eps128[:, 0:1])
        else:
            _act(mk, [ps, cx.epsc], [r32], r32[:, :T], ps[:, :T], AF.Sqrt, bias=cx.epsc[:, 0:1])
        mk.op('dve', [r32], [r32], lambda e, r32=r32: e.reciprocal(out=r32[:, :T], in_=r32[:, :T]))
        _tt(mk, 'pool', [L.qkT, r32], [L.qkT], L.qkT[:, f, :T], L.qkT[:, f, :T], r32[:, :T], ALU.mult)


def alloc_l1(cx, es):
    mk = cx.mk
    B = Ctx()
    B.x = mk.tile([128, 1, D], F32, 'bx', es=es)
    B.hT = mk.tile([128, 8, 128], BF16, 'bhT', es=es)
    B.t32 = mk.tile([128, D], F32, 'bt32', es=es)
    B.hb = mk.tile([128, D], BF16, 'bhb', es=es)
    B.nss = mk.tile([128, 4], F32, 'bnss', es=es)
    B.nrs = mk.tile([128, 4], F32, 'bnrs', es=es)
    B.sq = mk.tile([128, D], F32, 'bsq', es=es)
    B.ss = mk.tile([128, 16], F32, 'bss', es=es)
    B.rs = mk.tile([128, 16], F32, 'brs', es=es)
    B.kn = mk.tile([128, 16, 64], F32, 'bkn', es=es)
    B.ko = mk.tile([128, 16, 64], F32, 'bko', es=es)
    B.rt = [mk.tile([128, 16, 8], F32, f'brt{i}', es=es) for i in range(4)]
    B.cs = mk.tile([128, 2, 8], F32, 'bcs', es=es)
    B.kb = mk.tile([128, 8, 128], BF16, 'bkb', es=es)
    B.kT = mk.tile([128, 8, 128], BF16, 'bkT', es=es)
    B.v32 = mk.tile([128, 8, 128], F32, 'bv32', es=es)
    B.vb = mk.tile([128, 8, VW], BF16, 'bvb', es=es)
    B.sc = {'junk': B.hb, 'ss': B.nss, 'rstd': B.nrs, 'hb': B.hb, 't32': B.t32}
    return B


def headnorm_rope(cx, B, src_list, nh, gain, cs_ap):
    mk = cx.mk
    ng = 2 * nh
    _load(mk, B.cs, B.cs[:], cs_ap)
    for i, ps in enumerate(src_list):
        w = min(512, nh * 128 - i * 512)
        _act(mk, [ps], [B.sq], B.sq[:, i * 512:i * 512 + w], ps[:, :w], AF.Square)
    mk.op('dve', [B.sq], [B.ss], lambda e: e.tensor_reduce(
        out=B.ss[:, :ng], in_=B.sq[:, :ng * 64].rearrange("p (g d) -> p g d", d=64), axis=AX.X, op=ALU.add))
    _act(mk, [B.ss, cx.epsc], [B.rs], B.rs[:, :ng], B.ss[:, :ng], AF.Sqrt, scale=1.0 / 64, bias=cx.epsc[:, 0:1])
    mk.op('dve', [B.rs], [B.rs], lambda e: e.reciprocal(out=B.rs[:, :ng], in_=B.rs[:, :ng]))
    for i, ps in enumerate(src_list):
        w = min(512, nh * 128 - i * 512)
        g0, gn = i * 8, w // 64
        _tt(mk, 'dve', [ps, B.rs], [B.kn], B.kn[:, g0:g0 + gn, :], ps[:, :w].rearrange("p (g d) -> p g d", d=64),
            B.rs[:, g0:g0 + gn].unsqueeze(2).broadcast_to([128, gn, 64]), ALU.mult)
    _tt(mk, 'pool', [B.kn, gain], [B.ko], B.ko[:, :ng, :], B.kn[:, :ng, :], gain[:].unsqueeze(1).broadcast_to([128, ng, 64]), ALU.mult)
    cosb = B.cs[:, 0:1, :].broadcast_to([128, ng, 8])
    sinb = B.cs[:, 1:2, :].broadcast_to([128, ng, 8])
    x1, x2 = B.ko[:, :ng, 0:8], B.ko[:, :ng, 8:16]
    t = B.rt
    _tt(mk, 'pool', [B.ko, B.cs], [t[0]], t[0][:, :ng, :], x1, cosb, ALU.mult)
    _tt(mk, 'pool', [B.ko, B.cs], [t[1]], t[1][:, :ng, :], x2, sinb, ALU.mult)
    _tt(mk, 'pool', [B.ko, B.cs], [t[2]], t[2][:, :ng, :], x2, cosb, ALU.mult)
    _tt(mk, 'pool', [B.ko, B.cs], [t[3]], t[3][:, :ng, :], x1, sinb, ALU.mult)
    _tt(mk, 'pool', [t[0], t[1]], [B.ko], x1, t[0][:, :ng, :], t[1][:, :ng, :], ALU.subtract)
    _tt(mk, 'pool', [t[2], t[3]], [B.ko], x2, t[2][:, :ng, :], t[3][:, :ng, :], ALU.add)


def kv_block(cx, B, x_ap, Akv, Shkv, wkv, nh, kgain, cs_ap, k_out_ap, v_out_ap, KTs_ap, Vs_ap, KTs, Vs, x_reads=()):
    mk = cx.mk
    _load(mk, B.x, B.x[:, 0, :], x_ap, reads=x_reads)
    rms_mod_T(cx, B.x, 1, Akv, Shkv, B.hT, B.sc)
    nk = nh * 128
    kps = []
    for i in range((nk + 511) // 512):
        ps = nextps(cx)
        w = min(512, nk - i * 512)
        for kc in range(8):
            _mm(mk, [B.hT, wkv], [ps], ps[:, :w], B.hT[:, kc, :], wkv[:, kc, i * 512:i * 512 + w], start=(kc == 0), stop=(kc == 7))
        kps.append(ps)
    if KVSTEP < 2:
        return
    headnorm_rope(cx, B, kps, nh, kgain, cs_ap)
    if KVSTEP < 3:
        return
    _store(mk, B.ko, k_out_ap, B.ko[:, :2 * nh, :].rearrange("p g d -> p (g d)"))
    _copy(mk, 'act', [B.ko], [B.kb], B.kb[:, :nh, :], B.ko[:, :2 * nh, :].rearrange("p (h c) d -> p h (c d)", c=2))
    pt = nextpst(cx)
    for h in range(nh):
        _tr(mk, [B.kb, cx.identb], [pt], pt[:, h * 128:(h + 1) * 128], B.kb[:, h, :], cx.identb[:])
    _copy(mk, 'dve', [pt], [B.kT], B.kT[:, :nh, :], pt[:, :nh * 128].rearrange("p (h t) -> p h t", h=nh))
    if KVSTEP < 4:
        return
    for h in range(nh):
        _store(mk, B.kT, KTs_ap[h], B.kT[:, h, :], writes=[KTs], is_output=False)
    if KVSTEP < 5:
        return
    for i in range((nk + 511) // 512):
        ps = nextps(cx)
        w = min(512, nk - i * 512)
        for kc in range(8):
            _mm(mk, [B.hT, wkv], [ps], ps[:, :w], B.hT[:, kc, :], wkv[:, kc, nk + i * 512:nk + i * 512 + w], start=(kc == 0), stop=(kc == 7))
        hh = w // 128
        if VVAR != 1:
            _copy(mk, 'act' if VVAR == 0 else 'dve', [ps], [B.v32], B.v32[:, i * 4:i * 4 + hh, :].rearrange("p h e -> p (h e)"), ps[:, :w])
        if VVAR != 2:
            _copy(mk, 'dve', [ps], [B.vb], B.vb[:, i * 4:i * 4 + hh, 0:128], ps[:, :w].rearrange("p (h e) -> p h e", e=128))
    if KVSTEP < 7:
        return
    _store(mk, B.v32, v_out_ap, B.v32[:, :nh, :].rearrange("p h e -> p (h e)"))
    if KVSTEP < 8:
        return
    for h in range(nh):
        _store(mk, B.vb, Vs_ap[h], B.vb[:, h, :], writes=[Vs], is_output=False)


def alloc_attn(cx, es, nh):
    mk = cx.mk
    A = Ctx()
    A.ktile = [mk.tile([128, 4096], BF16, f'akt{i}', es=es) for i in range(2)]
    A.vtile = [mk.tile([128, 32, VW], BF16, f'avt{i}', es=es) for i in range(2)]
    A.QT = mk.tile([128, nh, 2, 128], BF16, 'aQT', es=es)
    A.zb = mk.tile([128, nh * 128], BF16, 'azb', es=es)
    A.PT = [mk.tile([128, 2, 2, 128], BF16, f'aPT{i}', es=es) for i in range(2)]
    A.oatt = mk.tile([128, nh, 128], F32, 'aoatt', es=es)
    A.ot = mk.tile([128, 128], F32, 'aot', es=es)
    A.rr = mk.tile([128, 2], F32, 'arr', es=es)
    A.ss = mk.tile([128, 8], F32, 'ass', es=es)
    A.og = mk.tile([128, nh, 128], BF16, 'aog', es=es)
    A.ogT = mk.tile([128, nh, 128], BF16, 'aogT', es=es)
    A.y = mk.tile([128, D], F32, 'ay', es=es)
    return A


def l1_qz(cx, B, A, Ab, Shb, winb, nh, qgain, cs_ap, zgain):
    mk = cx.mk
    rms_mod_T(cx, B.x, 1, Ab, Shb, B.hT, B.sc)
    nk = nh * 128
    qps = []
    for i in range((nk + 511) // 512):
        ps = nextps(cx)
        w = min(512, nk - i * 512)
        for kc in range(8):
            _mm(mk, [B.hT, winb], [ps], ps[:, :w], B.hT[:, kc, :], winb[:, kc, i * 512:i * 512 + w], start=(kc == 0), stop=(kc == 7))
        qps.append(ps)
    headnorm_rope(cx, B, qps, nh, qgain, cs_ap)
    _copy(mk, 'act', [B.ko], [B.kb], B.kb[:, :nh, :], B.ko[:, :2 * nh, :].rearrange("p (h c) d -> p h (c d)", c=2))
    pt = nextpst(cx)
    for h in range(nh):
        _tr(mk, [B.kb, cx.identb], [pt], pt[:, h * 128:(h + 1) * 128], B.kb[:, h, :], cx.identb[:])
    for c in range(2):
        _ts(mk, 'dve', [pt, cx.cm], [A.QT], A.QT[:, :, c, :], pt[:, :nh * 128].rearrange("p (h t) -> p h t", h=nh), cx.cm[:, c:c + 1], ALU.mult)
    for i in range((nk + 511) // 512):
        ps = nextps(cx)
        w = min(512, nk - i * 512)
        for kc in range(8):
            _mm(mk, [B.hT, winb], [ps], ps[:, :w], B.hT[:, kc, :], winb[:, kc, nk + i * 512:nk + i * 512 + w], start=(kc == 0), stop=(kc == 7))
        _act(mk, [ps], [A.zb], A.zb[:, i * 512:i * 512 + w], ps[:, :w], AF.Silu)
    _tt(mk, 'pool', [A.zb, zgain], [A.zb], A.zb[:].rearrange("p (h e) -> p h e", e=128),
        A.zb[:].rearrange("p (h e) -> p h e", e=128), zgain[:].unsqueeze(1).broadcast_to([128, nh, 128]), ALU.mult)


def attn_head(cx, A, h, kt, vt, nkb, masks, nlam):
    mk = cx.mk
    acc = [cx.ps[4], cx.ps[5]]
    npair = (nkb + 1) // 2
    for kp in range(npair):
        ps = nextps4(cx)
        PT = A.PT[kp % 2]
        nj = min(2, nkb - kp * 2)
        for j in range(nj):
            kb = kp * 2 + j
            _mm(mk, [kt, A.QT], [ps], ps[:, j * 256:(j + 1) * 256], kt[:, kb * 128:(kb + 1) * 128],
                A.QT[:, h, :, :].rearrange("p c q -> p (c q)"))
        _act(mk, [ps, cx.m4c], [PT], PT[:, :nj, :, :].rearrange("p j c q -> p (j c q)"), ps[:, :nj * 256], AF.Exp,
             scale=0.125, bias=cx.m4c[:, 0:1])
        for j in range(nj):
            kb = kp * 2 + j
            if kb in masks:
                mt, map_ = masks[kb]
                for c in range(2):
                    _tt(mk, 'pool', [PT, mt], [PT], PT[:, j, c, :], PT[:, j, c, :], map_, ALU.mult)
        for j in range(nj):
            kb = kp * 2 + j
            for c in range(2):
                _mm(mk, [PT, vt], [acc[c]], acc[c][:, 0:129], PT[:, j, c, :], vt[:, kb, 0:129],
                    start=(kb == 0), stop=(kb == nkb - 1))
    mk.op('dve', [acc[0]], [A.rr], lambda e: e.reciprocal(out=A.rr[:, 0:1], in_=acc[0][:, 128:129]))
    mk.op('dve', [acc[1]], [A.rr], lambda e: e.reciprocal(out=A.rr[:, 1:2], in_=acc[1][:, 128:129]))
    _tt(mk, 'dve', [A.rr, nlam], [A.rr], A.rr[:, 1:2], A.rr[:, 1:2], nlam[:, 0:1], ALU.mult)
    _ts(mk, 'dve', [acc[0], A.rr], [A.ot], A.ot[:], acc[0][:, 0:128], A.rr[:, 0:1], ALU.mult)
    _stt(mk, [acc[1], A.rr, A.ot], [A.oatt], A.oatt[:, h, :], acc[1][:, 0:128], A.rr[:, 1:2], A.ot[:], ALU.mult, ALU.add)


def attn_post(cx, B, A, nh):
    mk = cx.mk
    _tt(mk, 'pool', [A.oatt], [B.sq], B.sq[:, :nh * 128].rearrange("p (h e) -> p h e", e=128), A.oatt[:], A.oatt[:], ALU.mult)
    mk.op('dve', [B.sq], [A.ss], lambda e: e.tensor_reduce(
        out=A.ss[:, :nh], in_=B.sq[:, :nh * 128].rearrange("p (h e) -> p h e", e=128), axis=AX.X, op=ALU.add))
    _act(mk, [A.ss, cx.epsc], [A.ss], A.ss[:, :nh], A.ss[:, :nh], AF.Sqrt, scale=1.0 / 128, bias=cx.epsc[:, 0:1])
    mk.op('dve', [A.ss], [A.ss], lambda e: e.reciprocal(out=A.ss[:, :nh], in_=A.ss[:, :nh]))
    _tt(mk, 'dve', [A.oatt, A.ss], [A.oatt], A.oatt[:], A.oatt[:], A.ss[:, :nh].unsqueeze(2).broadcast_to([128, nh, 128]), ALU.mult)
    _tt(mk, 'pool', [A.oatt, A.zb], [A.og], A.og[:], A.oatt[:], A.zb[:].rearrange("p (h e) -> p h e", e=128), ALU.mult)
    pt = nextpst(cx)
    for h in range(nh):
        _tr(mk, [A.og, cx.identb], [pt], pt[:, h * 128:(h + 1) * 128], A.og[:, h, :], cx.identb[:])
    _copy(mk, 'act', [pt], [A.ogT], A.ogT[:], pt[:, :nh * 128].rearrange("p (h t) -> p h t", h=nh))


def out_proj_res(cx, A, ogT, nh_all, wob, x_tile_ap, x_tile, Gb, y_out_ap):
    mk = cx.mk
    for half in range(2):
        ps = nextps(cx)
        for h in range(nh_all):
            _mm(mk, [ogT, wob], [ps], ps[:], ogT[:, h, :], wob[:, h, half * 512:(half + 1) * 512], start=(h == 0), stop=(h == nh_all - 1))
        cs = slice(half * 512, (half + 1) * 512)
        _tt(mk, 'dve', [ps, Gb], [A.y], A.y[:, cs], ps[:], Gb[:, cs], ALU.mult)
        _tt(mk, 'pool', [A.y, x_tile], [A.y], A.y[:, cs], A.y[:, cs], x_tile_ap[:, cs], ALU.add)
    _store(mk, A.y, y_out_ap, A.y[:])


LAM_INIT = 0.8 - 0.6 * math.exp(-0.3 * 1)
U32 = mybir.dt.uint32


def phase_mods(cx, din, es, nm, w, cT_name):
    mk = cx.mk
    A_ = mk.tile([128, D], BF16, 'A' + nm, es=es); Sh_ = mk.tile([128, D], BF16, 'Sh' + nm, es=es)
    G_ = mk.tile([128, D], F32, 'G' + nm, es=es) if w == 3 else None
    with ExitStack() as est:
        cT = mk.tile([128, 8, 128], F32, 'cT' + nm, es=est); _load(mk, cT, cT[:], din[cT_name])
        gn = mk.tile([128, D], F32, 'gn' + nm, es=est); _load(mk, gn, gn[:], din[f'norm_{nm}'])
        o2 = [mk.tile([128, D], F32, f'ada_o{nm}{i}', es=est) for i in range(2)]
        st = alloc_stage(cx, est)
        ada_mod(cx, cT, din[f'ada_w_{nm}'], din[f'ada_b_{nm}'], w * D, [o2[0], o2[1]] + ([G_] if w == 3 else []), st)
        _copy(mk, 'pool', [o2[0]], [Sh_], Sh_[:], o2[0][:])
        _stt(mk, [o2[1], gn], [A_], A_[:], o2[1][:], 1.0, gn[:], ALU.add, ALU.mult)

    barrier(mk)
    return A_, Sh_, G_


def build_program(NBLK=32, with_sample=False):
    nc = bass.Bass("TRN2", target_bir_lowering=False)
    T = 256
    NT = NBLK * 128
    NG = NT // T
    NSLOT = NBLK // 2
    din = {}

    def inp(name, shape, dt=F32):
        din[name] = nc.dram_tensor(name, list(shape), dt, kind="ExternalInput").ap()

    def outp(name, shape, dt=F32):
        return nc.dram_tensor(name, list(shape), dt, kind="ExternalOutput").ap()

    inp('x_p', [NT, D]); inp('cT_p', [128, 8, 128])
    for nm, w in (('a', 3), ('kv', 2), ('b', 3)):
        inp(f'ada_w_{nm}', [D, w * D]); inp(f'ada_b_{nm}', [128, w * D]); inp(f'norm_{nm}', [128, D])
    inp('w_in_a', [D, NIN]); inp('cw', [128, 32, 4]); inp('a_log', [128, 16]); inp('dt_bias', [128, 16])
    inp('ogain', [128, 128]); inp('w_out_a', [2048, D])
    inp('w_kv', [D, 2048]); inp('kgain', [128, 64])
    inp('w_in_b', [D, 2048]); inp('qgain', [128, 64]); inp('lam', [128, 4, 64]); inp('subgain', [128, 128])
    inp('w_out_b', [D, D])
    inp('cs_p', [NT, 2, 8]); inp('cs_q', [NSLOT * 128, 2, 8]); inp('amask', [128, 2, 128]); inp('qtok', [128, NSLOT], U32)
    for k in ['m1', 'm2', 'nm_strict', 'nm_incl']:
        inp('p_' + k, [128, 128])
    inp('p_ml', [128, 7, 2, 128])
    inp('x_s', [128, D]); inp('cT_s', [128, 8, 128]); inp('S_s', [4, 16, 128, 128]); inp('conv_s_in', [4, 96, 128]); inp('vm_s', [128, 4])
    S_s_out = outp('S_s_out', [4, 16, 128, 128]); conv_s_out = outp('conv_s_out', [4, 96, 128]); x1loc = outp('x1loc', [32, D])
    y_p = outp('y_p', [NSLOT * 128, D]); S_p = outp('S_p', [16, 128, 128]); conv_p = outp('conv_p', [96, 128])
    k_p = outp('k_p', [NT, D]); v_p = outp('v_p', [NT, D])

    with ExitStack() as es:
        mk = MK(nc, es)
        cx = Ctx()
        cx.mk = mk
        cx.ps = [mk.psum([128, 512], F32, f'ps{i}') for i in range(6)]
        cx.pst = [mk.psum([128, 1024], BF16, f'pst{i}') for i in range(2)]
        cx.psi = 0
        cx.psti = 0
        setup_consts(cx, din)
        cx.epsc = mk.tile([128, 1], F32, 'epsc')
        cx.onec = mk.tile([128, 1], F32, 'onec')
        cx.eps128 = mk.tile([128, 1], F32, 'eps128')
        cx.m4c = mk.tile([128, 1], F32, 'm4c')
        cx.cm = mk.tile([128, 2], F32, 'cm')
        mk.op('pool', [], [cx.cm], lambda e: e.memset(cx.cm[:], 0.0))
        mk.op('pool', [cx.cm], [cx.cm], lambda e: e.memset(cx.cm[0:64, 0:1], 1.0))
        mk.op('pool', [cx.cm], [cx.cm], lambda e: e.memset(cx.cm[64:128, 1:2], 1.0))
        mk.op('pool', [], [cx.epsc], lambda e: e.memset(cx.epsc[:], EPS))
        mk.op('pool', [], [cx.onec], lambda e: e.memset(cx.onec[:], 1.0))
        mk.op('pool', [], [cx.eps128], lambda e: e.memset(cx.eps128[:], 128.0 * EPS))
        mk.op('pool', [], [cx.m4c], lambda e: e.memset(cx.m4c[:], -4.0))
        cx.cw = mk.tile([128, 32, 4], F32, 'cw'); _load(mk, cx.cw, cx.cw[:], din['cw'])
        cx.dtb = mk.tile([128, 16], F32, 'dtb'); _load(mk, cx.dtb, cx.dtb[:], din['dt_bias'])
        cx.nea = mk.tile([128, 16], F32, 'nea'); _load(mk, cx.nea, cx.nea[:], din['a_log'])
        _act(mk, [cx.nea], [cx.nea], cx.nea[:], cx.nea[:], AF.Exp)
        _ts(mk, 'dve', [cx.nea], [cx.nea], cx.nea[:], cx.nea[:], -1.0, ALU.mult)
        cx.ogain = mk.tile([128, 128], F32, 'ogain'); _load(mk, cx.ogain, cx.ogain[:], din['ogain'])
        kgain = mk.tile([128, 64], F32, 'kgain'); _load(mk, kgain, kgain[:], din['kgain'])
        qgain = mk.tile([128, 64], F32, 'qgain'); _load(mk, qgain, qgain[:], din['qgain'])
        zgain = mk.tile([128, 128], F32, 'zgain'); _load(mk, zgain, zgain[:], din['subgain'])
        _ts(mk, 'dve', [zgain], [zgain], zgain[:], zgain[:], 1.0 - LAM_INIT, ALU.mult)
        amask = mk.tile([128, 2, 128], BF16, 'amask')
        qtok = mk.tile([128, NSLOT], U32, 'qtok'); _load(mk, qtok, qtok[:], din['qtok'])
        ls = mk.tile([128, 2], F32, 'ls'); nlam = mk.tile([128, 1], F32, 'nlam')
        A0s = mk.tile([128, D], BF16, 'A0s'); Sh0s = mk.tile([128, D], BF16, 'Sh0s'); G0s = mk.tile([128, D], F32, 'G0s')
        vms = mk.tile([128, 4], F32, 'vms'); _load(mk, vms, vms[:], din['vm_s'])
        mods = {}
        for nm in ('a',):
            mods[nm] = (mk.tile([128, D], BF16, 'A' + nm), mk.tile([128, D], BF16, 'Sh' + nm),
                        mk.tile([128, D], F32, 'G' + nm) if nm != 'kv' else None)
        cx.wina_bf = mk.dram('wina_bf', [128, 8, NIN], BF16)
        woa_d = mk.dram('woa_bf', [128, 16, D], BF16)
        wkv_d = mk.dram('wkv_bf', [128, 8, 2048], BF16)
        winb_d = mk.dram('winb_bf', [128, 8, 2048], BF16)
        wob_d = mk.dram('wob_bf', [128, 8, D], BF16)
        x1s = mk.dram('x1s', [NT, D], F32)
        KTs = mk.dram('scr_KTs', [8, 128, NT], BF16, kind='ExternalOutput')
        Vs = mk.dram('scr_Vs', [8, 128, NBLK, VW], BF16, kind='ExternalOutput')
        with ExitStack() as es1:
            Mp = load_masks(cx, din, 'p_', 7, es1)
            amf = mk.tile([128, 2, 128], F32, 'amf', es=es1); _load(mk, amf, amf[:], din['amask'])
            _copy(mk, 'pool', [amf], [amask], amask[:], amf[:])
            lamt = mk.tile([128, 4, 64], F32, 'lamt', es=es1); _load(mk, lamt, lamt[:], din['lam'])
            lp = mk.tile([128, 2, 64], F32, 'lp', es=es1)
            _tt(mk, 'dve', [lamt], [lp], lp[:], lamt[:].rearrange("p (a b) d -> p a b d", b=2)[:, :, 0, :],
                lamt[:].rearrange("p (a b) d -> p a b d", b=2)[:, :, 1, :], ALU.mult)
            mk.op('dve', [lp], [ls], lambda e: e.tensor_reduce(out=ls[:], in_=lp[:], axis=AX.X, op=ALU.add))
            _act(mk, [ls], [ls], ls[:], ls[:], AF.Exp)
            _tt(mk, 'dve', [ls], [nlam], nlam[:], ls[:, 1:2], ls[:, 0:1], ALU.subtract)
            _ts(mk, 'dve', [nlam], [nlam], nlam[:], nlam[:], -LAM_INIT, ALU.add)
            cT = mk.tile([128, 8, 128], F32, 'cT', es=es1); _load(mk, cT, cT[:], din['cT_p'])
            gn = mk.tile([128, D], F32, 'gn', es=es1)
            o3 = [mk.tile([128, D], F32, f'ada_o{i}', es=es1) for i in range(3)]
            st = alloc_stage(cx, es1)
            for nm, w in (('a', 3),):
                A_, Sh_, G_ = mods[nm]
                _load(mk, gn, gn[:], din[f'norm_{nm}'])
                outs = [o3[0], o3[1]] + ([G_] if w == 3 else [])
                ada_mod(cx, cT, din[f'ada_w_{nm}'], din[f'ada_b_{nm}'], w * D, outs, st)
                _copy(mk, 'pool', [o3[0]], [Sh_], Sh_[:], o3[0][:])
                _stt(mk, [o3[1], gn], [A_], A_[:], o3[1][:], 1.0, gn[:], ALU.add, ALU.mult)
            cTs = mk.tile([128, 8, 128], F32, 'cTs', es=es1); _load(mk, cTs, cTs[:], din['cT_s'])
            _load(mk, gn, gn[:], din['norm_a'])
            ada_mod(cx, cTs, din['ada_w_a'], din['ada_b_a'], 3 * D, [o3[0], o3[1], G0s], st)
            _copy(mk, 'pool', [o3[0]], [Sh0s], Sh0s[:], o3[0][:])
            _stt(mk, [o3[1], gn], [A0s], A0s[:], o3[1][:], 1.0, gn[:], ALU.add, ALU.mult)
            prep_w_bf(cx, din['w_in_a'], 8, NIN, cx.wina_bf, st)
            prep_w_bf(cx, din['w_out_a'], 16, D, woa_d, st)
            prep_w_bf(cx, din['w_kv'], 8, 2048, wkv_d, st)
            prep_w_bf(cx, din['w_in_b'], 8, 2048, winb_d, st)
            prep_w_bf(cx, din['w_out_b'], 8, D, wob_d, st)
        barrier(mk)
        A0, Sh0, G0 = mods['a']
        with ExitStack() as es2:
            woa = mk.tile([128, 16, D], BF16, 'woa', es=es2)
            _load(mk, woa, woa[:], woa_d[:], reads=[woa_d])
            L = alloc_l0(cx, es2, T)
            S32 = mk.tile([128, 16, 128], F32, 'S32', es=es2)
            Sb = mk.tile([128, 16, 128], BF16, 'Sb', es=es2)
            mk.op('pool', [], [S32], lambda e: e.memset(S32[:], 0.0))
            mk.op('pool', [], [Sb], lambda e: e.memset(Sb[:], 0.0))
            mk.op('pool', [], [L.hist], lambda e: e.memset(L.hist[:], 0.0))
            sc_t = {'junk': L.junk, 'ss': L.nss, 'rstd': L.nrs, 'hb': L.hb, 't32': L.t32}
            if with_sample:
                xs = L.x[0]
                _load(mk, xs, xs[:, 0, :], din['x_s'])
                rms_mod_T(cx, xs, 1, A0s, Sh0s, L.hT, sc_t)
                hs_in = mk.tile([128, 128], F32, 'hs_in', es=es2)
                Ms = Ctx(); Ms.__dict__.update(Mp.__dict__); Ms.nlev = 3
                mk.op('pool', [], [hs_in], lambda e: e.memset(hs_in[:], 0.0))
                for lb in range(4):
                    _load(mk, hs_in, hs_in[0:96, :], din['conv_s_in'][lb])
                    pc = nextps(cx)
                    mk.op('pe', [hs_in, cx.identf], [pc], lambda e, pc=pc: e.transpose(out=pc[:, 0:128], in_=hs_in[:], identity=cx.identf[:]))
                    _copy(mk, 'dve', [pc], [L.hist], L.hist[:].rearrange("p f j -> p j f"), pc[:, 0:96].rearrange("p (j f) -> p j f", j=3))
                    l0_project(cx, L, 0, True, T=128, cpos=8 * lb + 8, hpos=8 * lb)
                    _load(mk, S32, S32[:], din['S_s'][lb].rearrange("h k v -> k h v"))
                    _copy(mk, 'pool', [S32], [Sb], Sb[:], S32[:])
                    gdn_block(cx, L, 0, Ms, S32, Sb, vm=(vms, vms[:, lb:lb + 1]))
                    gdn_out(cx, L, 0, xs, L.t32, G0s, woa)
                    _store(mk, L.t32, x1loc[8 * lb:8 * lb + 8, :], L.t32[8 * lb:8 * lb + 8, :])
                    _store(mk, S32, S_s_out[lb].rearrange("h k v -> k h v"), S32[:])
                    conv_state_out(cx, L, es2, conv_s_out[lb], f's{lb}')
                mk.op('pool', [], [S32], lambda e: e.memset(S32[:], 0.0))
                mk.op('pool', [], [Sb], lambda e: e.memset(Sb[:], 0.0))
                mk.op('pool', [], [L.hist], lambda e: e.memset(L.hist[:], 0.0))
            for gi in range(NG):
                xt = L.x[0]
                _load(mk, xt, xt[:], din['x_p'][gi * T:(gi + 1) * T, :].rearrange("(b p) d -> p b d", p=128))
                rms_mod_T(cx, xt, L.nb, A0, Sh0, L.hT, sc_t)
                l0_project(cx, L, gi, gi == NG - 1)
                for blk in range(L.nb):
                    gdn_block(cx, L, blk, Mp, S32, Sb)
                    x1 = L.t32
                    gdn_out(cx, L, blk, xt, x1, G0, woa)
                    r0 = gi * T + blk * 128
                    _store(mk, x1, x1s[r0:r0 + 128, :], x1[:], writes=[x1s], is_output=False)
            _store(mk, S32, S_p.rearrange("h k v -> k h v"), S32[:])
            conv_state_out(cx, L, es2, conv_p[:, :], 'p')
        barrier(mk)
        with ExitStack() as es3:
          if MAXPHASE >= 2:
              Akv, Shkv, _ = phase_mods(cx, din, es3, 'kv', 2, 'cT_p')
              wkv = mk.tile([128, 8, 2048], BF16, 'wkv', es=es3)
              _load(mk, wkv, wkv[:], wkv_d[:], reads=[wkv_d])
              B = alloc_l1(cx, es3)
              mk.op('pool', [], [B.vb], lambda e, B=B: e.memset(B.vb[:], 1.0))
              for blk in range(NBLK):
                  r0 = blk * 128
                  kv_block(cx, B, x1s[r0:r0 + 128, :], Akv, Shkv, wkv, 8, kgain, din['cs_p'][r0:r0 + 128, :, :],
                           k_p[r0:r0 + 128, :], v_p[r0:r0 + 128, :], KTs[:, :, r0:r0 + 128], Vs[:, :, blk, :], KTs, Vs, x_reads=[x1s])
        barrier(mk)
        with ExitStack() as es4:
          if MAXPHASE >= 3:
              Ab, Shb, Gb = phase_mods(cx, din, es4, 'b', 3, 'cT_p')
              winb = mk.tile([128, 8, 2048], BF16, 'winb', es=es4)
              wob = mk.tile([128, 8, D], BF16, 'wob', es=es4)
              _load(mk, winb, winb[:], winb_d[:], reads=[winb_d])
              _load(mk, wob, wob[:], wob_d[:], reads=[wob_d])
              B = alloc_l1(cx, es4)
              A = alloc_attn(cx, es4, 8)
              for i in range(NSLOT):
                  nkb = 2 * i + 2
                  mk.dma('pool', B.x, [x1s, qtok], [B.x], lambda e, i=i: e.indirect_dma_start(
                      out=B.x[:, 0, :], out_offset=None, in_=x1s[:, :],
                      in_offset=bass.IndirectOffsetOnAxis(ap=qtok[:, i:i + 1], axis=0)))
                  l1_qz(cx, B, A, Ab, Shb, winb, 8, qgain, din['cs_q'][i * 128:(i + 1) * 128, :, :], zgain)
                  if P3STEP < 2:
                      continue
                  masks = {nkb - 2: (amask, amask[:, 0, :]), nkb - 1: (amask, amask[:, 1, :])}
                  for h in range(8):
                      kt, vt = A.ktile[h % 2], A.vtile[h % 2]
                      _load(mk, kt, kt[:, :nkb * 128], KTs[h, :, 0:nkb * 128], reads=[KTs])
                      _load(mk, vt, vt[:, :nkb, :], Vs[h, :, 0:nkb, :], reads=[Vs])
                      attn_head(cx, A, h, kt, vt, nkb, masks, nlam)
                  if P3STEP < 3:
                      continue
                  attn_post(cx, B, A, 8)
                  if P3STEP < 4:
                      continue
                  out_proj_res(cx, A, A.ogT, 8, wob, B.x[:, 0, :], B.x, Gb, y_p[i * 128:(i + 1) * 128, :])
        mk.finish()
    return nc


def _rep(v, n=128):
    v = np.asarray(v, np.float32).reshape(1, -1)
    return np.ascontiguousarray(np.broadcast_to(v, (n, v.shape[1])))


def _rope_table(pos):
    inv_freq = (500000.0 ** (-np.arange(0, 16, 2, dtype=np.float32) / 16)).astype(np.float32)
    ang = pos.astype(np.float32)[:, None] * inv_freq[None, :]
    return np.stack([np.cos(ang), np.sin(ang)], axis=1).astype(np.float32)


def make_in_maps(inputs, NBLK=32):
    NT = NBLK * 128
    NSLOT = NBLK // 2
    hc = host_consts(False)
    g = lambda k: np.asarray(inputs[k])
    shared = {
        'ada_w_a': g('ada_w_a')[0], 'ada_b_a': _rep(g('ada_b_a')[0]), 'norm_a': _rep(g('norm_a')[0]),
        'ada_w_kv': g('ada_w_kv'), 'ada_b_kv': _rep(g('ada_b_kv')), 'norm_kv': _rep(g('norm_kv')),
        'ada_w_b': g('ada_w_b')[0], 'ada_b_b': _rep(g('ada_b_b')[0]), 'norm_b': _rep(g('norm_b')[0]),
        'w_in_a': g('w_in_a')[0],
        'cw': np.ascontiguousarray(g('conv_w_a')[0].reshape(4, 32, 128).transpose(2, 1, 0)),
        'a_log': _rep(g('a_log')[0]), 'dt_bias': _rep(g('dt_bias')[0]), 'ogain': _rep(g('gdn_out_gain')[0]),
        'w_out_a': g('w_out_a')[0], 'w_kv': g('w_kv'), 'kgain': _rep(g('k_gain')),
        'w_in_b': g('w_in_b')[0], 'qgain': _rep(g('q_gain')[0]),
        'lam': np.ascontiguousarray(np.broadcast_to(g('lam_params')[0][None], (128, 4, 64))).astype(np.float32),
        'subgain': _rep(g('subln_gain')[0]), 'w_out_b': g('w_out_b')[0],
        'cs_p': _rope_table(np.arange(NT)),
        'p_m1': hc['m1'], 'p_m2': hc['m2'], 'p_nm_strict': hc['nm_strict'], 'p_nm_incl': hc['nm_incl'], 'p_ml': hc['ml'],
    }
    i128 = np.arange(128)
    tri = (i128[:, None] <= i128[None, :]).astype(np.float32)
    maps = []
    for c in range(8):
        b, r = c // 2, c % 2
        m = dict(shared)
        m['x_p'] = np.ascontiguousarray(g('x_prompt')[b, :NT])
        cvec = g('c_prompt')[b]
        m['cT_p'] = np.ascontiguousarray(np.broadcast_to(cvec.reshape(8, 128).T[:, :, None], (128, 8, 128))).astype(np.float32)
        qblk = 2 * np.arange(NSLOT) + r
        qpos = (qblk[:, None] * 128 + i128[None, :]).reshape(-1)
        m['cs_q'] = _rope_table(qpos)
        m['qtok'] = np.ascontiguousarray((qblk[None, :] * 128 + i128[:, None]).astype(np.uint32))
        am = np.zeros((128, 2, 128), np.float32)
        if r == 0:
            am[:, 0, :] = tri
        else:
            am[:, 0, :] = 1.0
            am[:, 1, :] = tri
        m['amask'] = am
        if 'x_sample' in inputs:
            xs = np.zeros((128, D), np.float32)
            xs[:32] = g('x_sample')[4 * c:4 * c + 4].reshape(32, D)
            m['x_s'] = xs
            cs_ = np.zeros((D, 128), np.float32)
            cs_[:, :32] = np.repeat(g('c_sample')[4 * c:4 * c + 4], 8, axis=0).T
            m['cT_s'] = np.ascontiguousarray(cs_.reshape(8, 128, 128).transpose(1, 0, 2))
            m['S_s'] = np.ascontiguousarray(g('state_gdn')[0, 4 * c:4 * c + 4])
            m['conv_s_in'] = np.ascontiguousarray(g('state_conv')[0, 4 * c:4 * c + 4].reshape(4, 96, 128))
            vm = np.zeros((128, 4), np.float32)
            for lb in range(4):
                vm[8 * lb:8 * lb + 8, lb] = 1.0
            m['vm_s'] = vm
        maps.append(m)
    return maps


def _common_ctx(nc, es):
    mk = MK(nc, es)
    cx = Ctx()
    cx.mk = mk
    cx.ps = [mk.psum([128, 512], F32, f'ps{i}') for i in range(6)]
    cx.pst = [mk.psum([128, 1024], BF16, f'pst{i}') for i in range(2)]
    cx.psi = 0
    cx.psti = 0
    setup_consts(cx, {})
    cx.epsc = mk.tile([128, 1], F32, 'epsc')
    cx.m4c = mk.tile([128, 1], F32, 'm4c')
    cx.cm = mk.tile([128, 2], F32, 'cm')
    mk.op('pool', [], [cx.epsc], lambda e: e.memset(cx.epsc[:], EPS))
    mk.op('pool', [], [cx.m4c], lambda e: e.memset(cx.m4c[:], -4.0))
    mk.op('pool', [], [cx.cm], lambda e: e.memset(cx.cm[:], 0.0))
    mk.op('pool', [cx.cm], [cx.cm], lambda e: e.memset(cx.cm[0:64, 0:1], 1.0))
    mk.op('pool', [cx.cm], [cx.cm], lambda e: e.memset(cx.cm[64:128, 1:2], 1.0))
    return mk, cx


def build_sample_attn(NB=32, NPG=64):
    nc = bass.Bass("TRN2", target_bir_lowering=False)
    NTK = NB * 8
    NTILE = NTK // 128
    NGRP = NPG // 16
    din = {}

    def inp(name, shape, dt=F32):
        din[name] = nc.dram_tensor(name, list(shape), dt, kind="ExternalInput").ap()

    def outp(name, shape, dt=F32):
        return nc.dram_tensor(name, list(shape), dt, kind="ExternalOutput").ap()

    inp('x1all', [NTK, D])
    for j in range(NTILE):
        inp(f'cT_all{j}', [128, 8, 128])
    for nm, w in (('kv', 2), ('b', 3)):
        inp(f'ada_w_{nm}', [D, w * D]); inp(f'ada_b_{nm}', [128, w * D]); inp(f'norm_{nm}', [128, D])
    inp('w_kv_h', [D, 256]); inp('kgain', [128, 64]); inp('w_in_b_h', [D, 256]); inp('qgain', [128, 64])
    inp('subgain', [128, 128]); inp('lam', [128, 4, 64]); inp('cs_s', [NTK, 2, 8])
    inp('pool_k', [2560 * 8, 2048]); inp('pool_v', [2560 * 8, 2048])
    inp('pt_rep', [128, NB * NGRP], I32); inp('sub8', [128, 1]); inp('nmask', [128, 16, 8])
    og_h = outp('og_h', [NTK, 128]); k_s = outp('k_s_h', [NTK, 128]); v_s = outp('v_s_h', [NTK, 128])
    oat = outp('scr_oatt', [NTK, 128])
    KTs = nc.dram_tensor('scr_KTn', [1, 128, NTK], BF16, kind="ExternalOutput").ap()
    Vs = nc.dram_tensor('scr_Vn', [1, 128, NTILE, VW], BF16, kind="ExternalOutput").ap()
    with ExitStack() as es:
        mk, cx = _common_ctx(nc, es)
        KTs_t, Vs_t, oat_t = Tile('KTn', KTs), Tile('Vn', Vs), Tile('oat', oat)
        kgain = mk.tile([128, 64], F32, 'kgain'); _load(mk, kgain, kgain[:], din['kgain'])
        qgain = mk.tile([128, 64], F32, 'qgain'); _load(mk, qgain, qgain[:], din['qgain'])
        zgain = mk.tile([128, 128], F32, 'zgain'); _load(mk, zgain, zgain[:], din['subgain'])
        _ts(mk, 'dve', [zgain], [zgain], zgain[:], zgain[:], 1.0 - LAM_INIT, ALU.mult)
        nmf = mk.tile([128, 16, 8], F32, 'nmf'); _load(mk, nmf, nmf[:], din['nmask'])
        nmask = mk.tile([128, 16, 8], BF16, 'nmask'); _copy(mk, 'pool', [nmf], [nmask], nmask[:], nmf[:])
        lamt = mk.tile([128, 4, 64], F32, 'lamt'); _load(mk, lamt, lamt[:], din['lam'])
        lp = mk.tile([128, 2, 64], F32, 'lp'); ls = mk.tile([128, 2], F32, 'ls'); nlam = mk.tile([128, 1], F32, 'nlam')
        _tt(mk, 'dve', [lamt], [lp], lp[:], lamt[:].rearrange("p (a b) d -> p a b d", b=2)[:, :, 0, :],
            lamt[:].rearrange("p (a b) d -> p a b d", b=2)[:, :, 1, :], ALU.mult)
        mk.op('dve', [lp], [ls], lambda e: e.tensor_reduce(out=ls[:], in_=lp[:], axis=AX.X, op=ALU.add))
        _act(mk, [ls], [ls], ls[:], ls[:], AF.Exp)
        _tt(mk, 'dve', [ls], [nlam], nlam[:], ls[:, 1:2], ls[:, 0:1], ALU.subtract)
        _ts(mk, 'dve', [nlam], [nlam], nlam[:], nlam[:], -LAM_INIT, ALU.add)
        pti = mk.tile([128, NB * NGRP], I32, 'pti'); _load(mk, pti, pti[:], din['pt_rep'])
        sub8 = mk.tile([128, 1], F32, 'sub8'); _load(mk, sub8, sub8[:], din['sub8'])
        ptf = mk.tile([128, NB * NGRP], F32, 'ptf'); _copy(mk, 'dve', [pti], [ptf], ptf[:], pti[:])
        _ts(mk, 'dve', [ptf, sub8], [ptf], ptf[:], ptf[:], 8.0, ALU.mult, sub8[:, 0:1], ALU.add)
        idx = mk.tile([128, NB * NGRP], U32, 'idx'); _copy(mk, 'dve', [ptf], [idx], idx[:], ptf[:])
        wkv = mk.tile([128, 8, 256], BF16, 'wkvh'); winb = mk.tile([128, 8, 256], BF16, 'winbh')
        QTall = mk.tile([128, NTILE, 2, 128], BF16, 'QTall')
        zball = mk.tile([128, NTILE, 128], BF16, 'zball')
        KTn = mk.tile([128, NTILE, 128], BF16, 'KTnew')
        Vn = mk.tile([128, NTILE, VW], BF16, 'Vnew')
        with ExitStack() as es1:
            wf = mk.tile([128, 8, 256], F32, 'wf', es=es1)
            _load(mk, wf, wf[:], din['w_kv_h'].rearrange("(kc p) n -> p kc n", p=128))
            _copy(mk, 'pool', [wf], [wkv], wkv[:], wf[:])
            _load(mk, wf, wf[:], din['w_in_b_h'].rearrange("(kc p) n -> p kc n", p=128))
            _copy(mk, 'pool', [wf], [winb], winb[:], wf[:])
            B = alloc_l1(cx, es1)
            mk.op('pool', [], [B.vb], lambda e, B=B: e.memset(B.vb[:], 1.0))
            A = alloc_attn_small(cx, es1)
            for j in range(NTILE):
                with ExitStack() as esm:
                    Akv, Shkv, _ = phase_mods(cx, din, esm, 'kv', 2, f'cT_all{j}')
                    Ab, Shb, _g = phase_mods(cx, din, esm, 'b', 2, f'cT_all{j}')
                    r0 = j * 128
                    kv_block(cx, B, din['x1all'][r0:r0 + 128, :], Akv, Shkv, wkv, 1, kgain, din['cs_s'][r0:r0 + 128, :, :],
                             k_s[r0:r0 + 128, :], v_s[r0:r0 + 128, :], KTs[:, :, r0:r0 + 128], Vs[:, :, j, :], KTs_t, Vs_t)
                    _copy(mk, 'pool', [B.kT], [KTn], KTn[:, j, :], B.kT[:, 0, :])
                    _copy(mk, 'pool', [B.vb], [Vn], Vn[:, j, :], B.vb[:, 0, :])
                    l1_qz(cx, B, A, Ab, Shb, winb, 1, qgain, din['cs_s'][r0:r0 + 128, :, :], zgain)
                    _copy(mk, 'pool', [A.QT], [QTall], QTall[:, j, :, :], A.QT[:, 0, :, :])
                    _copy(mk, 'pool', [A.zb], [zball], zball[:, j, :], A.zb[:])
                    barrier(mk)
        barrier(mk)
        with ExitStack() as es2:
            kf = [mk.tile([128, 2048], F32, f'kf{i}', es=es2) for i in range(2)]
            vf = [mk.tile([128, 2048], F32, f'vf{i}', es=es2) for i in range(2)]
            kb16 = mk.tile([128, 16, 128], BF16, 'kb16', es=es2)
            vb16 = [mk.tile([128, 16, VW], BF16, f'vb16{i}', es=es2) for i in range(2)]
            KTg = mk.tile([128, 16, 128], BF16, 'KTg', es=es2)
            PT = [mk.tile([128, 16, 2, 8], BF16, f'PTs{i}', es=es2) for i in range(2)]
            PTn = mk.tile([128, 2, 8], BF16, 'PTn', es=es2)
            rr = mk.tile([128, 2], F32, 'rrs', es=es2); ot = mk.tile([128, 128], F32, 'ots', es=es2)
            ob = mk.tile([128, 128], F32, 'obs', es=es2)
            for t_ in vb16:
                mk.op('pool', [], [t_], lambda e, t_=t_: e.memset(t_[:], 1.0))
            pk = Tile('pool_k', din['pool_k']); pv = Tile('pool_v', din['pool_v'])
            n = 0
            for b in range(NB):
                j, bb = b // 16, b % 16
                qT = QTall[:, j, :, 8 * bb:8 * bb + 8]
                acc = [cx.ps[4], cx.ps[5]]
                for g in range(NGRP):
                    n += 1
                    kf_, vf_, vb_, PT_ = kf[n % 2], vf[n % 2], vb16[n % 2], PT[n % 2]
                    col = b * NGRP + g
                    mk.dma('pool', kf_, [idx], [kf_], lambda e, kf_=kf_, col=col: e.indirect_dma_start(
                        out=kf_[:], out_offset=None, in_=din['pool_k'],
                        in_offset=bass.IndirectOffsetOnAxis(ap=idx[:, col:col + 1], axis=0)))
                    mk.dma('pool', vf_, [idx], [vf_], lambda e, vf_=vf_, col=col: e.indirect_dma_start(
                        out=vf_[:], out_offset=None, in_=din['pool_v'],
                        in_offset=bass.IndirectOffsetOnAxis(ap=idx[:, col:col + 1], axis=0)))
                    _copy(mk, 'dve', [kf_], [kb16], kb16[:].rearrange("p t d -> p (t d)"), kf_[:])
                    _copy(mk, 'act', [vf_], [vb_], vb_[:, :, 0:128], vf_[:].rearrange("p (t d) -> p t d", d=128))
                    for half in range(2):
                        pt = nextpst(cx)
                        for t8 in range(8):
                            _tr(mk, [kb16, cx.identb], [pt], pt[:, t8 * 128:(t8 + 1) * 128], kb16[:, half * 8 + t8, :], cx.identb[:])
                        _copy(mk, 'dve', [pt], [KTg], KTg[:, half * 8:half * 8 + 8, :], pt[:].rearrange("p (t k) -> p t k", t=8))
                    ps = nextps4(cx)
                    for tl in range(16):
                        _mm(mk, [KTg, QTall], [ps], ps[:, tl * 16:(tl + 1) * 16], KTg[:, tl, :], qT)
                    _act(mk, [ps, cx.m4c], [PT_], PT_[:].rearrange("p t c q -> p (t c q)"), ps[:, 0:256], AF.Exp,
                         scale=0.125, bias=cx.m4c[:, 0:1])
                    for tl in range(16):
                        for c in range(2):
                            _mm(mk, [PT_, vb_], [acc[c]], acc[c][0:8, 0:129], PT_[:, tl, c, :], vb_[:, tl, 0:129],
                                start=(g == 0 and tl == 0), stop=False)
                ps = nextps4(cx)
                _mm(mk, [KTn, QTall], [ps], ps[:, 0:16], KTn[:, j, :], qT)
                _act(mk, [ps, cx.m4c], [PTn], PTn[:].rearrange("p c q -> p (c q)"), ps[:, 0:16], AF.Exp, scale=0.125, bias=cx.m4c[:, 0:1])
                _tt(mk, 'pool', [PTn, nmask], [PTn], PTn[:], PTn[:], nmask[:, bb:bb + 1, :].broadcast_to([128, 2, 8]), ALU.mult)
                for c in range(2):
                    _mm(mk, [PTn, Vn], [acc[c]], acc[c][0:8, 0:129], PTn[:, c, :], Vn[:, j, 0:129], start=False, stop=True)
                mk.op('dve', [acc[0]], [rr], lambda e, acc=acc: e.reciprocal(out=rr[0:8, 0:1], in_=acc[0][0:8, 128:129]))
                mk.op('dve', [acc[1]], [rr], lambda e, acc=acc: e.reciprocal(out=rr[0:8, 1:2], in_=acc[1][0:8, 128:129]))
                _tt(mk, 'dve', [rr, nlam], [rr], rr[0:8, 1:2], rr[0:8, 1:2], nlam[0:8, 0:1], ALU.mult)
                _ts(mk, 'dve', [acc[0], rr], [ot], ot[0:8, :], acc[0][0:8, 0:128], rr[0:8, 0:1], ALU.mult)
                _stt(mk, [acc[1], rr, ot], [ob], ob[0:8, :], acc[1][0:8, 0:128], rr[0:8, 1:2], ot[0:8, :], ALU.mult, ALU.add)
                _store(mk, ob, oat[8 * b:8 * b + 8, :], ob[0:8, :], writes=[oat_t])
            A2 = alloc_attn_small(cx, es2)
            sqt = mk.tile([128, D], F32, 'sqt', es=es2)
            B2 = Ctx(); B2.sq = sqt
            for j in range(NTILE):
                _load(mk, A2.oatt, A2.oatt[:, 0, :], oat[j * 128:(j + 1) * 128, :], reads=[oat_t])
                _copy(mk, 'pool', [zball], [A2.zb], A2.zb[:], zball[:, j, :])
                attn_post_tok(cx, B2, A2)
                _store(mk, A2.y, og_h[j * 128:(j + 1) * 128, :], A2.y[:, 0:128])
        mk.finish()
    return nc


def alloc_attn_small(cx, es):
    mk = cx.mk
    A = Ctx()
    A.QT = mk.tile([128, 1, 2, 128], BF16, 'sQT', es=es)
    A.zb = mk.tile([128, 128], BF16, 'szb', es=es)
    A.oatt = mk.tile([128, 1, 128], F32, 'soatt', es=es)
    A.ss = mk.tile([128, 8], F32, 'sss', es=es)
    A.y = mk.tile([128, D], F32, 'sy', es=es)
    return A


def attn_post_tok(cx, B, A):
    mk = cx.mk
    _tt(mk, 'pool', [A.oatt], [B.sq], B.sq[:, 0:128], A.oatt[:, 0, :], A.oatt[:, 0, :], ALU.mult)
    mk.op('dve', [B.sq], [A.ss], lambda e: e.tensor_reduce(out=A.ss[:, 0:1], in_=B.sq[:, 0:128], axis=AX.X, op=ALU.add))
    _act(mk, [A.ss, cx.epsc], [A.ss], A.ss[:, 0:1], A.ss[:, 0:1], AF.Sqrt, scale=1.0 / 128, bias=cx.epsc[:, 0:1])
    mk.op('dve', [A.ss], [A.ss], lambda e: e.reciprocal(out=A.ss[:, 0:1], in_=A.ss[:, 0:1]))
    _ts(mk, 'dve', [A.oatt, A.ss], [A.oatt], A.oatt[:, 0, :], A.oatt[:, 0, :], A.ss[:, 0:1], ALU.mult)
    _tt(mk, 'pool', [A.oatt, A.zb], [A.y], A.y[:, 0:128], A.oatt[:, 0, :], A.zb[:], ALU.mult)


def build_sample_out():
    nc = bass.Bass("TRN2", target_bir_lowering=False)
    din = {}

    def inp(name, shape, dt=F32):
        din[name] = nc.dram_tensor(name, list(shape), dt, kind="ExternalInput").ap()

    inp('x1l', [128, D]); inp('og_l', [128, D]); inp('cT_s', [128, 8, 128])
    inp('ada_w_b', [D, 3 * D]); inp('ada_b_b', [128, 3 * D]); inp('norm_b', [128, D]); inp('w_out_b', [D, D])
    y_s = nc.dram_tensor('y_s', [128, D], F32, kind="ExternalOutput").ap()
    with ExitStack() as es:
        mk, cx = _common_ctx(nc, es)
        wob = mk.tile([128, 8, D], BF16, 'wob')
        A = Ctx()
        A.y = mk.tile([128, D], F32, 'y')
        x1 = mk.tile([128, D], F32, 'x1'); _load(mk, x1, x1[:], din['x1l'])
        og = mk.tile([128, D], F32, 'og'); _load(mk, og, og[:], din['og_l'])
        ogb = mk.tile([128, 8, 128], BF16, 'ogb'); _copy(mk, 'dve', [og], [ogb], ogb[:].rearrange("p h e -> p (h e)"), og[:])
        ogT = mk.tile([128, 8, 128], BF16, 'ogT')
        _Ab, _Shb, Gb = phase_mods(cx, din, es, 'b', 3, 'cT_s')
        with ExitStack() as es1:
            wf = mk.tile([128, 8, D], F32, 'wf', es=es1)
            _load(mk, wf, wf[:], din['w_out_b'].rearrange("(kc p) n -> p kc n", p=128))
            _copy(mk, 'pool', [wf], [wob], wob[:], wf[:])
            pt = nextpst(cx)
            for h in range(8):
                _tr(mk, [ogb, cx.identb], [pt], pt[:, h * 128:(h + 1) * 128], ogb[:, h, :], cx.identb[:])
            _copy(mk, 'act', [pt], [ogT], ogT[:], pt[:].rearrange("p (h t) -> p h t", h=8))
            out_proj_res(cx, A, ogT, 8, wob, x1[:], x1, Gb, y_s[:, :])
        mk.finish()
    return nc


_PROG_CACHE = {}
DBG = {}


def kernel_impl(inputs, NBLK=32):
    g = lambda k: np.asarray(inputs[k])
    NT = NBLK * 128
    if ('p1', NBLK) not in _PROG_CACHE:
        _PROG_CACHE[('p1', NBLK)] = build_program(NBLK, True)
        _PROG_CACHE['p2'] = build_sample_attn(32, 64)
        _PROG_CACHE['p3'] = build_sample_out()
    maps = make_in_maps(inputs, NBLK)
    res1 = run_bass_kernel_spmd(_PROG_CACHE[('p1', NBLK)], maps, core_ids=list(range(8))).results
    y_p = np.zeros((4, NT, D), np.float32)
    for b in range(4):
        yv = y_p[b].reshape(NBLK // 2, 2, 128, D)
        yv[:, 0] = res1[2 * b]['y_p'].reshape(-1, 128, D)
        yv[:, 1] = res1[2 * b + 1]['y_p'].reshape(-1, 128, D)
    st_p = np.stack([res1[2 * b]['S_p'] for b in range(4)])[None]
    conv_p = np.stack([res1[2 * b]['conv_p'].reshape(3, 4096) for b in range(4)])[None]
    k_p = np.stack([res1[2 * b]['k_p'].reshape(NT, 8, 128) for b in range(4)])
    v_p = np.stack([res1[2 * b]['v_p'].reshape(NT, 8, 128) for b in range(4)])
    st_s = np.concatenate([res1[c]['S_s_out'] for c in range(8)])[None]
    conv_s = np.concatenate([res1[c]['conv_s_out'].reshape(4, 3, 4096) for c in range(8)])[None]
    x1all = np.concatenate([res1[c]['x1loc'] for c in range(8)])
    cs = np.repeat(g('c_sample'), 8, axis=0)
    cT_all = [np.ascontiguousarray(cs[j * 128:(j + 1) * 128].T.reshape(8, 128, 128).transpose(1, 0, 2)) for j in range(2)]
    pt = g('page_table').astype(np.int32)
    pt_rep = np.ascontiguousarray(np.repeat(pt.reshape(32, 4, 16), 8, axis=2).transpose(2, 0, 1).reshape(128, 128))
    i128 = np.arange(128)
    nmask = np.zeros((128, 16, 8), np.float32)
    for bb in range(16):
        for q in range(8):
            nmask[8 * bb:8 * bb + q + 1, bb, q] = 1.0
    pos = 8192 + (np.arange(256) % 8)
    shared2 = {
        'x1all': x1all, 'cT_all0': cT_all[0], 'cT_all1': cT_all[1],
        'ada_w_kv': g('ada_w_kv'), 'ada_b_kv': _rep(g('ada_b_kv')), 'norm_kv': _rep(g('norm_kv')),
        'ada_w_b': g('ada_w_b')[0], 'ada_b_b': _rep(g('ada_b_b')[0]), 'norm_b': _rep(g('norm_b')[0]),
        'kgain': _rep(g('k_gain')), 'qgain': _rep(g('q_gain')[0]), 'subgain': _rep(g('subln_gain')[0]),
        'lam': np.ascontiguousarray(np.broadcast_to(g('lam_params')[0][None], (128, 4, 64))).astype(np.float32),
        'cs_s': _rope_table(pos), 'pt_rep': pt_rep, 'sub8': (i128 % 8).astype(np.float32).reshape(128, 1), 'nmask': nmask,
    }
    maps2 = []
    wkv, winb = g('w_kv'), g('w_in_b')[0]
    ck, cv = g('cache_k'), g('cache_v')
    for h in range(8):
        m = dict(shared2)
        m['w_kv_h'] = np.ascontiguousarray(np.concatenate([wkv[:, h * 128:(h + 1) * 128], wkv[:, 1024 + h * 128:1024 + (h + 1) * 128]], axis=1))
        m['w_in_b_h'] = np.ascontiguousarray(np.concatenate([winb[:, h * 128:(h + 1) * 128], winb[:, 1024 + h * 128:1024 + (h + 1) * 128]], axis=1))
        m['pool_k'] = np.ascontiguousarray(ck[:, :, h, :]).reshape(2560 * 8, 2048)
        m['pool_v'] = np.ascontiguousarray(cv[:, :, h, :]).reshape(2560 * 8, 2048)
        maps2.append(m)
    res2 = run_bass_kernel_spmd(_PROG_CACHE['p2'], maps2, core_ids=list(range(8))).results
    k_s = np.stack([res2[h]['k_s_h'] for h in range(8)], axis=1).reshape(32, 8, 8, 128)
    v_s = np.stack([res2[h]['v_s_h'] for h in range(8)], axis=1).reshape(32, 8, 8, 128)
    og_all = np.stack([res2[h]['og_h'] for h in range(8)], axis=1).reshape(256, D)
    DBG['x1all'] = x1all; DBG['og_all'] = og_all; DBG['oatt'] = np.stack([res2[h]['scr_oatt'] for h in range(8)], axis=1)
    maps3 = []
    for c in range(8):
        x1l = np.zeros((128, D), np.float32); x1l[:32] = x1all[32 * c:32 * c + 32]
        ogl = np.zeros((128, D), np.float32); ogl[:32] = og_all[32 * c:32 * c + 32]
        maps3.append({'x1l': x1l, 'og_l': ogl, 'cT_s': maps[c]['cT_s'], 'ada_w_b': g('ada_w_b')[0],
                      'ada_b_b': _rep(g('ada_b_b')[0]), 'norm_b': _rep(g('norm_b')[0]), 'w_out_b': g('w_out_b')[0]})
    res3 = run_bass_kernel_spmd(_PROG_CACHE['p3'], maps3, core_ids=list(range(8))).results
    y_s = np.concatenate([res3[c]['y_s'][:32] for c in range(8)]).reshape(32, 8, D)
    return (y_p, y_s, st_p, conv_p, k_p, v_p, st_s, conv_s, k_s, v_s)


def kernel(**inputs):
    return kernel_impl(inputs, 32)
```

```python
import math
from contextlib import ExitStack

import numpy as np
import concourse.bass as bass
import concourse.mybir as mybir
from concourse.bass_utils import run_bass_kernel_spmd

F32 = mybir.dt.float32
BF16 = mybir.dt.bfloat16
I32 = mybir.dt.int32
AF = mybir.ActivationFunctionType
ALU = mybir.AluOpType
AX = mybir.AxisListType
EPOCH = 1 << 28
EPS = 1e-6
DEBUG = False
P3STEP = 9
VVAR = 3
NDSEM = 24
KVSTEP = 9
VW = 160
MAXPHASE = 3
INV_LEVELS = 7
NEG = -30000.0

D = 1024
HV = 16
HQ = 8
NQKV = 4096
NIN = 6176
DH = 64


class Buf:
    def __init__(self, name):
        self.name = name
        self.writer = None
        self.readers = []
        self.dsem = None
        self.dcount = 0


class Tile(Buf):
    def __init__(self, name, t):
        super().__init__(name)
        self.t = t

    def __getitem__(self, k):
        return self.t[k]


class MK:
    def __init__(self, nc, es):
        self.nc = nc
        self.es = es
        self.names = ['pe', 'dve', 'act', 'pool', 'sp']
        self.cnt = {k: 0 for k in self.names}
        self.sems = {k: [] for k in self.names}
        self.waited = {k: {} for k in self.names}
        self.q = {k: [] for k in self.names}
        self.n = 0
        self.out_owners = []
        self.all_owners = []
        self.groups = []
        self.ngrp = 0
        self.same_sync = {'pe': False, 'dve': True, 'act': True, 'pool': True, 'sp': True}

    def tile(self, shape, dt, name=None, es=None):
        self.n += 1
        name = name or f"t{self.n}"
        t = (es or self.es).enter_context(self.nc.sbuf_tensor(f"{name}_{self.n}", list(shape), dt))
        return Tile(name, t)

    def psum(self, shape, dt, name=None, es=None):
        self.n += 1
        name = name or f"p{self.n}"
        t = (es or self.es).enter_context(self.nc.psum_tensor(f"{name}_{self.n}", list(shape), dt))
        return Tile(name, t)

    def dram(self, name, shape, dt, kind="Internal"):
        t = self.nc.dram_tensor(name, list(shape), dt, kind=kind)
        return Tile(name, t.ap())

    def _sem(self, e, epoch):
        while len(self.sems[e]) <= epoch:
            self.sems[e].append(self.es.enter_context(self.nc.semaphore(f"s_{e}_{len(self.sems[e])}")))
        return self.sems[e][epoch]

    def _wait_tok(self, E, tok):
        if tok[0] == 'e':
            _, e, count = tok
            if e == E and not self.same_sync[E]:
                return
            key = ('e', e)
            if self.waited[E].get(key, 0) >= count:
                return
            self.waited[E][key] = count
            ep, v = (count - 1) // EPOCH, (count - 1) % EPOCH + 1
            self.q[E].append(('w', self._sem(e, ep), v))
        else:
            g = tok[1].grp
            key = ('d', id(g))
            if self.waited[E].get(key, 0) >= g.dcount:
                return
            self.waited[E][key] = g.dcount
            self.q[E].append(('w', g.dsem, 16 * g.dcount))

    def deps(self, E, reads, writes):
        for b in reads:
            if b.writer is not None:
                self._wait_tok(E, b.writer)
        for b in writes:
            if b.writer is not None:
                self._wait_tok(E, b.writer)
            for r in b.readers:
                self._wait_tok(E, r)

    def _record(self, tok, reads, writes):
        for b in writes:
            b.writer = tok
            b.readers = []
        for b in reads:
            if b not in writes:
                b.readers.append(tok)
                if len(b.readers) > 24:
                    b.readers = b.readers[-24:] if False else b.readers

    def op(self, E, reads, writes, fn):
        self.deps(E, reads, writes)
        self.cnt[E] += 1
        c = self.cnt[E]
        self.q[E].append(('i', fn, self._sem(E, (c - 1) // EPOCH), 1))
        self._record(('e', E, c), reads, writes)

    def dma(self, Q, owner, reads, writes, fn, is_output=False):
        self.deps(Q, reads, writes)
        if getattr(owner, 'grp', None) is None:
            if len(self.groups) < NDSEM:
                g = Buf(f"grp{len(self.groups)}")
                g.dsem = self.es.enter_context(self.nc.semaphore(f"dgrp_{len(self.groups)}"))
                self.groups.append(g)
            owner.grp = self.groups[self.ngrp % NDSEM]
            self.ngrp += 1
        g = owner.grp
        g.dcount += 1
        owner.dcount += 1
        if owner not in self.all_owners:
            self.all_owners.append(owner)
        self.q[Q].append(('i', fn, g.dsem, 16))
        self._record(('d', owner), reads, writes)
        if is_output and owner not in self.out_owners:
            self.out_owners.append(owner)

    def finish(self):
        for g in self.groups:
            if g.dcount:
                self.q['sp'].append(('w', g.dsem, 16 * g.dcount))
        nc = self.nc
        with nc.Block() as block:
            regs = {'pe': block.tensor, 'dve': block.vector, 'act': block.scalar,
                    'pool': block.gpsimd, 'sp': block.sync}
            for E in self.names:
                items = self.q[E]

                def body(e, items=items):
                    for it in items:
                        if it[0] == 'w':
                            e.wait_ge(it[1], it[2])
                        else:
                            ins = it[1](e)
                            if it[2] is not None:
                                ins.then_inc(it[2], it[3])
                regs[E](body)


def host_consts(sample):
    i = np.arange(128)
    if sample:
        same = (i[:, None] // 8) == (i[None, :] // 8)
    else:
        same = np.ones((128, 128), bool)
    c = {}
    c['m1'] = ((i[:, None] <= i[None, :]) & same).astype(np.float32)
    c['m2'] = ((i[:, None] > i[None, :]) & same).astype(np.float32)
    c['nm_strict'] = np.where((i[:, None] < i[None, :]) & same, 0.0, NEG).astype(np.float32)
    c['nm_incl'] = np.where((i[:, None] <= i[None, :]) & same, 0.0, NEG).astype(np.float32)
    c['same'] = same.astype(np.float32)
    nlev = 3 if sample else 7
    ml = np.zeros((128, nlev, 2, 128), np.float32)
    for lv in range(nlev):
        b = 1 << lv
        r, cc = i[:, None], i[None, :]
        low = ((r // (2 * b)) == (cc // (2 * b))) & ((r // b) % 2 == 1) & ((cc // b) % 2 == 0)
        ml[:, lv, 1, :] = -low.astype(np.float32)
        ml[:, lv, 0, :] = -low.T.astype(np.float32)
    c['ml'] = ml
    return c


class Ctx:
    pass


def _act(mk, reads, writes, out, in_, func, **kw):
    mk.op('act', reads, writes, lambda e: e.activation(out=out, in_=in_, func=func, **kw))


def _tt(mk, E, reads, writes, out, in0, in1, op):
    mk.op(E, reads, writes, lambda e: e.tensor_tensor(out=out, in0=in0, in1=in1, op=op))


def _ts(mk, E, reads, writes, out, in0, s1, op0, s2=None, op1=None):
    if op1 is None:
        mk.op(E, reads, writes, lambda e: e.tensor_scalar(out=out, in0=in0, scalar1=s1, scalar2=None, op0=op0))
    else:
        mk.op(E, reads, writes, lambda e: e.tensor_scalar(out=out, in0=in0, scalar1=s1, scalar2=s2, op0=op0, op1=op1))


def _stt(mk, reads, writes, out, in0, scalar, in1, op0, op1, E='dve'):
    mk.op(E, reads, writes, lambda e: e.scalar_tensor_tensor(out=out, in0=in0, scalar=scalar, in1=in1, op0=op0, op1=op1))


def _mm(mk, reads, writes, out, lhsT, rhs, start=True, stop=True):
    mk.op('pe', reads, writes, lambda e: e.matmul(out, lhsT=lhsT, rhs=rhs, start=start, stop=stop))


def _tr(mk, reads, writes, out, in_, ident):
    mk.op('pe', reads, writes, lambda e: e.transpose(out=out, in_=in_, identity=ident))


def _copy(mk, E, reads, writes, out, in_):
    if E == 'act':
        mk.op('act', reads, writes, lambda e: e.activation(out=out, in_=in_, func=AF.Copy))
    else:
        mk.op(E, reads, writes, lambda e: e.tensor_copy(out=out, in_=in_))


def _load(mk, tile, out_ap, in_ap, Q='sp', reads=()):
    mk.dma(Q, tile, list(reads), [tile], lambda e: e.dma_start(out=out_ap, in_=in_ap))


def _store(mk, tile, out_ap, in_ap, Q='sp', writes=(), is_output=True):
    mk.dma(Q, tile, [tile], list(writes), lambda e: e.dma_start(out=out_ap, in_=in_ap), is_output=is_output)


def setup_consts(cx, din):
    mk = cx.mk
    cx.identb = mk.tile([128, 128], BF16, 'identb')
    cx.identf = mk.tile([128, 128], F32, 'identf')
    cx.onesf = mk.tile([128, 128], F32, 'onesf')
    cx.onesb = mk.tile([128, 128], BF16, 'onesb')
    mk.op('pool', [], [cx.onesf], lambda e: e.memset(cx.onesf[:], 1.0))
    mk.op('pool', [], [cx.onesb], lambda e: e.memset(cx.onesb[:], 1.0))
    mk.op('pool', [], [cx.identf], lambda e: e.memset(cx.identf[:], 1.0))
    mk.op('pool', [cx.identf], [cx.identf], lambda e: e.affine_select(
        out=cx.identf[:], in_=cx.identf[:], pattern=[[-1, 128]], compare_op=ALU.is_equal, fill=0.0,
        base=0, channel_multiplier=1))
    _copy(mk, 'pool', [cx.identf], [cx.identb], cx.identb[:], cx.identf[:])


def load_masks(cx, din, pre, nlev=7, es_tmp=None):
    mk = cx.mk
    m = Ctx()
    m.nlev = nlev
    m.m1 = mk.tile([128, 128], F32, pre + 'm1')
    m.m2 = mk.tile([128, 128], F32, pre + 'm2')
    m.nm_incl4 = mk.tile([128, 4, 128], BF16, pre + 'nmi4')
    m.strict01 = mk.tile([128, 128], F32, pre + 's01')
    m.ML = mk.tile([128, nlev, 2, 128], BF16, pre + 'ML')
    tmp = mk.tile([128, 2, 128], F32, pre + 'nmtmp', es=es_tmp)
    mlf = mk.tile([128, nlev, 2, 128], F32, pre + 'mlf', es=es_tmp)
    _load(mk, m.m1, m.m1[:], din[pre + 'm1'])
    _load(mk, m.m2, m.m2[:], din[pre + 'm2'])
    _load(mk, tmp, tmp[:, 0, :], din[pre + 'nm_strict'])
    _load(mk, tmp, tmp[:, 1, :], din[pre + 'nm_incl'])
    _copy(mk, 'pool', [tmp], [m.nm_incl4], m.nm_incl4[:], tmp[:, 1:2, :].broadcast_to([128, 4, 128]))
    _ts(mk, 'pool', [tmp], [m.strict01], m.strict01[:], tmp[:, 0, :], 0.0, ALU.is_equal)
    _load(mk, mlf, mlf[:], din[pre + 'ml'])
    _copy(mk, 'pool', [mlf], [m.ML], m.ML[:], mlf[:])
    return m


def barrier(mk):
    owners = [o for o in mk.all_owners if o.dcount > 0] if hasattr(mk, 'all_owners') else []
    for E in mk.names:
        for e in mk.names:
            if e != E and e != 'sp' and mk.cnt[e] > 0:
                mk._wait_tok(E, ('e', e, mk.cnt[e]))
        for o in owners:
            mk._wait_tok(E, ('d', o))


def alloc_stage(cx, es):
    mk = cx.mk
    st = Ctx()
    st.f = [mk.tile([128, 4096], F32, f"stg{i}", es=es) for i in range(2)]
    st.b = [mk.tile([128, 4096], BF16, f"stb{i}", es=es) for i in range(2)]
    st.bias = [mk.tile([128, 512], F32, f"adab{i}", es=es) for i in range(2)]
    st.n = 0
    return st


def ada_mod(cx, cT, w_ap, b_ap, ncols, outs, st):
    mk = cx.mk
    wv = w_ap.rearrange("(kc p) n -> p kc n", p=128)
    for ci in range(ncols // 512):
        st.n += 1
        wt, b = st.f[st.n % 2], st.bias[st.n % 2]
        w = wt[:].rearrange("p (k c) -> p k c", k=8)
        ps = cx.ps[ci % 2]
        _load(mk, wt, w, wv[:, :, ci * 512:(ci + 1) * 512])
        _load(mk, b, b[:], b_ap[:, ci * 512:(ci + 1) * 512])
        for kc in range(8):
            _mm(mk, [cT, wt], [ps], ps[:], cT[:, kc, :], w[:, kc, :], start=(kc == 0), stop=(kc == 7))
        o = outs[ci // 2]
        _tt(mk, 'dve', [ps, b], [o], o[:, (ci % 2) * 512:(ci % 2 + 1) * 512], ps[:], b[:], ALU.add)


def prep_w_bf(cx, w_ap, nk, ncols, dst, st, to_dram=True):
    mk = cx.mk
    wv = w_ap.rearrange("(kc p) n -> p kc n", p=128)
    CH = 4096 // nk
    engs = ['pool', 'act']
    for ci, c0 in enumerate(range(0, ncols, CH)):
        w = min(CH, ncols - c0)
        st.n += 1
        sf, sb = st.f[st.n % 2], st.b[st.n % 2]
        s = sf[:].rearrange("p (k c) -> p k c", k=nk)
        b = sb[:].rearrange("p (k c) -> p k c", k=nk)
        _load(mk, sf, s[:, :, :w], wv[:, :, c0:c0 + w])
        _copy(mk, engs[ci % 2], [sf], [sb], b[:, :, :w], s[:, :, :w])
        _store(mk, sb, dst[:, :, c0:c0 + w], b[:, :, :w], writes=[dst], is_output=False)


def rms_mod_T(cx, x, nblk, A, Sh, hT, es_tiles):
    mk = cx.mk
    junk, ss, rstd, hb = es_tiles['junk'], es_tiles['ss'], es_tiles['rstd'], es_tiles['hb']
    for blk in range(nblk):
        _act(mk, [x], [junk, ss], junk[:], x[:, blk, :], AF.Square, accum_out=ss[:, blk:blk + 1])
    _act(mk, [ss, cx.epsc], [rstd], rstd[:, :nblk], ss[:, :nblk], AF.Sqrt, scale=1.0 / D, bias=cx.epsc[:, 0:1])
    mk.op('dve', [rstd], [rstd], lambda e: e.reciprocal(out=rstd[:, :nblk], in_=rstd[:, :nblk]))
    for blk in range(nblk):
        t = es_tiles['t32']
        _stt(mk, [x, rstd, A], [t], t[:], x[:, blk, :], rstd[:, blk:blk + 1], A[:], ALU.mult, ALU.mult)
        _tt(mk, 'pool', [t, Sh], [hb], hb[:], t[:], Sh[:], ALU.add)
        pt = cx.pst[blk % 2]
        for kc in range(8):
            _tr(mk, [hb, cx.identb], [pt], pt[:, kc * 128:(kc + 1) * 128], hb[:, kc * 128:(kc + 1) * 128], cx.identb[:])
        _copy(mk, 'act' if blk % 2 == 0 else 'dve', [pt], [hT],
              hT[:, :, blk * 128:(blk + 1) * 128], pt[:].rearrange("p (k t) -> p k t", k=8))


def nextps(cx):
    cx.psi = (cx.psi + 1) % len(cx.ps)
    return cx.ps[cx.psi]


def nextps4(cx):
    cx.psi4 = (getattr(cx, 'psi4', 0) + 1) % 4
    return cx.ps[cx.psi4]


def nextpst(cx):
    cx.psti = (cx.psti + 1) % len(cx.pst)
    return cx.pst[cx.psti]


def alloc_l0(cx, es, T):
    mk = cx.mk
    L = Ctx()
    nb = T // 128
    L.T, L.nb = T, nb
    L.x = [mk.tile([128, nb, D], F32, 'x', es=es)] * 2
    L.hT = mk.tile([128, 8, T], BF16, 'hT', es=es)
    L.wbuf = [mk.tile([128, 8, 256], BF16, f'wbuf{i}', es=es) for i in range(2)]
    L.raw = [mk.tile([128, 3 + T], BF16, f'raw{i}', es=es) for i in range(2)]
    L.hist = mk.tile([128, 32, 3], BF16, 'hist', es=es)
    L.hist32 = mk.tile([128, 32, 3], F32, 'hist32', es=es)
    L.diag = [mk.tile([128, 4, 128], BF16, f'diag{i}', es=es) for i in range(2)]
    L.qkT = mk.tile([128, 16, T], BF16, 'qkT', es=es)
    L.vT = mk.tile([128, 16, T], BF16, 'vT', es=es)
    L.zg = mk.tile([128, nb, 2048], BF16, 'zg', es=es)
    L.ba = mk.tile([128, nb, 32], F32, 'ba', es=es)
    L.sq = [mk.tile([128, T], BF16, 'sq', es=es)] * 2
    L.r32 = [mk.tile([128, T], F32, 'r32', es=es)] * 2
    L.h2 = mk.tile([128, 3, 32], F32, 'h2', es=es)
    L.h3 = mk.tile([128, 128], F32, 'h3', es=es)
    L.sm = {k: mk.tile([128, 16], F32, 'sm_' + k, es=es) for k in
            ['beta', 'xa', 'ax', 'e1', 'l1', 'sp', 'g', 'gc', 'ngc', 'eg', 's1', 's2', 'egl', 'ss', 'rstd']}
    L.rhsB = mk.tile([128, 16, 128], F32, 'rhsB', es=es)
    L.rhsBeta = mk.tile([128, 16, 128], BF16, 'rhsBeta', es=es)
    L.egB = mk.tile([128, 16, 128], BF16, 'egB', es=es)
    L.QgT = L.egB
    L.DmQ = mk.tile([128, 16, 128], BF16, 'DmQ', es=es)
    L.Ms = mk.tile([128, 16, 128], BF16, 'Ms', es=es)
    L.X = L.Ms
    L.QKT = mk.tile([128, 16, 128], BF16, 'QKT', es=es)
    L.inv = [mk.tile([128, 16, 2, 128], BF16, 'AA', es=es)]
    L.DD = [mk.tile([128, 2, 2, 128], BF16, f'DD{i}', es=es) for i in range(8)]
    L.ZZ = [mk.tile([128, 2, 2, 128], BF16, f'ZZ{i}', es=es) for i in range(8)]
    L.AOs = [mk.tile([128, 16, 2, 128], BF16, f'AO{i}', es=es) for i in range(2)]
    L.AO = L.AOs[0]
    L.Kbg = mk.tile([128, 16, 128], BF16, 'Kbg', es=es)
    L.Kd = mk.tile([128, 16, 128], BF16, 'Kd', es=es)
    L.Vb = mk.tile([128, 16, 128], BF16, 'Vb', es=es)
    L.nWT = L.DmQ
    L.vn = L.Ms
    L.o32 = L.rhsB
    L.og = L.rhsBeta
    L.ogT = L.Kbg
    L.x1 = [None, None]
    L.t32 = mk.tile([128, D], F32, 't32', es=es)
    L.hb = mk.tile([128, D], BF16, 'hb', es=es)
    L.junk = L.hb
    L.nss = mk.tile([128, 4], F32, 'nss', es=es)
    L.nrs = mk.tile([128, 4], F32, 'nrs', es=es)
    return L


def gdn_gates(cx, L, blk, M, vm=None):
    mk = cx.mk
    s = L.sm
    bin_ = L.ba[:, blk, 0:16]
    ain = L.ba[:, blk, 16:32]
    _act(mk, [L.ba], [s['beta']], s['beta'][:], bin_, AF.Sigmoid)
    _tt(mk, 'dve', [L.ba, cx.dtb], [s['xa']], s['xa'][:], ain, cx.dtb[:], ALU.add)
    _act(mk, [s['xa']], [s['ax']], s['ax'][:], s['xa'][:], AF.Abs)
    _act(mk, [s['ax']], [s['e1']], s['e1'][:], s['ax'][:], AF.Exp, scale=-1.0)
    _act(mk, [s['e1'], cx.onec], [s['l1']], s['l1'][:], s['e1'][:], AF.Ln, bias=cx.onec[:, 0:1])
    _stt(mk, [s['xa'], s['l1']], [s['sp']], s['sp'][:], s['xa'][:], 0.0, s['l1'][:], ALU.max, ALU.add)
    _tt(mk, 'dve', [s['sp'], cx.nea], [s['g']], s['g'][:], s['sp'][:], cx.nea[:], ALU.mult)
    if vm is not None:
        _ts(mk, 'dve', [s['beta'], vm[0]], [s['beta']], s['beta'][:], s['beta'][:], vm[1], ALU.mult)
        _ts(mk, 'dve', [s['g'], vm[0]], [s['g']], s['g'][:], s['g'][:], vm[1], ALU.mult)
    ps = nextps(cx)
    _mm(mk, [M.m1, s['g']], [ps], ps[:, 0:16], M.m1[:], s['g'][:])
    _mm(mk, [M.m2, s['g']], [ps], ps[:, 16:32], M.m2[:], s['g'][:])
    _mm(mk, [cx.onesf, s['g']], [ps], ps[:, 32:48], cx.onesf[:], s['g'][:])
    _copy(mk, 'dve', [ps], [s['gc']], s['gc'][:], ps[:, 0:16])
    _act(mk, [ps], [s['ngc']], s['ngc'][:], ps[:, 0:16], AF.Copy, scale=-1.0)
    _act(mk, [ps], [s['eg']], s['eg'][:], ps[:, 0:16], AF.Exp)
    _act(mk, [ps], [s['s2']], s['s2'][:], ps[:, 16:32], AF.Exp)
    _act(mk, [ps], [s['egl']], s['egl'][:], ps[:, 32:48], AF.Exp)
    _tt(mk, 'dve', [s['beta'], s['eg']], [s['s1']], s['s1'][:], s['beta'][:], s['eg'][:], ALU.mult)
    _tt(mk, 'dve', [M.m1, s['g']], [L.rhsB], L.rhsB[:],
        M.m1[:].unsqueeze(1).broadcast_to([128, 16, 128]),
        s['g'][:].unsqueeze(2).broadcast_to([128, 16, 128]), ALU.mult)
    _tt(mk, 'pool', [cx.identf, s['beta']], [L.rhsBeta], L.rhsBeta[:],
        cx.identf[:].unsqueeze(1).broadcast_to([128, 16, 128]),
        s['beta'][:].unsqueeze(2).broadcast_to([128, 16, 128]), ALU.mult)


def gdn_block(cx, L, blk, M, S32, Sb, vm=None):
    mk = cx.mk
    s = L.sm
    b0 = blk * 128
    gdn_gates(cx, L, blk, M, vm)
    for hg in range(4):
        hs = slice(hg * 4, hg * 4 + 4)
        pe_ = nextps(cx)
        _mm(mk, [cx.onesf, L.rhsB], [pe_], pe_[:], cx.onesf[:], L.rhsB[:, hs, :].rearrange("p h i -> p (h i)"))
        _act(mk, [pe_], [L.egB], L.egB[:, hs, :].rearrange("p h i -> p (h i)"), pe_[:], AF.Exp)
        pd = nextps(cx)
        _mm(mk, [cx.onesf, L.rhsB], [pd], pd[:], cx.onesf[:], L.rhsB[:, hs, :].rearrange("p h i -> p (h i)"), start=True, stop=False)
        _mm(mk, [cx.identb, M.nm_incl4], [pd], pd[:], cx.identb[:], M.nm_incl4[:].rearrange("p h i -> p (h i)"), start=False, stop=True)
        for hh in range(4):
            h = hg * 4 + hh
            _act(mk, [pd, s['ngc']], [L.DmQ], L.DmQ[:, h, :], pd[:, hh * 128:(hh + 1) * 128], AF.Exp, bias=s['ngc'][:, h:h + 1])
        pb = nextps(cx)
        _mm(mk, [cx.onesb, L.rhsBeta], [pb], pb[:], cx.onesb[:], L.rhsBeta[:, hs, :].rearrange("p h i -> p (h i)"))
        _tt(mk, 'dve', [pb, M.strict01], [L.Ms], L.Ms[:, hs, :], pb[:].rearrange("p (h i) -> p h i", h=4),
            M.strict01[:].unsqueeze(1).broadcast_to([128, 4, 128]), ALU.mult)
    _tt(mk, 'pool', [L.DmQ, L.Ms], [L.X], L.X[:], L.DmQ[:], L.Ms[:], ALU.mult)
    inv0 = L.inv[0]
    for half in range(2):
        pk = nextps(cx)
        pq = nextps(cx)
        for j in range(4):
            hq = half * 4 + j
            kT = L.qkT[:, 8 + hq, b0:b0 + 128]
            qT = L.qkT[:, hq, b0:b0 + 128]
            _mm(mk, [L.qkT], [pk], pk[:, j * 128:(j + 1) * 128], kT, kT)
            _mm(mk, [L.qkT], [pq], pq[:, j * 128:(j + 1) * 128], kT, qT)
        hs = slice(half * 8, half * 8 + 8)
        _tt(mk, 'dve', [pk, L.X], [inv0], inv0[:, hs, 0, :].rearrange("p (a b) i -> p a b i", b=2),
            pk[:].rearrange("p (a i) -> p a i", a=4).unsqueeze(2).broadcast_to([128, 4, 2, 128]),
            L.X[:, hs, :].rearrange("p (a b) i -> p a b i", b=2), ALU.mult)
        _tt(mk, 'dve', [pq, L.DmQ], [L.QKT], L.QKT[:, hs, :].rearrange("p (a b) i -> p a b i", b=2),
            pq[:].rearrange("p (a i) -> p a i", a=4).unsqueeze(2).broadcast_to([128, 4, 2, 128]),
            L.DmQ[:, hs, :].rearrange("p (a b) i -> p a b i", b=2), ALU.mult)
    for half in range(2):
        pt = nextpst(cx)
        for j in range(8):
            h = half * 8 + j
            _tr(mk, [inv0, cx.identb], [pt], pt[:, j * 128:(j + 1) * 128], inv0[:, h, 0, :], cx.identb[:])
        _copy(mk, 'act', [pt], [inv0], inv0[:, half * 8:half * 8 + 8, 1, :], pt[:].rearrange("p (h i) -> p h i", h=8))
    _tt(mk, 'pool', [L.qkT, L.egB], [L.QgT], L.QgT[:].rearrange("p (a b) i -> p a b i", b=2),
        L.qkT[:, 0:8, b0:b0 + 128].unsqueeze(2).broadcast_to([128, 8, 2, 128]),
        L.egB[:].rearrange("p (a b) i -> p a b i", b=2), ALU.mult)
    AA = inv0[:]
    ML = M.ML
    for sl in range(2):
        _tt(mk, 'pool', [inv0, ML], [L.AO], L.AO[:, :, sl, :], inv0[:, :, sl, :], ML[:, 0:1, sl, :].broadcast_to([128, 16, 128]), ALU.mult)
    for hp in range(8):
        _tt(mk, 'pool', [L.AO, cx.identb], [L.DD[hp]], L.DD[hp][:], L.AO[:, hp * 2:hp * 2 + 2, :, :],
            cx.identb[:].unsqueeze(1).unsqueeze(1).broadcast_to([128, 2, 2, 128]), ALU.add)
    nlev = min(M.nlev, INV_LEVELS)
    for lv in range(1, nlev):
        last = (lv == nlev - 1)
        AOl = L.AOs[lv % 2]
        for sl in range(2):
            _tt(mk, 'pool', [inv0, ML], [AOl], AOl[:, :, sl, :], inv0[:, :, sl, :], ML[:, lv:lv + 1, sl, :].broadcast_to([128, 16, 128]), ALU.mult)
        for hp in range(8):
            ps = nextps(cx)
            DD, ZZ = L.DD[hp], L.ZZ[hp]
            for hh in range(2):
                h = hp * 2 + hh
                c0 = hh * 256
                if not last:
                    _mm(mk, [AOl, DD], [ps], ps[:, c0:c0 + 128], AOl[:, h, 0, :], DD[:, hh, 1, :])
                _mm(mk, [AOl, DD], [ps], ps[:, c0 + 128:c0 + 256], AOl[:, h, 1, :], DD[:, hh, 0, :])
            if not last:
                _copy(mk, 'act', [ps], [ZZ], ZZ[:].rearrange("p h s i -> p (h s i)"), ps[:])
            else:
                _copy(mk, 'act', [ps], [ZZ], ZZ[:, :, 1, :], ps[:].rearrange("p (h s i) -> p h s i", h=2, s=2)[:, :, 1, :])
        for hp in range(8):
            ps = nextps(cx)
            DD, ZZ = L.DD[hp], L.ZZ[hp]
            for hh in range(2):
                c0 = hh * 256
                _mm(mk, [ZZ, DD], [ps], ps[:, c0:c0 + 128], DD[:, hh, 1, :], ZZ[:, hh, 1, :])
                if not last:
                    _mm(mk, [ZZ, DD], [ps], ps[:, c0 + 128:c0 + 256], DD[:, hh, 0, :], ZZ[:, hh, 0, :])
            if not last:
                _tt(mk, 'dve', [ps, DD], [DD], DD[:].rearrange("p h s i -> p (h s i)"), ps[:],
                    DD[:].rearrange("p h s i -> p (h s i)"), ALU.add)
            else:
                _tt(mk, 'dve', [ps, DD], [DD], DD[:, :, 0, :], ps[:].rearrange("p (h s i) -> p h s i", h=2, s=2)[:, :, 0, :],
                    DD[:, :, 0, :], ALU.add)

    class _TT:
        def __getitem__(self_, key):
            return None
    TTt = lambda h: L.DD[h // 2]
    TTa = lambda h: L.DD[h // 2][:, h % 2, 0, :]
    pt = nextpst(cx)
    for hq in range(8):
        _tr(mk, [L.qkT, cx.identb], [pt], pt[:, hq * 128:(hq + 1) * 128], L.qkT[:, 8 + hq, b0:b0 + 128], cx.identb[:])
    kview = pt[:].rearrange("p (a i) -> p a i", a=8).unsqueeze(2).broadcast_to([128, 8, 2, 128])
    _tt(mk, 'dve', [pt, s['s1']], [L.Kbg], L.Kbg[:].rearrange("p (a b) i -> p a b i", b=2), kview,
        s['s1'][:].rearrange("p (a b) -> p a b", b=2).unsqueeze(3).broadcast_to([128, 8, 2, 128]), ALU.mult)
    _tt(mk, 'dve', [pt, s['s2']], [L.Kd], L.Kd[:].rearrange("p (a b) i -> p a b i", b=2), kview,
        s['s2'][:].rearrange("p (a b) -> p a b", b=2).unsqueeze(3).broadcast_to([128, 8, 2, 128]), ALU.mult)
    for half in range(2):
        pt = nextpst(cx)
        for j in range(8):
            h = half * 8 + j
            _tr(mk, [L.vT, cx.identb], [pt], pt[:, j * 128:(j + 1) * 128], L.vT[:, h, b0:b0 + 128], cx.identb[:])
        _tt(mk, 'dve', [pt, s['beta']], [L.Vb], L.Vb[:, half * 8:half * 8 + 8, :], pt[:].rearrange("p (a i) -> p a i", a=8),
            s['beta'][:, half * 8:half * 8 + 8].unsqueeze(2).broadcast_to([128, 8, 128]), ALU.mult)
    for hg in range(4):
        ps = nextps(cx)
        for hh in range(4):
            h = hg * 4 + hh
            _mm(mk, [L.Kbg, TTt(h)], [ps], ps[:, hh * 128:(hh + 1) * 128], L.Kbg[:, h, :], TTa(h))
        _act(mk, [ps], [L.nWT], L.nWT[:, hg * 4:hg * 4 + 4, :].rearrange("p h i -> p (h i)"), ps[:], AF.Copy, scale=-1.0)
    for hg in range(4):
        ps = nextps(cx)
        for hh in range(4):
            h = hg * 4 + hh
            _mm(mk, [TTt(h), L.Vb], [ps], ps[:, hh * 128:(hh + 1) * 128], TTa(h), L.Vb[:, h, :], start=True, stop=False)
            _mm(mk, [L.nWT, Sb], [ps], ps[:, hh * 128:(hh + 1) * 128], L.nWT[:, h, :], Sb[:, h, :], start=False, stop=True)
        _copy(mk, 'dve' if hg % 2 else 'act', [ps], [L.vn], L.vn[:, hg * 4:hg * 4 + 4, :].rearrange("p h i -> p (h i)"), ps[:])
    for hg in range(4):
        ps = nextps(cx)
        for hh in range(4):
            h = hg * 4 + hh
            _mm(mk, [L.QgT, Sb], [ps], ps[:, hh * 128:(hh + 1) * 128], L.QgT[:, h, :], Sb[:, h, :], start=True, stop=False)
            _mm(mk, [L.QKT, L.vn], [ps], ps[:, hh * 128:(hh + 1) * 128], L.QKT[:, h, :], L.vn[:, h, :], start=False, stop=True)
        _copy(mk, 'act', [ps], [L.o32], L.o32[:, hg * 4:hg * 4 + 4, :].rearrange("p h i -> p (h i)"), ps[:])
    for hg in range(4):
        ps = nextps(cx)
        for hh in range(4):
            h = hg * 4 + hh
            _mm(mk, [L.Kd, L.vn], [ps], ps[:, hh * 128:(hh + 1) * 128], L.Kd[:, h, :], L.vn[:, h, :])
        for hh in range(4):
            h = hg * 4 + hh
            _stt(mk, [S32, s['egl'], ps], [S32], S32[:, h, :], S32[:, h, :], s['egl'][:, h:h + 1],
                 ps[:, hh * 128:(hh + 1) * 128], ALU.mult, ALU.add)
    _copy(mk, 'pool', [S32], [Sb], Sb[:], S32[:])


def gdn_out(cx, L, blk, x_tile, x1, gate, woa):
    mk = cx.mk
    s = L.sm
    osq = L.AO[:, :, 0, :]
    _tt(mk, 'pool', [L.o32], [L.AO], osq, L.o32[:], L.o32[:], ALU.mult)
    mk.op('dve', [L.AO], [s['ss']], lambda e: e.tensor_reduce(out=s['ss'][:], in_=osq, axis=AX.X, op=ALU.add))
    _act(mk, [s['ss'], cx.epsc], [s['rstd']], s['rstd'][:], s['ss'][:], AF.Sqrt, scale=1.0 / 128, bias=cx.epsc[:, 0:1])
    mk.op('dve', [s['rstd']], [s['rstd']], lambda e: e.reciprocal(out=s['rstd'][:], in_=s['rstd'][:]))
    _tt(mk, 'dve', [L.o32, s['rstd']], [L.o32], L.o32[:], L.o32[:], s['rstd'][:].unsqueeze(2).broadcast_to([128, 16, 128]), ALU.mult)
    _tt(mk, 'pool', [L.o32, L.zg], [L.og], L.og[:].rearrange("p h i -> p (h i)"), L.o32[:].rearrange("p h i -> p (h i)"), L.zg[:, blk, :], ALU.mult)
    for half in range(2):
        pt = nextpst(cx)
        for j in range(8):
            h = half * 8 + j
            _tr(mk, [L.og, cx.identb], [pt], pt[:, j * 128:(j + 1) * 128], L.og[:, h, :], cx.identb[:])
        _copy(mk, 'act' if half else 'dve', [pt], [L.ogT], L.ogT[:, half * 8:half * 8 + 8, :], pt[:].rearrange("p (h i) -> p h i", h=8))
    for half in range(2):
        ps = nextps(cx)
        for h in range(16):
            _mm(mk, [L.ogT, woa], [ps], ps[:], L.ogT[:, h, :], woa[:, h, half * 512:(half + 1) * 512], start=(h == 0), stop=(h == 15))
        cs = slice(half * 512, (half + 1) * 512)
        _tt(mk, 'dve', [ps, gate], [L.t32], L.t32[:, cs], ps[:], gate[:, cs], ALU.mult)
        _tt(mk, 'pool', [L.t32, x_tile], [L.t32], L.t32[:, cs], L.t32[:, cs], x_tile[:, blk, cs], ALU.add)


def conv_state_out(cx, L, es, out_ap, name):
    mk = cx.mk
    h2 = L.h2
    _copy(mk, 'dve', [L.hist32], [h2], h2[:], L.hist32[:].rearrange("p f j -> p j f"))
    pc = nextps(cx)
    mk.op('pe', [h2, cx.identf], [pc], lambda e: e.transpose(out=pc[0:96, 0:128], in_=h2[:].rearrange("p j f -> p (j f)"), identity=cx.identf[:]))
    h3 = L.h3
    _copy(mk, 'dve', [pc], [h3], h3[0:96, :], pc[0:96, 0:128])
    _store(mk, h3, out_ap, h3[0:96, :])


def l0_project(cx, L, gi, last, T=None, cpos=None, hpos=0):
    mk = cx.mk
    T = T or L.T
    nb = T // 128
    cpos = T if cpos is None else cpos
    for ct in range(25):
        wb = L.wbuf[ct % 2]
        w = 256 if ct < 24 else 32
        _load(mk, wb, wb[:, :, :w], cx.wina_bf[:, :, ct * 256:ct * 256 + w], reads=[cx.wina_bf])
        if ct < 16:
            for j4 in range(2):
                f = ct * 2 + j4
                pa = nextps(cx)
                for kc in range(8):
                    _mm(mk, [wb, L.hT], [pa], pa[:, :T], wb[:, kc, j4 * 128:(j4 + 1) * 128], L.hT[:, kc, :T],
                        start=(kc == 0), stop=(kc == 7))
                raw = L.raw[f % 2]
                _copy(mk, 'act', [pa], [raw], raw[:, 3:3 + T], pa[:, :T])
                _copy(mk, 'pool', [L.hist], [raw], raw[:, hpos:hpos + 3], L.hist[:, f, :])
                if last:
                    _copy(mk, 'act', [pa], [L.hist32], L.hist32[:, f, :], pa[:, cpos - 3:cpos])
                dg = L.diag[f % 2]
                for j in range(4):
                    _ts(mk, 'pool', [cx.identb, cx.cw], [dg], dg[:, j, :], cx.identb[:], cx.cw[:, f, j:j + 1], ALU.mult)
                pb = nextps(cx)
                for j in range(4):
                    _mm(mk, [dg, raw], [pb], pb[:, :T], dg[:, j, :], raw[:, j:j + T], start=(j == 0), stop=(j == 3))
                dt_, dst = (L.qkT, L.qkT[:, f, :T]) if f < 16 else (L.vT, L.vT[:, f - 16, :T])
                _act(mk, [pb], [dt_], dst, pb[:, :T], AF.Silu)
                _copy(mk, 'pool', [raw], [L.hist], L.hist[:, f, :], raw[:, cpos:cpos + 3])
        elif ct < 24:
            for blk in range(nb):
                pz = nextps(cx)
                for kc in range(8):
                    _mm(mk, [wb, L.hT], [pz], pz[:, :256], L.hT[:, kc, blk * 128:(blk + 1) * 128], wb[:, kc, :],
                        start=(kc == 0), stop=(kc == 7))
                _act(mk, [pz], [L.zg], L.zg[:, blk, (ct - 16) * 256:(ct - 15) * 256], pz[:, :256], AF.Silu)
        else:
            for blk in range(nb):
                pz = nextps(cx)
                for kc in range(8):
                    _mm(mk, [wb, L.hT], [pz], pz[:, 0:32], L.hT[:, kc, blk * 128:(blk + 1) * 128], wb[:, kc, 0:32],
                        start=(kc == 0), stop=(kc == 7))
                _copy(mk, 'dve', [pz], [L.ba], L.ba[:, blk, :], pz[:, 0:32])
    for blk in range(nb):
        _tt(mk, 'pool', [L.zg, cx.ogain], [L.zg], L.zg[:, blk, :].rearrange("p (h i) -> p h i", h=16),
            L.zg[:, blk, :].rearrange("p (h i) -> p h i", h=16),
            cx.ogain[:].unsqueeze(1).broadcast_to([128, 16, 128]), ALU.mult)
    for f in range(16):
        sq, r32 = L.sq[f % 2], L.r32[f % 2]
        _tt(mk, 'dve', [L.qkT], [sq], sq[:, :T], L.qkT[:, f, :T], L.qkT[:, f, :T], ALU.mult)
        ps = nextps(cx)
        _mm(mk, [cx.onesb, sq], [ps], ps[:, :T], cx.onesb[:], sq[:, :T])
        if f < 8:
            _act(mk, [ps, cx.eps128], [r32], r32[:, :T], ps[:, :T], AF.Sqrt, scale=128.0, bias=cx.eps128[:, 0:1])
        else:
            _act(mk, [ps, cx.epsc], [r32], r32[:, :T], ps[:, :T], AF.Sqrt, bias=cx.epsc[:, 0:1])
        mk.op('dve', [r32], [r32], lambda e, r32=r32: e.reciprocal(out=r32[:, :T], in_=r32[:, :T]))
        _tt(mk, 'pool', [L.qkT, r32], [L.qkT], L.qkT[:, f, :T], L.qkT[:, f, :T], r32[:, :T], ALU.mult)


def alloc_l1(cx, es):
    mk = cx.mk
    B = Ctx()
    B.x = mk.tile([128, 1, D], F32, 'bx', es=es)
    B.hT = mk.tile([128, 8, 128], BF16, 'bhT', es=es)
    B.t32 = mk.tile([128, D], F32, 'bt32', es=es)
    B.hb = mk.tile([128, D], BF16, 'bhb', es=es)
    B.nss = mk.tile([128, 4], F32, 'bnss', es=es)
    B.nrs = mk.tile([128, 4], F32, 'bnrs', es=es)
    B.sq = mk.tile([128, D], F32, 'bsq', es=es)
    B.ss = mk.tile([128, 16], F32, 'bss', es=es)
    B.rs = mk.tile([128, 16], F32, 'brs', es=es)
    B.kn = mk.tile([128, 16, 64], F32, 'bkn', es=es)
    B.ko = mk.tile([128, 16, 64], F32, 'bko', es=es)
    B.rt = [mk.tile([128, 16, 8], F32, f'brt{i}', es=es) for i in range(4)]
    B.cs = mk.tile([128, 2, 8], F32, 'bcs', es=es)
    B.kb = mk.tile([128, 8, 128], BF16, 'bkb', es=es)
    B.kT = mk.tile([128, 8, 128], BF16, 'bkT', es=es)
    B.v32 = mk.tile([128, 8, 128], F32, 'bv32', es=es)
    B.vb = mk.tile([128, 8, VW], BF16, 'bvb', es=es)
    B.sc = {'junk': B.hb, 'ss': B.nss, 'rstd': B.nrs, 'hb': B.hb, 't32': B.t32}
    return B


def headnorm_rope(cx, B, src_list, nh, gain, cs_ap):
    mk = cx.mk
    ng = 2 * nh
    _load(mk, B.cs, B.cs[:], cs_ap)
    for i, ps in enumerate(src_list):
        w = min(512, nh * 128 - i * 512)
        _act(mk, [ps], [B.sq], B.sq[:, i * 512:i * 512 + w], ps[:, :w], AF.Square)
    mk.op('dve', [B.sq], [B.ss], lambda e: e.tensor_reduce(
        out=B.ss[:, :ng], in_=B.sq[:, :ng * 64].rearrange("p (g d) -> p g d", d=64), axis=AX.X, op=ALU.add))
    _act(mk, [B.ss, cx.epsc], [B.rs], B.rs[:, :ng], B.ss[:, :ng], AF.Sqrt, scale=1.0 / 64, bias=cx.epsc[:, 0:1])
    mk.op('dve', [B.rs], [B.rs], lambda e: e.reciprocal(out=B.rs[:, :ng], in_=B.rs[:, :ng]))
    for i, ps in enumerate(src_list):
        w = min(512, nh * 128 - i * 512)
        g0, gn = i * 8, w // 64
        _tt(mk, 'dve', [ps, B.rs], [B.kn], B.kn[:, g0:g0 + gn, :], ps[:, :w].rearrange("p (g d) -> p g d", d=64),
            B.rs[:, g0:g0 + gn].unsqueeze(2).broadcast_to([128, gn, 64]), ALU.mult)
    _tt(mk, 'pool', [B.kn, gain], [B.ko], B.ko[:, :ng, :], B.kn[:, :ng, :], gain[:].unsqueeze(1).broadcast_to([128, ng, 64]), ALU.mult)
    cosb = B.cs[:, 0:1, :].broadcast_to([128, ng, 8])
    sinb = B.cs[:, 1:2, :].broadcast_to([128, ng, 8])
    x1, x2 = B.ko[:, :ng, 0:8], B.ko[:, :ng, 8:16]
    t = B.rt
    _tt(mk, 'pool', [B.ko, B.cs], [t[0]], t[0][:, :ng, :], x1, cosb, ALU.mult)
    _tt(mk, 'pool', [B.ko, B.cs], [t[1]], t[1][:, :ng, :], x2, sinb, ALU.mult)
    _tt(mk, 'pool', [B.ko, B.cs], [t[2]], t[2][:, :ng, :], x2, cosb, ALU.mult)
    _tt(mk, 'pool', [B.ko, B.cs], [t[3]], t[3][:, :ng, :], x1, sinb, ALU.mult)
    _tt(mk, 'pool', [t[0], t[1]], [B.ko], x1, t[0][:, :ng, :], t[1][:, :ng, :], ALU.subtract)
    _tt(mk, 'pool', [t[2], t[3]], [B.ko], x2, t[2][:, :ng, :], t[3][:, :ng, :], ALU.add)


def kv_block(cx, B, x_ap, Akv, Shkv, wkv, nh, kgain, cs_ap, k_out_ap, v_out_ap, KTs_ap, Vs_ap, KTs, Vs, x_reads=()):
    mk = cx.mk
    _load(mk, B.x, B.x[:, 0, :], x_ap, reads=x_reads)
    rms_mod_T(cx, B.x, 1, Akv, Shkv, B.hT, B.sc)
    nk = nh * 128
    kps = []
    for i in range((nk + 511) // 512):
        ps = nextps(cx)
        w = min(512, nk - i * 512)
        for kc in range(8):
            _mm(mk, [B.hT, wkv], [ps], ps[:, :w], B.hT[:, kc, :], wkv[:, kc, i * 512:i * 512 + w], start=(kc == 0), stop=(kc == 7))
        kps.append(ps)
    if KVSTEP < 2:
        return
    headnorm_rope(cx, B, kps, nh, kgain, cs_ap)
    if KVSTEP < 3:
        return
    _store(mk, B.ko, k_out_ap, B.ko[:, :2 * nh, :].rearrange("p g d -> p (g d)"))
    _copy(mk, 'act', [B.ko], [B.kb], B.kb[:, :nh, :], B.ko[:, :2 * nh, :].rearrange("p (h c) d -> p h (c d)", c=2))
    pt = nextpst(cx)
    for h in range(nh):
        _tr(mk, [B.kb, cx.identb], [pt], pt[:, h * 128:(h + 1) * 128], B.kb[:, h, :], cx.identb[:])
    _copy(mk, 'dve', [pt], [B.kT], B.kT[:, :nh, :], pt[:, :nh * 128].rearrange("p (h t) -> p h t", h=nh))
    if KVSTEP < 4:
        return
    for h in range(nh):
        _store(mk, B.kT, KTs_ap[h], B.kT[:, h, :], writes=[KTs], is_output=False)
    if KVSTEP < 5:
        return
    for i in range((nk + 511) // 512):
        ps = nextps(cx)
        w = min(512, nk - i * 512)
        for kc in range(8):
            _mm(mk, [B.hT, wkv], [ps], ps[:, :w], B.hT[:, kc, :], wkv[:, kc, nk + i * 512:nk + i * 512 + w], start=(kc == 0), stop=(kc == 7))
        hh = w // 128
        if VVAR != 1:
            _copy(mk, 'act' if VVAR == 0 else 'dve', [ps], [B.v32], B.v32[:, i * 4:i * 4 + hh, :].rearrange("p h e -> p (h e)"), ps[:, :w])
        if VVAR != 2:
            _copy(mk, 'dve', [ps], [B.vb], B.vb[:, i * 4:i * 4 + hh, 0:128], ps[:, :w].rearrange("p (h e) -> p h e", e=128))
    if KVSTEP < 7:
        return
    _store(mk, B.v32, v_out_ap, B.v32[:, :nh, :].rearrange("p h e -> p (h e)"))
    if KVSTEP < 8:
        return
    for h in range(nh):
        _store(mk, B.vb, Vs_ap[h], B.vb[:, h, :], writes=[Vs], is_output=False)


def alloc_attn(cx, es, nh):
    mk = cx.mk
    A = Ctx()
    A.ktile = [mk.tile([128, 4096], BF16, f'akt{i}', es=es) for i in range(2)]
    A.vtile = [mk.tile([128, 32, VW], BF16, f'avt{i}', es=es) for i in range(2)]
    A.QT = mk.tile([128, nh, 2, 128], BF16, 'aQT', es=es)
    A.zb = mk.tile([128, nh * 128], BF16, 'azb', es=es)
    A.PT = [mk.tile([128, 2, 2, 128], BF16, f'aPT{i}', es=es) for i in range(2)]
    A.oatt = mk.tile([128, nh, 128], F32, 'aoatt', es=es)
    A.ot = mk.tile([128, 128], F32, 'aot', es=es)
    A.rr = mk.tile([128, 2], F32, 'arr', es=es)
    A.ss = mk.tile([128, 8], F32, 'ass', es=es)
    A.og = mk.tile([128, nh, 128], BF16, 'aog', es=es)
    A.ogT = mk.tile([128, nh, 128], BF16, 'aogT', es=es)
    A.y = mk.tile([128, D], F32, 'ay', es=es)
    return A


def l1_qz(cx, B, A, Ab, Shb, winb, nh, qgain, cs_ap, zgain):
    mk = cx.mk
    rms_mod_T(cx, B.x, 1, Ab, Shb, B.hT, B.sc)
    nk = nh * 128
    qps = []
    for i in range((nk + 511) // 512):
        ps = nextps(cx)
        w = min(512, nk - i * 512)
        for kc in range(8):
            _mm(mk, [B.hT, winb], [ps], ps[:, :w], B.hT[:, kc, :], winb[:, kc, i * 512:i * 512 + w], start=(kc == 0), stop=(kc == 7))
        qps.append(ps)
    headnorm_rope(cx, B, qps, nh, qgain, cs_ap)
    _copy(mk, 'act', [B.ko], [B.kb], B.kb[:, :nh, :], B.ko[:, :2 * nh, :].rearrange("p (h c) d -> p h (c d)", c=2))
    pt = nextpst(cx)
    for h in range(nh):
        _tr(mk, [B.kb, cx.identb], [pt], pt[:, h * 128:(h + 1) * 128], B.kb[:, h, :], cx.identb[:])
    for c in range(2):
        _ts(mk, 'dve', [pt, cx.cm], [A.QT], A.QT[:, :, c, :], pt[:, :nh * 128].rearrange("p (h t) -> p h t", h=nh), cx.cm[:, c:c + 1], ALU.mult)
    for i in range((nk + 511) // 512):
        ps = nextps(cx)
        w = min(512, nk - i * 512)
        for kc in range(8):
            _mm(mk, [B.hT, winb], [ps], ps[:, :w], B.hT[:, kc, :], winb[:, kc, nk + i * 512:nk + i * 512 + w], start=(kc == 0), stop=(kc == 7))
        _act(mk, [ps], [A.zb], A.zb[:, i * 512:i * 512 + w], ps[:, :w], AF.Silu)
    _tt(mk, 'pool', [A.zb, zgain], [A.zb], A.zb[:].rearrange("p (h e) -> p h e", e=128),
        A.zb[:].rearrange("p (h e) -> p h e", e=128), zgain[:].unsqueeze(1).broadcast_to([128, nh, 128]), ALU.mult)


def attn_head(cx, A, h, kt, vt, nkb, masks, nlam):
    mk = cx.mk
    acc = [cx.ps[4], cx.ps[5]]
    npair = (nkb + 1) // 2
    for kp in range(npair):
        ps = nextps4(cx)
        PT = A.PT[kp % 2]
        nj = min(2, nkb - kp * 2)
        for j in range(nj):
            kb = kp * 2 + j
            _mm(mk, [kt, A.QT], [ps], ps[:, j * 256:(j + 1) * 256], kt[:, kb * 128:(kb + 1) * 128],
                A.QT[:, h, :, :].rearrange("p c q -> p (c q)"))
        _act(mk, [ps, cx.m4c], [PT], PT[:, :nj, :, :].rearrange("p j c q -> p (j c q)"), ps[:, :nj * 256], AF.Exp,
             scale=0.125, bias=cx.m4c[:, 0:1])
        for j in range(nj):
            kb = kp * 2 + j
            if kb in masks:
                mt, map_ = masks[kb]
                for c in range(2):
                    _tt(mk, 'pool', [PT, mt], [PT], PT[:, j, c, :], PT[:, j, c, :], map_, ALU.mult)
        for j in range(nj):
            kb = kp * 2 + j
            for c in range(2):
                _mm(mk, [PT, vt], [acc[c]], acc[c][:, 0:129], PT[:, j, c, :], vt[:, kb, 0:129],
                    start=(kb == 0), stop=(kb == nkb - 1))
    mk.op('dve', [acc[0]], [A.rr], lambda e: e.reciprocal(out=A.rr[:, 0:1], in_=acc[0][:, 128:129]))
    mk.op('dve', [acc[1]], [A.rr], lambda e: e.reciprocal(out=A.rr[:, 1:2], in_=acc[1][:, 128:129]))
    _tt(mk, 'dve', [A.rr, nlam], [A.rr], A.rr[:, 1:2], A.rr[:, 1:2], nlam[:, 0:1], ALU.mult)
    _ts(mk, 'dve', [acc[0], A.rr], [A.ot], A.ot[:], acc[0][:, 0:128], A.rr[:, 0:1], ALU.mult)
    _stt(mk, [acc[1], A.rr, A.ot], [A.oatt], A.oatt[:, h, :], acc[1][:, 0:128], A.rr[:, 1:2], A.ot[:], ALU.mult, ALU.add)


def attn_post(cx, B, A, nh):
    mk = cx.mk
    _tt(mk, 'pool', [A.oatt], [B.sq], B.sq[:, :nh * 128].rearrange("p (h e) -> p h e", e=128), A.oatt[:], A.oatt[:], ALU.mult)
    mk.op('dve', [B.sq], [A.ss], lambda e: e.tensor_reduce(
        out=A.ss[:, :nh], in_=B.sq[:, :nh * 128].rearrange("p (h e) -> p h e", e=128), axis=AX.X, op=ALU.add))
    _act(mk, [A.ss, cx.epsc], [A.ss], A.ss[:, :nh], A.ss[:, :nh], AF.Sqrt, scale=1.0 / 128, bias=cx.epsc[:, 0:1])
    mk.op('dve', [A.ss], [A.ss], lambda e: e.reciprocal(out=A.ss[:, :nh], in_=A.ss[:, :nh]))
    _tt(mk, 'dve', [A.oatt, A.ss], [A.oatt], A.oatt[:], A.oatt[:], A.ss[:, :nh].unsqueeze(2).broadcast_to([128, nh, 128]), ALU.mult)
    _tt(mk, 'pool', [A.oatt, A.zb], [A.og], A.og[:], A.oatt[:], A.zb[:].rearrange("p (h e) -> p h e", e=128), ALU.mult)
    pt = nextpst(cx)
    for h in range(nh):
        _tr(mk, [A.og, cx.identb], [pt], pt[:, h * 128:(h + 1) * 128], A.og[:, h, :], cx.identb[:])
    _copy(mk, 'act', [pt], [A.ogT], A.ogT[:], pt[:, :nh * 128].rearrange("p (h t) -> p h t", h=nh))


def out_proj_res(cx, A, ogT, nh_all, wob, x_tile_ap, x_tile, Gb, y_out_ap):
    mk = cx.mk
    for half in range(2):
        ps = nextps(cx)
        for h in range(nh_all):
            _mm(mk, [ogT, wob], [ps], ps[:], ogT[:, h, :], wob[:, h, half * 512:(half + 1) * 512], start=(h == 0), stop=(h == nh_all - 1))
        cs = slice(half * 512, (half + 1) * 512)
        _tt(mk, 'dve', [ps, Gb], [A.y], A.y[:, cs], ps[:], Gb[:, cs], ALU.mult)
        _tt(mk, 'pool', [A.y, x_tile], [A.y], A.y[:, cs], A.y[:, cs], x_tile_ap[:, cs], ALU.add)
    _store(mk, A.y, y_out_ap, A.y[:])


LAM_INIT = 0.8 - 0.6 * math.exp(-0.3 * 1)
U32 = mybir.dt.uint32


def phase_mods(cx, din, es, nm, w, cT_name):
    mk = cx.mk
    A_ = mk.tile([128, D], BF16, 'A' + nm, es=es); Sh_ = mk.tile([128, D], BF16, 'Sh' + nm, es=es)
    G_ = mk.tile([128, D], F32, 'G' + nm, es=es) if w == 3 else None
    with ExitStack() as est:
        cT = mk.tile([128, 8, 128], F32, 'cT' + nm, es=est); _load(mk, cT, cT[:], din[cT_name])
        gn = mk.tile([128, D], F32, 'gn' + nm, es=est); _load(mk, gn, gn[:], din[f'norm_{nm}'])
        o2 = [mk.tile([128, D], F32, f'ada_o{nm}{i}', es=est) for i in range(2)]
        st = alloc_stage(cx, est)
        ada_mod(cx, cT, din[f'ada_w_{nm}'], din[f'ada_b_{nm}'], w * D, [o2[0], o2[1]] + ([G_] if w == 3 else []), st)
        _copy(mk, 'pool', [o2[0]], [Sh_], Sh_[:], o2[0][:])
        _stt(mk, [o2[1], gn], [A_], A_[:], o2[1][:], 1.0, gn[:], ALU.add, ALU.mult)

    barrier(mk)
    return A_, Sh_, G_


def build_program(NBLK=32, with_sample=False):
    nc = bass.Bass("TRN2", target_bir_lowering=False)
    T = 256
    NT = NBLK * 128
    NG = NT // T
    NSLOT = NBLK // 2
    din = {}

    def inp(name, shape, dt=F32):
        din[name] = nc.dram_tensor(name, list(shape), dt, kind="ExternalInput").ap()

    def outp(name, shape, dt=F32):
        return nc.dram_tensor(name, list(shape), dt, kind="ExternalOutput").ap()

    inp('x_p', [NT, D]); inp('cT_p', [128, 8, 128])
    for nm, w in (('a', 3), ('kv', 2), ('b', 3)):
        inp(f'ada_w_{nm}', [D, w * D]); inp(f'ada_b_{nm}', [128, w * D]); inp(f'norm_{nm}', [128, D])
    inp('w_in_a', [D, NIN]); inp('cw', [128, 32, 4]); inp('a_log', [128, 16]); inp('dt_bias', [128, 16])
    inp('ogain', [128, 128]); inp('w_out_a', [2048, D])
    inp('w_kv', [D, 2048]); inp('kgain', [128, 64])
    inp('w_in_b', [D, 2048]); inp('qgain', [128, 64]); inp('lam', [128, 4, 64]); inp('subgain', [128, 128])
    inp('w_out_b', [D, D])
    inp('cs_p', [NT, 2, 8]); inp('cs_q', [NSLOT * 128, 2, 8]); inp('amask', [128, 2, 128]); inp('qtok', [128, NSLOT], U32)
    for k in ['m1', 'm2', 'nm_strict', 'nm_incl']:
        inp('p_' + k, [128, 128])
    inp('p_ml', [128, 7, 2, 128])
    inp('x_s', [128, D]); inp('cT_s', [128, 8, 128]); inp('S_s', [4, 16, 128, 128]); inp('conv_s_in', [4, 96, 128]); inp('vm_s', [128, 4])
    S_s_out = outp('S_s_out', [4, 16, 128, 128]); conv_s_out = outp('conv_s_out', [4, 96, 128]); x1loc = outp('x1loc', [32, D])
    y_p = outp('y_p', [NSLOT * 128, D]); S_p = outp('S_p', [16, 128, 128]); conv_p = outp('conv_p', [96, 128])
    k_p = outp('k_p', [NT, D]); v_p = outp('v_p', [NT, D])

    with ExitStack() as es:
        mk = MK(nc, es)
        cx = Ctx()
        cx.mk = mk
        cx.ps = [mk.psum([128, 512], F32, f'ps{i}') for i in range(6)]
        cx.pst = [mk.psum([128, 1024], BF16, f'pst{i}') for i in range(2)]
        cx.psi = 0
        cx.psti = 0
        setup_consts(cx, din)
        cx.epsc = mk.tile([128, 1], F32, 'epsc')
        cx.onec = mk.tile([128, 1], F32, 'onec')
        cx.eps128 = mk.tile([128, 1], F32, 'eps128')
        cx.m4c = mk.tile([128, 1], F32, 'm4c')
        cx.cm = mk.tile([128, 2], F32, 'cm')
        mk.op('pool', [], [cx.cm], lambda e: e.memset(cx.cm[:], 0.0))
        mk.op('pool', [cx.cm], [cx.cm], lambda e: e.memset(cx.cm[0:64, 0:1], 1.0))
        mk.op('pool', [cx.cm], [cx.cm], lambda e: e.memset(cx.cm[64:128, 1:2], 1.0))
        mk.op('pool', [], [cx.epsc], lambda e: e.memset(cx.epsc[:], EPS))
        mk.op('pool', [], [cx.onec], lambda e: e.memset(cx.onec[:], 1.0))
        mk.op('pool', [], [cx.eps128], lambda e: e.memset(cx.eps128[:], 128.0 * EPS))
        mk.op('pool', [], [cx.m4c], lambda e: e.memset(cx.m4c[:], -4.0))
        cx.cw = mk.tile([128, 32, 4], F32, 'cw'); _load(mk, cx.cw, cx.cw[:], din['cw'])
        cx.dtb = mk.tile([128, 16], F32, 'dtb'); _load(mk, cx.dtb, cx.dtb[:], din['dt_bias'])
        cx.nea = mk.tile([128, 16], F32, 'nea'); _load(mk, cx.nea, cx.nea[:], din['a_log'])
        _act(mk, [cx.nea], [cx.nea], cx.nea[:], cx.nea[:], AF.Exp)
        _ts(mk, 'dve', [cx.nea], [cx.nea], cx.nea[:], cx.nea[:], -1.0, ALU.mult)
        cx.ogain = mk.tile([128, 128], F32, 'ogain'); _load(mk, cx.ogain, cx.ogain[:], din['ogain'])
        kgain = mk.tile([128, 64], F32, 'kgain'); _load(mk, kgain, kgain[:], din['kgain'])
        qgain = mk.tile([128, 64], F32, 'qgain'); _load(mk, qgain, qgain[:], din['qgain'])
        zgain = mk.tile([128, 128], F32, 'zgain'); _load(mk, zgain, zgain[:], din['subgain'])
        _ts(mk, 'dve', [zgain], [zgain], zgain[:], zgain[:], 1.0 - LAM_INIT, ALU.mult)
        amask = mk.tile([128, 2, 128], BF16, 'amask')
        qtok = mk.tile([128, NSLOT], U32, 'qtok'); _load(mk, qtok, qtok[:], din['qtok'])
        ls = mk.tile([128, 2], F32, 'ls'); nlam = mk.tile([128, 1], F32, 'nlam')
        A0s = mk.tile([128, D], BF16, 'A0s'); Sh0s = mk.tile([128, D], BF16, 'Sh0s'); G0s = mk.tile([128, D], F32, 'G0s')
        vms = mk.tile([128, 4], F32, 'vms'); _load(mk, vms, vms[:], din['vm_s'])
        mods = {}
        for nm in ('a',):
            mods[nm] = (mk.tile([128, D], BF16, 'A' + nm), mk.tile([128, D], BF16, 'Sh' + nm),
                        mk.tile([128, D], F32, 'G' + nm) if nm != 'kv' else None)
        cx.wina_bf = mk.dram('wina_bf', [128, 8, NIN], BF16)
        woa_d = mk.dram('woa_bf', [128, 16, D], BF16)
        wkv_d = mk.dram('wkv_bf', [128, 8, 2048], BF16)
        winb_d = mk.dram('winb_bf', [128, 8, 2048], BF16)
        wob_d = mk.dram('wob_bf', [128, 8, D], BF16)
        x1s = mk.dram('x1s', [NT, D], F32)
        KTs = mk.dram('scr_KTs', [8, 128, NT], BF16, kind='ExternalOutput')
        Vs = mk.dram('scr_Vs', [8, 128, NBLK, VW], BF16, kind='ExternalOutput')
        with ExitStack() as es1:
            Mp = load_masks(cx, din, 'p_', 7, es1)
            amf = mk.tile([128, 2, 128], F32, 'amf', es=es1); _load(mk, amf, amf[:], din['amask'])
            _copy(mk, 'pool', [amf], [amask], amask[:], amf[:])
            lamt = mk.tile([128, 4, 64], F32, 'lamt', es=es1); _load(mk, lamt, lamt[:], din['lam'])
            lp = mk.tile([128, 2, 64], F32, 'lp', es=es1)
            _tt(mk, 'dve', [lamt], [lp], lp[:], lamt[:].rearrange("p (a b) d -> p a b d", b=2)[:, :, 0, :],
                lamt[:].rearrange("p (a b) d -> p a b d", b=2)[:, :, 1, :], ALU.mult)
            mk.op('dve', [lp], [ls], lambda e: e.tensor_reduce(out=ls[:], in_=lp[:], axis=AX.X, op=ALU.add))
            _act(mk, [ls], [ls], ls[:], ls[:], AF.Exp)
            _tt(mk, 'dve', [ls], [nlam], nlam[:], ls[:, 1:2], ls[:, 0:1], ALU.subtract)
            _ts(mk, 'dve', [nlam], [nlam], nlam[:], nlam[:], -LAM_INIT, ALU.add)
            cT = mk.tile([128, 8, 128], F32, 'cT', es=es1); _load(mk, cT, cT[:], din['cT_p'])
            gn = mk.tile([128, D], F32, 'gn', es=es1)
            o3 = [mk.tile([128, D], F32, f'ada_o{i}', es=es1) for i in range(3)]
            st = alloc_stage(cx, es1)
            for nm, w in (('a', 3),):
                A_, Sh_, G_ = mods[nm]
                _load(mk, gn, gn[:], din[f'norm_{nm}'])
                outs = [o3[0], o3[1]] + ([G_] if w == 3 else [])
                ada_mod(cx, cT, din[f'ada_w_{nm}'], din[f'ada_b_{nm}'], w * D, outs, st)
                _copy(mk, 'pool', [o3[0]], [Sh_], Sh_[:], o3[0][:])
                _stt(mk, [o3[1], gn], [A_], A_[:], o3[1][:], 1.0, gn[:], ALU.add, ALU.mult)
            cTs = mk.tile([128, 8, 128], F32, 'cTs', es=es1); _load(mk, cTs, cTs[:], din['cT_s'])
            _load(mk, gn, gn[:], din['norm_a'])
            ada_mod(cx, cTs, din['ada_w_a'], din['ada_b_a'], 3 * D, [o3[0], o3[1], G0s], st)
            _copy(mk, 'pool', [o3[0]], [Sh0s], Sh0s[:], o3[0][:])
            _stt(mk, [o3[1], gn], [A0s], A0s[:], o3[1][:], 1.0, gn[:], ALU.add, ALU.mult)
            prep_w_bf(cx, din['w_in_a'], 8, NIN, cx.wina_bf, st)
            prep_w_bf(cx, din['w_out_a'], 16, D, woa_d, st)
            prep_w_bf(cx, din['w_kv'], 8, 2048, wkv_d, st)
            prep_w_bf(cx, din['w_in_b'], 8, 2048, winb_d, st)
            prep_w_bf(cx, din['w_out_b'], 8, D, wob_d, st)
        barrier(mk)
        A0, Sh0, G0 = mods['a']
        with ExitStack() as es2:
            woa = mk.tile([128, 16, D], BF16, 'woa', es=es2)
            _load(mk, woa, woa[:], woa_d[:], reads=[woa_d])
            L = alloc_l0(cx, es2, T)
            S32 = mk.tile([128, 16, 128], F32, 'S32', es=es2)
            Sb = mk.tile([128, 16, 128], BF16, 'Sb', es=es2)
            mk.op('pool', [], [S32], lambda e: e.memset(S32[:], 0.0))
            mk.op('pool', [], [Sb], lambda e: e.memset(Sb[:], 0.0))
            mk.op('pool', [], [L.hist], lambda e: e.memset(L.hist[:], 0.0))
            sc_t = {'junk': L.junk, 'ss': L.nss, 'rstd': L.nrs, 'hb': L.hb, 't32': L.t32}
            if with_sample:
                xs = L.x[0]
                _load(mk, xs, xs[:, 0, :], din['x_s'])
                rms_mod_T(cx, xs, 1, A0s, Sh0s, L.hT, sc_t)
                hs_in = L.h3
                Ms = Ctx(); Ms.__dict__.update(Mp.__dict__); Ms.nlev = 3
                mk.op('pool', [], [hs_in], lambda e: e.memset(hs_in[:], 0.0))
                for lb in range(4):
                    _load(mk, hs_in, hs_in[0:96, :], din['conv_s_in'][lb])
                    pc = nextps(cx)
                    mk.op('pe', [hs_in, cx.identf], [pc], lambda e, pc=pc: e.transpose(out=pc[:, 0:128], in_=hs_in[:], identity=cx.identf[:]))
                    _copy(mk, 'dve', [pc], [L.hist], L.hist[:].rearrange("p f j -> p j f"), pc[:, 0:96].rearrange("p (j f) -> p j f", j=3))
                    l0_project(cx, L, 0, True, T=128, cpos=8 * lb + 8, hpos=8 * lb)
                    _load(mk, S32, S32[:], din['S_s'][lb].rearrange("h k v -> k h v"))
                    _copy(mk, 'pool', [S32], [Sb], Sb[:], S32[:])
                    gdn_block(cx, L, 0, Ms, S32, Sb, vm=(vms, vms[:, lb:lb + 1]))
                    gdn_out(cx, L, 0, xs, L.t32, G0s, woa)
                    _store(mk, L.t32, x1loc[8 * lb:8 * lb + 8, :], L.t32[8 * lb:8 * lb + 8, :])
                    _store(mk, S32, S_s_out[lb].rearrange("h k v -> k h v"), S32[:])
                    conv_state_out(cx, L, es2, conv_s_out[lb], f's{lb}')
                mk.op('pool', [], [S32], lambda e: e.memset(S32[:], 0.0))
                mk.op('pool', [], [Sb], lambda e: e.memset(Sb[:], 0.0))
                mk.op('pool', [], [L.hist], lambda e: e.memset(L.hist[:], 0.0))
            for gi in range(NG):
                xt = L.x[0]
                _load(mk, xt, xt[:], din['x_p'][gi * T:(gi + 1) * T, :].rearrange("(b p) d -> p b d", p=128))
                rms_mod_T(cx, xt, L.nb, A0, Sh0, L.hT, sc_t)
                l0_project(cx, L, gi, gi == NG - 1)
                for blk in range(L.nb):
                    gdn_block(cx, L, blk, Mp, S32, Sb)
                    x1 = L.t32
                    gdn_out(cx, L, blk, xt, x1, G0, woa)
                    r0 = gi * T + blk * 128
                    _store(mk, x1, x1s[r0:r0 + 128, :], x1[:], writes=[x1s], is_output=False)
            _store(mk, S32, S_p.rearrange("h k v -> k h v"), S32[:])
            conv_state_out(cx, L, es2, conv_p[:, :], 'p')
        barrier(mk)
        with ExitStack() as es3:
          if MAXPHASE >= 2:
              Akv, Shkv, _ = phase_mods(cx, din, es3, 'kv', 2, 'cT_p')
              wkv = mk.tile([128, 8, 2048], BF16, 'wkv', es=es3)
              _load(mk, wkv, wkv[:], wkv_d[:], reads=[wkv_d])
              B = alloc_l1(cx, es3)
              mk.op('pool', [], [B.vb], lambda e, B=B: e.memset(B.vb[:], 1.0))
              for blk in range(NBLK):
                  r0 = blk * 128
                  kv_block(cx, B, x1s[r0:r0 + 128, :], Akv, Shkv, wkv, 8, kgain, din['cs_p'][r0:r0 + 128, :, :],
                           k_p[r0:r0 + 128, :], v_p[r0:r0 + 128, :], KTs[:, :, r0:r0 + 128], Vs[:, :, blk, :], KTs, Vs, x_reads=[x1s])
        barrier(mk)
        with ExitStack() as es4:
          if MAXPHASE >= 3:
              Ab, Shb, Gb = phase_mods(cx, din, es4, 'b', 3, 'cT_p')
              winb = mk.tile([128, 8, 2048], BF16, 'winb', es=es4)
              wob = mk.tile([128, 8, D], BF16, 'wob', es=es4)
              _load(mk, winb, winb[:], winb_d[:], reads=[winb_d])
              _load(mk, wob, wob[:], wob_d[:], reads=[wob_d])
              B = alloc_l1(cx, es4)
              A = alloc_attn(cx, es4, 8)
              for i in range(NSLOT):
                  nkb = 2 * i + 2
                  mk.dma('pool', B.x, [x1s, qtok], [B.x], lambda e, i=i: e.indirect_dma_start(
                      out=B.x[:, 0, :], out_offset=None, in_=x1s[:, :],
                      in_offset=bass.IndirectOffsetOnAxis(ap=qtok[:, i:i + 1], axis=0)))
                  l1_qz(cx, B, A, Ab, Shb, winb, 8, qgain, din['cs_q'][i * 128:(i + 1) * 128, :, :], zgain)
                  if P3STEP < 2:
                      continue
                  masks = {nkb - 2: (amask, amask[:, 0, :]), nkb - 1: (amask, amask[:, 1, :])}
                  for h in range(8):
                      kt, vt = A.ktile[h % 2], A.vtile[h % 2]
                      _load(mk, kt, kt[:, :nkb * 128], KTs[h, :, 0:nkb * 128], reads=[KTs])
                      _load(mk, vt, vt[:, :nkb, :], Vs[h, :, 0:nkb, :], reads=[Vs])
                      attn_head(cx, A, h, kt, vt, nkb, masks, nlam)
                  if P3STEP < 3:
                      continue
                  attn_post(cx, B, A, 8)
                  if P3STEP < 4:
                      continue
                  out_proj_res(cx, A, A.ogT, 8, wob, B.x[:, 0, :], B.x, Gb, y_p[i * 128:(i + 1) * 128, :])
        mk.finish()
    return nc


def _rep(v, n=128):
    v = np.asarray(v, np.float32).reshape(1, -1)
    return np.ascontiguousarray(np.broadcast_to(v, (n, v.shape[1])))


def _rope_table(pos):
    inv_freq = (500000.0 ** (-np.arange(0, 16, 2, dtype=np.float32) / 16)).astype(np.float32)
    ang = pos.astype(np.float32)[:, None] * inv_freq[None, :]
    return np.stack([np.cos(ang), np.sin(ang)], axis=1).astype(np.float32)


def make_in_maps(inputs, NBLK=32):
    NT = NBLK * 128
    NSLOT = NBLK // 2
    hc = host_consts(False)
    g = lambda k: np.asarray(inputs[k])
    shared = {
        'ada_w_a': g('ada_w_a')[0], 'ada_b_a': _rep(g('ada_b_a')[0]), 'norm_a': _rep(g('norm_a')[0]),
        'ada_w_kv': g('ada_w_kv'), 'ada_b_kv': _rep(g('ada_b_kv')), 'norm_kv': _rep(g('norm_kv')),
        'ada_w_b': g('ada_w_b')[0], 'ada_b_b': _rep(g('ada_b_b')[0]), 'norm_b': _rep(g('norm_b')[0]),
        'w_in_a': g('w_in_a')[0],
        'cw': np.ascontiguousarray(g('conv_w_a')[0].reshape(4, 32, 128).transpose(2, 1, 0)),
        'a_log': _rep(g('a_log')[0]), 'dt_bias': _rep(g('dt_bias')[0]), 'ogain': _rep(g('gdn_out_gain')[0]),
        'w_out_a': g('w_out_a')[0], 'w_kv': g('w_kv'), 'kgain': _rep(g('k_gain')),
        'w_in_b': g('w_in_b')[0], 'qgain': _rep(g('q_gain')[0]),
        'lam': np.ascontiguousarray(np.broadcast_to(g('lam_params')[0][None], (128, 4, 64))).astype(np.float32),
        'subgain': _rep(g('subln_gain')[0]), 'w_out_b': g('w_out_b')[0],
        'cs_p': _rope_table(np.arange(NT)),
        'p_m1': hc['m1'], 'p_m2': hc['m2'], 'p_nm_strict': hc['nm_strict'], 'p_nm_incl': hc['nm_incl'], 'p_ml': hc['ml'],
    }
    i128 = np.arange(128)
    tri = (i128[:, None] <= i128[None, :]).astype(np.float32)
    maps = []
    for c in range(8):
        b, r = c // 2, c % 2
        m = dict(shared)
        m['x_p'] = np.ascontiguousarray(g('x_prompt')[b, :NT])
        cvec = g('c_prompt')[b]
        m['cT_p'] = np.ascontiguousarray(np.broadcast_to(cvec.reshape(8, 128).T[:, :, None], (128, 8, 128))).astype(np.float32)
        qblk = 2 * np.arange(NSLOT) + r
        qpos = (qblk[:, None] * 128 + i128[None, :]).reshape(-1)
        m['cs_q'] = _rope_table(qpos)
        m['qtok'] = np.ascontiguousarray((qblk[None, :] * 128 + i128[:, None]).astype(np.uint32))
        am = np.zeros((128, 2, 128), np.float32)
        if r == 0:
            am[:, 0, :] = tri
        else:
            am[:, 0, :] = 1.0
            am[:, 1, :] = tri
        m['amask'] = am
        if 'x_sample' in inputs:
            xs = np.zeros((128, D), np.float32)
            xs[:32] = g('x_sample')[4 * c:4 * c + 4].reshape(32, D)
            m['x_s'] = xs
            cs_ = np.zeros((D, 128), np.float32)
            cs_[:, :32] = np.repeat(g('c_sample')[4 * c:4 * c + 4], 8, axis=0).T
            m['cT_s'] = np.ascontiguousarray(cs_.reshape(8, 128, 128).transpose(1, 0, 2))
            m['S_s'] = np.ascontiguousarray(g('state_gdn')[0, 4 * c:4 * c + 4])
            m['conv_s_in'] = np.ascontiguousarray(g('state_conv')[0, 4 * c:4 * c + 4].reshape(4, 96, 128))
            vm = np.zeros((128, 4), np.float32)
            for lb in range(4):
                vm[8 * lb:8 * lb + 8, lb] = 1.0
            m['vm_s'] = vm
        maps.append(m)
    return maps


def _common_ctx(nc, es):
    mk = MK(nc, es)
    cx = Ctx()
    cx.mk = mk
    cx.ps = [mk.psum([128, 512], F32, f'ps{i}') for i in range(6)]
    cx.pst = [mk.psum([128, 1024], BF16, f'pst{i}') for i in range(2)]
    cx.psi = 0
    cx.psti = 0
    setup_consts(cx, {})
    cx.epsc = mk.tile([128, 1], F32, 'epsc')
    cx.m4c = mk.tile([128, 1], F32, 'm4c')
    cx.cm = mk.tile([128, 2], F32, 'cm')
    mk.op('pool', [], [cx.epsc], lambda e: e.memset(cx.epsc[:], EPS))
    mk.op('pool', [], [cx.m4c], lambda e: e.memset(cx.m4c[:], -4.0))
    mk.op('pool', [], [cx.cm], lambda e: e.memset(cx.cm[:], 0.0))
    mk.op('pool', [cx.cm], [cx.cm], lambda e: e.memset(cx.cm[0:64, 0:1], 1.0))
    mk.op('pool', [cx.cm], [cx.cm], lambda e: e.memset(cx.cm[64:128, 1:2], 1.0))
    return mk, cx


def build_sample_attn(NB=32, NPG=64):
    nc = bass.Bass("TRN2", target_bir_lowering=False)
    NTK = NB * 8
    NTILE = NTK // 128
    NGRP = NPG // 16
    din = {}

    def inp(name, shape, dt=F32):
        din[name] = nc.dram_tensor(name, list(shape), dt, kind="ExternalInput").ap()

    def outp(name, shape, dt=F32):
        return nc.dram_tensor(name, list(shape), dt, kind="ExternalOutput").ap()

    inp('x1all', [NTK, D])
    for j in range(NTILE):
        inp(f'cT_all{j}', [128, 8, 128])
    for nm, w in (('kv', 2), ('b', 3)):
        inp(f'ada_w_{nm}', [D, w * D]); inp(f'ada_b_{nm}', [128, w * D]); inp(f'norm_{nm}', [128, D])
    inp('w_kv_h', [D, 256]); inp('kgain', [128, 64]); inp('w_in_b_h', [D, 256]); inp('qgain', [128, 64])
    inp('subgain', [128, 128]); inp('lam', [128, 4, 64]); inp('cs_s', [NTK, 2, 8])
    inp('pool_k', [2560 * 8, 2048]); inp('pool_v', [2560 * 8, 2048])
    inp('pt_rep', [128, NB * NGRP], I32); inp('sub8', [128, 1]); inp('nmask', [128, 16, 8])
    og_h = outp('og_h', [NTK, 128]); k_s = outp('k_s_h', [NTK, 128]); v_s = outp('v_s_h', [NTK, 128])
    oat = outp('scr_oatt', [NTK, 128])
    KTs = nc.dram_tensor('scr_KTn', [1, 128, NTK], BF16, kind="ExternalOutput").ap()
    Vs = nc.dram_tensor('scr_Vn', [1, 128, NTILE, VW], BF16, kind="ExternalOutput").ap()
    with ExitStack() as es:
        mk, cx = _common_ctx(nc, es)
        KTs_t, Vs_t, oat_t = Tile('KTn', KTs), Tile('Vn', Vs), Tile('oat', oat)
        kgain = mk.tile([128, 64], F32, 'kgain'); _load(mk, kgain, kgain[:], din['kgain'])
        qgain = mk.tile([128, 64], F32, 'qgain'); _load(mk, qgain, qgain[:], din['qgain'])
        zgain = mk.tile([128, 128], F32, 'zgain'); _load(mk, zgain, zgain[:], din['subgain'])
        _ts(mk, 'dve', [zgain], [zgain], zgain[:], zgain[:], 1.0 - LAM_INIT, ALU.mult)
        nmf = mk.tile([128, 16, 8], F32, 'nmf'); _load(mk, nmf, nmf[:], din['nmask'])
        nmask = mk.tile([128, 16, 8], BF16, 'nmask'); _copy(mk, 'pool', [nmf], [nmask], nmask[:], nmf[:])
        lamt = mk.tile([128, 4, 64], F32, 'lamt'); _load(mk, lamt, lamt[:], din['lam'])
        lp = mk.tile([128, 2, 64], F32, 'lp'); ls = mk.tile([128, 2], F32, 'ls'); nlam = mk.tile([128, 1], F32, 'nlam')
        _tt(mk, 'dve', [lamt], [lp], lp[:], lamt[:].rearrange("p (a b) d -> p a b d", b=2)[:, :, 0, :],
            lamt[:].rearrange("p (a b) d -> p a b d", b=2)[:, :, 1, :], ALU.mult)
        mk.op('dve', [lp], [ls], lambda e: e.tensor_reduce(out=ls[:], in_=lp[:], axis=AX.X, op=ALU.add))
        _act(mk, [ls], [ls], ls[:], ls[:], AF.Exp)
        _tt(mk, 'dve', [ls], [nlam], nlam[:], ls[:, 1:2], ls[:, 0:1], ALU.subtract)
        _ts(mk, 'dve', [nlam], [nlam], nlam[:], nlam[:], -LAM_INIT, ALU.add)
        pti = mk.tile([128, NB * NGRP], I32, 'pti'); _load(mk, pti, pti[:], din['pt_rep'])
        sub8 = mk.tile([128, 1], F32, 'sub8'); _load(mk, sub8, sub8[:], din['sub8'])
        ptf = mk.tile([128, NB * NGRP], F32, 'ptf'); _copy(mk, 'dve', [pti], [ptf], ptf[:], pti[:])
        _ts(mk, 'dve', [ptf, sub8], [ptf], ptf[:], ptf[:], 8.0, ALU.mult, sub8[:, 0:1], ALU.add)
        idx = mk.tile([128, NB * NGRP], U32, 'idx'); _copy(mk, 'dve', [ptf], [idx], idx[:], ptf[:])
        wkv = mk.tile([128, 8, 256], BF16, 'wkvh'); winb = mk.tile([128, 8, 256], BF16, 'winbh')
        QTall = mk.tile([128, NTILE, 2, 128], BF16, 'QTall')
        zball = mk.tile([128, NTILE, 128], BF16, 'zball')
        KTn = mk.tile([128, NTILE, 128], BF16, 'KTnew')
        Vn = mk.tile([128, NTILE, VW], BF16, 'Vnew')
        with ExitStack() as es1:
            wf = mk.tile([128, 8, 256], F32, 'wf', es=es1)
            _load(mk, wf, wf[:], din['w_kv_h'].rearrange("(kc p) n -> p kc n", p=128))
            _copy(mk, 'pool', [wf], [wkv], wkv[:], wf[:])
            _load(mk, wf, wf[:], din['w_in_b_h'].rearrange("(kc p) n -> p kc n", p=128))
            _copy(mk, 'pool', [wf], [winb], winb[:], wf[:])
            B = alloc_l1(cx, es1)
            mk.op('pool', [], [B.vb], lambda e, B=B: e.memset(B.vb[:], 1.0))
            A = alloc_attn_small(cx, es1)
            for j in range(NTILE):
                with ExitStack() as esm:
                    Akv, Shkv, _ = phase_mods(cx, din, esm, 'kv', 2, f'cT_all{j}')
                    Ab, Shb, _g = phase_mods(cx, din, esm, 'b', 2, f'cT_all{j}')
                    r0 = j * 128
                    kv_block(cx, B, din['x1all'][r0:r0 + 128, :], Akv, Shkv, wkv, 1, kgain, din['cs_s'][r0:r0 + 128, :, :],
                             k_s[r0:r0 + 128, :], v_s[r0:r0 + 128, :], KTs[:, :, r0:r0 + 128], Vs[:, :, j, :], KTs_t, Vs_t)
                    _copy(mk, 'pool', [B.kT], [KTn], KTn[:, j, :], B.kT[:, 0, :])
                    _copy(mk, 'pool', [B.vb], [Vn], Vn[:, j, :], B.vb[:, 0, :])
                    l1_qz(cx, B, A, Ab, Shb, winb, 1, qgain, din['cs_s'][r0:r0 + 128, :, :], zgain)
                    _copy(mk, 'pool', [A.QT], [QTall], QTall[:, j, :, :], A.QT[:, 0, :, :])
                    _copy(mk, 'pool', [A.zb], [zball], zball[:, j, :], A.zb[:])
                    barrier(mk)
        barrier(mk)
        with ExitStack() as es2:
            kf = [mk.tile([128, 2048], F32, f'kf{i}', es=es2) for i in range(2)]
            vf = [mk.tile([128, 2048], F32, f'vf{i}', es=es2) for i in range(2)]
            kb16 = mk.tile([128, 16, 128], BF16, 'kb16', es=es2)
            vb16 = [mk.tile([128, 16, VW], BF16, f'vb16{i}', es=es2) for i in range(2)]
            KTg = mk.tile([128, 16, 128], BF16, 'KTg', es=es2)
            PT = [mk.tile([128, 16, 2, 8], BF16, f'PTs{i}', es=es2) for i in range(2)]
            PTn = mk.tile([128, 2, 8], BF16, 'PTn', es=es2)
            rr = mk.tile([128, 2], F32, 'rrs', es=es2); ot = mk.tile([128, 128], F32, 'ots', es=es2)
            ob = mk.tile([128, 128], F32, 'obs', es=es2)
            for t_ in vb16:
                mk.op('pool', [], [t_], lambda e, t_=t_: e.memset(t_[:], 1.0))
            pk = Tile('pool_k', din['pool_k']); pv = Tile('pool_v', din['pool_v'])
            n = 0
            for b in range(NB):
                j, bb = b // 16, b % 16
                qT = QTall[:, j, :, 8 * bb:8 * bb + 8]
                acc = [cx.ps[4], cx.ps[5]]
                for g in range(NGRP):
                    n += 1
                    kf_, vf_, vb_, PT_ = kf[n % 2], vf[n % 2], vb16[n % 2], PT[n % 2]
                    col = b * NGRP + g
                    mk.dma('pool', kf_, [idx], [kf_], lambda e, kf_=kf_, col=col: e.indirect_dma_start(
                        out=kf_[:], out_offset=None, in_=din['pool_k'],
                        in_offset=bass.IndirectOffsetOnAxis(ap=idx[:, col:col + 1], axis=0)))
                    mk.dma('pool', vf_, [idx], [vf_], lambda e, vf_=vf_, col=col: e.indirect_dma_start(
                        out=vf_[:], out_offset=None, in_=din['pool_v'],
                        in_offset=bass.IndirectOffsetOnAxis(ap=idx[:, col:col + 1], axis=0)))
                    _copy(mk, 'dve', [kf_], [kb16], kb16[:].rearrange("p t d -> p (t d)"), kf_[:])
                    _copy(mk, 'act', [vf_], [vb_], vb_[:, :, 0:128], vf_[:].rearrange("p (t d) -> p t d", d=128))
                    for half in range(2):
                        pt = nextpst(cx)
                        for t8 in range(8):
                            _tr(mk, [kb16, cx.identb], [pt], pt[:, t8 * 128:(t8 + 1) * 128], kb16[:, half * 8 + t8, :], cx.identb[:])
                        _copy(mk, 'dve', [pt], [KTg], KTg[:, half * 8:half * 8 + 8, :], pt[:].rearrange("p (t k) -> p t k", t=8))
                    ps = nextps4(cx)
                    for tl in range(16):
                        _mm(mk, [KTg, QTall], [ps], ps[:, tl * 16:(tl + 1) * 16], KTg[:, tl, :], qT)
                    _act(mk, [ps, cx.m4c], [PT_], PT_[:].rearrange("p t c q -> p (t c q)"), ps[:, 0:256], AF.Exp,
                         scale=0.125, bias=cx.m4c[:, 0:1])
                    for tl in range(16):
                        for c in range(2):
                            _mm(mk, [PT_, vb_], [acc[c]], acc[c][0:8, 0:129], PT_[:, tl, c, :], vb_[:, tl, 0:129],
                                start=(g == 0 and tl == 0), stop=False)
                ps = nextps4(cx)
                _mm(mk, [KTn, QTall], [ps], ps[:, 0:16], KTn[:, j, :], qT)
                _act(mk, [ps, cx.m4c], [PTn], PTn[:].rearrange("p c q -> p (c q)"), ps[:, 0:16], AF.Exp, scale=0.125, bias=cx.m4c[:, 0:1])
                _tt(mk, 'pool', [PTn, nmask], [PTn], PTn[:], PTn[:], nmask[:, bb:bb + 1, :].broadcast_to([128, 2, 8]), ALU.mult)
                for c in range(2):
                    _mm(mk, [PTn, Vn], [acc[c]], acc[c][0:8, 0:129], PTn[:, c, :], Vn[:, j, 0:129], start=False, stop=True)
                mk.op('dve', [acc[0]], [rr], lambda e, acc=acc: e.reciprocal(out=rr[0:8, 0:1], in_=acc[0][0:8, 128:129]))
                mk.op('dve', [acc[1]], [rr], lambda e, acc=acc: e.reciprocal(out=rr[0:8, 1:2], in_=acc[1][0:8, 128:129]))
                _tt(mk, 'dve', [rr, nlam], [rr], rr[0:8, 1:2], rr[0:8, 1:2], nlam[0:8, 0:1], ALU.mult)
                _ts(mk, 'dve', [acc[0], rr], [ot], ot[0:8, :], acc[0][0:8, 0:128], rr[0:8, 0:1], ALU.mult)
                _stt(mk, [acc[1], rr, ot], [ob], ob[0:8, :], acc[1][0:8, 0:128], rr[0:8, 1:2], ot[0:8, :], ALU.mult, ALU.add)
                _store(mk, ob, oat[8 * b:8 * b + 8, :], ob[0:8, :], writes=[oat_t])
            A2 = alloc_attn_small(cx, es2)
            sqt = mk.tile([128, D], F32, 'sqt', es=es2)
            B2 = Ctx(); B2.sq = sqt
            for j in range(NTILE):
                _load(mk, A2.oatt, A2.oatt[:, 0, :], oat[j * 128:(j + 1) * 128, :], reads=[oat_t])
                _copy(mk, 'pool', [zball], [A2.zb], A2.zb[:], zball[:, j, :])
                attn_post_tok(cx, B2, A2)
                _store(mk, A2.y, og_h[j * 128:(j + 1) * 128, :], A2.y[:, 0:128])
        mk.finish()
    return nc


def alloc_attn_small(cx, es):
    mk = cx.mk
    A = Ctx()
    A.QT = mk.tile([128, 1, 2, 128], BF16, 'sQT', es=es)
    A.zb = mk.tile([128, 128], BF16, 'szb', es=es)
    A.oatt = mk.tile([128, 1, 128], F32, 'soatt', es=es)
    A.ss = mk.tile([128, 8], F32, 'sss', es=es)
    A.y = mk.tile([128, D], F32, 'sy', es=es)
    return A


def attn_post_tok(cx, B, A):
    mk = cx.mk
    _tt(mk, 'pool', [A.oatt], [B.sq], B.sq[:, 0:128], A.oatt[:, 0, :], A.oatt[:, 0, :], ALU.mult)
    mk.op('dve', [B.sq], [A.ss], lambda e: e.tensor_reduce(out=A.ss[:, 0:1], in_=B.sq[:, 0:128], axis=AX.X, op=ALU.add))
    _act(mk, [A.ss, cx.epsc], [A.ss], A.ss[:, 0:1], A.ss[:, 0:1], AF.Sqrt, scale=1.0 / 128, bias=cx.epsc[:, 0:1])
    mk.op('dve', [A.ss], [A.ss], lambda e: e.reciprocal(out=A.ss[:, 0:1], in_=A.ss[:, 0:1]))
    _ts(mk, 'dve', [A.oatt, A.ss], [A.oatt], A.oatt[:, 0, :], A.oatt[:, 0, :], A.ss[:, 0:1], ALU.mult)
    _tt(mk, 'pool', [A.oatt, A.zb], [A.y], A.y[:, 0:128], A.oatt[:, 0, :], A.zb[:], ALU.mult)


def build_sample_out():
    nc = bass.Bass("TRN2", target_bir_lowering=False)
    din = {}

    def inp(name, shape, dt=F32):
        din[name] = nc.dram_tensor(name, list(shape), dt, kind="ExternalInput").ap()

    inp('x1l', [128, D]); inp('og_l', [128, D]); inp('cT_s', [128, 8, 128])
    inp('ada_w_b', [D, 3 * D]); inp('ada_b_b', [128, 3 * D]); inp('norm_b', [128, D]); inp('w_out_b', [D, D])
    y_s = nc.dram_tensor('y_s', [128, D], F32, kind="ExternalOutput").ap()
    with ExitStack() as es:
        mk, cx = _common_ctx(nc, es)
        wob = mk.tile([128, 8, D], BF16, 'wob')
        A = Ctx()
        A.y = mk.tile([128, D], F32, 'y')
        x1 = mk.tile([128, D], F32, 'x1'); _load(mk, x1, x1[:], din['x1l'])
        og = mk.tile([128, D], F32, 'og'); _load(mk, og, og[:], din['og_l'])
        ogb = mk.tile([128, 8, 128], BF16, 'ogb'); _copy(mk, 'dve', [og], [ogb], ogb[:].rearrange("p h e -> p (h e)"), og[:])
        ogT = mk.tile([128, 8, 128], BF16, 'ogT')
        _Ab, _Shb, Gb = phase_mods(cx, din, es, 'b', 3, 'cT_s')
        with ExitStack() as es1:
            wf = mk.tile([128, 8, D], F32, 'wf', es=es1)
            _load(mk, wf, wf[:], din['w_out_b'].rearrange("(kc p) n -> p kc n", p=128))
            _copy(mk, 'pool', [wf], [wob], wob[:], wf[:])
            pt = nextpst(cx)
            for h in range(8):
                _tr(mk, [ogb, cx.identb], [pt], pt[:, h * 128:(h + 1) * 128], ogb[:, h, :], cx.identb[:])
            _copy(mk, 'act', [pt], [ogT], ogT[:], pt[:].rearrange("p (h t) -> p h t", h=8))
            out_proj_res(cx, A, ogT, 8, wob, x1[:], x1, Gb, y_s[:, :])
        mk.finish()
    return nc


_PROG_CACHE = {}
DBG = {}


def kernel_impl(inputs, NBLK=32):
    g = lambda k: np.asarray(inputs[k])
    NT = NBLK * 128
    if ('p1', NBLK) not in _PROG_CACHE:
        _PROG_CACHE[('p1', NBLK)] = build_program(NBLK, True)
        _PROG_CACHE['p2'] = build_sample_attn(32, 64)
        _PROG_CACHE['p3'] = build_sample_out()
    maps = make_in_maps(inputs, NBLK)
    res1 = run_bass_kernel_spmd(_PROG_CACHE[('p1', NBLK)], maps, core_ids=list(range(8))).results
    y_p = np.zeros((4, NT, D), np.float32)
    for b in range(4):
        yv = y_p[b].reshape(NBLK // 2, 2, 128, D)
        yv[:, 0] = res1[2 * b]['y_p'].reshape(-1, 128, D)
        yv[:, 1] = res1[2 * b + 1]['y_p'].reshape(-1, 128, D)
    st_p = np.stack([res1[2 * b]['S_p'] for b in range(4)])[None]
    conv_p = np.stack([res1[2 * b]['conv_p'].reshape(3, 4096) for b in range(4)])[None]
    k_p = np.stack([res1[2 * b]['k_p'].reshape(NT, 8, 128) for b in range(4)])
    v_p = np.stack([res1[2 * b]['v_p'].reshape(NT, 8, 128) for b in range(4)])
    st_s = np.concatenate([res1[c]['S_s_out'] for c in range(8)])[None]
    conv_s = np.concatenate([res1[c]['conv_s_out'].reshape(4, 3, 4096) for c in range(8)])[None]
    x1all = np.concatenate([res1[c]['x1loc'] for c in range(8)])
    cs = np.repeat(g('c_sample'), 8, axis=0)
    cT_all = [np.ascontiguousarray(cs[j * 128:(j + 1) * 128].T.reshape(8, 128, 128).transpose(1, 0, 2)) for j in range(2)]
    pt = g('page_table').astype(np.int32)
    pt_rep = np.ascontiguousarray(np.repeat(pt.reshape(32, 4, 16), 8, axis=2).transpose(2, 0, 1).reshape(128, 128))
    i128 = np.arange(128)
    nmask = np.zeros((128, 16, 8), np.float32)
    for bb in range(16):
        for q in range(8):
            nmask[8 * bb:8 * bb + q + 1, bb, q] = 1.0
    pos = 8192 + (np.arange(256) % 8)
    shared2 = {
        'x1all': x1all, 'cT_all0': cT_all[0], 'cT_all1': cT_all[1],
        'ada_w_kv': g('ada_w_kv'), 'ada_b_kv': _rep(g('ada_b_kv')), 'norm_kv': _rep(g('norm_kv')),
        'ada_w_b': g('ada_w_b')[0], 'ada_b_b': _rep(g('ada_b_b')[0]), 'norm_b': _rep(g('norm_b')[0]),
        'kgain': _rep(g('k_gain')), 'qgain': _rep(g('q_gain')[0]), 'subgain': _rep(g('subln_gain')[0]),
        'lam': np.ascontiguousarray(np.broadcast_to(g('lam_params')[0][None], (128, 4, 64))).astype(np.float32),
        'cs_s': _rope_table(pos), 'pt_rep': pt_rep, 'sub8': (i128 % 8).astype(np.float32).reshape(128, 1), 'nmask': nmask,
    }
    maps2 = []
    wkv, winb = g('w_kv'), g('w_in_b')[0]
    ck, cv = g('cache_k'), g('cache_v')
    for h in range(8):
        m = dict(shared2)
        m['w_kv_h'] = np.ascontiguousarray(np.concatenate([wkv[:, h * 128:(h + 1) * 128], wkv[:, 1024 + h * 128:1024 + (h + 1) * 128]], axis=1))
        m['w_in_b_h'] = np.ascontiguousarray(np.concatenate([winb[:, h * 128:(h + 1) * 128], winb[:, 1024 + h * 128:1024 + (h + 1) * 128]], axis=1))
        m['pool_k'] = np.ascontiguousarray(ck[:, :, h, :]).reshape(2560 * 8, 2048)
        m['pool_v'] = np.ascontiguousarray(cv[:, :, h, :]).reshape(2560 * 8, 2048)
        maps2.append(m)
    res2 = run_bass_kernel_spmd(_PROG_CACHE['p2'], maps2, core_ids=list(range(8))).results
    k_s = np.stack([res2[h]['k_s_h'] for h in range(8)], axis=1).reshape(32, 8, 8, 128)
    v_s = np.stack([res2[h]['v_s_h'] for h in range(8)], axis=1).reshape(32, 8, 8, 128)
    og_all = np.stack([res2[h]['og_h'] for h in range(8)], axis=1).reshape(256, D)
    DBG['x1all'] = x1all; DBG['og_all'] = og_all; DBG['oatt'] = np.stack([res2[h]['scr_oatt'] for h in range(8)], axis=1)
    maps3 = []
    for c in range(8):
        x1l = np.zeros((128, D), np.float32); x1l[:32] = x1all[32 * c:32 * c + 32]
        ogl = np.zeros((128, D), np.float32); ogl[:32] = og_all[32 * c:32 * c + 32]
        maps3.append({'x1l': x1l, 'og_l': ogl, 'cT_s': maps[c]['cT_s'], 'ada_w_b': g('ada_w_b')[0],
                      'ada_b_b': _rep(g('ada_b_b')[0]), 'norm_b': _rep(g('norm_b')[0]), 'w_out_b': g('w_out_b')[0]})
    res3 = run_bass_kernel_spmd(_PROG_CACHE['p3'], maps3, core_ids=list(range(8))).results
    y_s = np.concatenate([res3[c]['y_s'][:32] for c in range(8)]).reshape(32, 8, D)
    return (y_p, y_s, st_p, conv_p, k_p, v_p, st_s, conv_s, k_s, v_s)


def kernel(**inputs):
    return kernel_impl(inputs, 32)
```
